# Optimizing a Trainium2 kernel written in Bass

```python
import math
import jax, jax.numpy as jnp
from jax import lax
import numpy as np

D_MODEL = 1024
BATCH = 16
SEQ = 4096
DEPTH = 4

D_FF = 2816
MLA_HEADS = 8
MLA_Q_RANK = 384
MLA_KV_RANK = 256
MLA_NOPE_DIM = 64
MLA_ROPE_DIM = 32
MLA_V_DIM = 64
ROPE_THETA = 10000.0
DIL_PAIRS = ((128, 1), (512, 4), (2048, 16))
DIL_HEADS_PER_GROUP = 4
DIL_HEADS = DIL_HEADS_PER_GROUP * len(DIL_PAIRS)
DIL_HEAD_DIM = 128
SSM_WIDTH = 512
SSM_GROUP_SIZE = 16
SSM_GROUPS = SSM_WIDTH // SSM_GROUP_SIZE
SSM_STATE = 64
SSM_CHUNK = 128
N_BRANCH = 3
BRANCH_WIDTH = 512
Q_BLOCK = 128
N_IN = MLA_Q_RANK + MLA_KV_RANK + MLA_ROPE_DIM + 3 * DIL_HEADS * DIL_HEAD_DIM + SSM_WIDTH + N_BRANCH * D_MODEL
SPLIT_POINTS = (
    MLA_Q_RANK,
    MLA_Q_RANK + MLA_KV_RANK,
    MLA_Q_RANK + MLA_KV_RANK + MLA_ROPE_DIM,
    MLA_Q_RANK + MLA_KV_RANK + MLA_ROPE_DIM + 3 * DIL_HEADS * DIL_HEAD_DIM,
    MLA_Q_RANK + MLA_KV_RANK + MLA_ROPE_DIM + 3 * DIL_HEADS * DIL_HEAD_DIM + SSM_WIDTH,
)
DEEPNORM_ALPHA = (2 * DEPTH) ** 0.25
DEEPNORM_BETA = (8 * DEPTH) ** -0.25
MACARON_WEIGHT = 0.5
LN_EPS = 1e-5
RMS_EPS = 1e-6

kernel_name = 'hybrid_mla_dilated_s5_macaron_deepnorm'


def layer_norm(x, g, b):
    xf = x.astype(jnp.float32)
    mu = jnp.mean(xf, axis=-1, keepdims=True)
    var = jnp.mean(jnp.square(xf - mu), axis=-1, keepdims=True)
    return ((xf - mu) * lax.rsqrt(var + LN_EPS)).astype(x.dtype) * g + b


def rms_norm(x, g):
    xf = x.astype(jnp.float32)
    return (xf * lax.rsqrt(jnp.mean(xf * xf, axis=-1, keepdims=True) + RMS_EPS)).astype(x.dtype) * g


def modulate(h, shift, scale):
    return h * (1.0 + scale[:, None, :]) + shift[:, None, :]


def post_norm(h, out, res_w, gate, g, b):
    return layer_norm(DEEPNORM_ALPHA * h + res_w * gate[:, None, :] * out, g, b)


def swiglu(h, w1, w3, w2):
    return (jax.nn.silu(h @ w1) * (h @ w3)) @ w2


def alibi_slopes(n):
    return jnp.exp2(-8.0 * (jnp.arange(n, dtype=jnp.float32) + 1.0) / n)


def rope(t, positions):
    half = MLA_ROPE_DIM // 2
    inv_freq = jnp.power(ROPE_THETA, -jnp.arange(half, dtype=jnp.float32) / half)
    ang = positions.astype(jnp.float32)[..., None] * inv_freq
    ang = ang.reshape(ang.shape[:2] + (1,) * (t.ndim - 3) + (half,))
    cos, sin = jnp.cos(ang), jnp.sin(ang)
    tf = t.astype(jnp.float32)
    t1, t2 = tf[..., :half], tf[..., half:]
    return jnp.concatenate([t1 * cos - t2 * sin, t2 * cos + t1 * sin], axis=-1).astype(t.dtype)


def causal_block_attention(q, k, v, scale):
    B, S, H, _ = q.shape
    nb = S // Q_BLOCK
    kpos = jnp.arange(S)

    def one_block(j):
        qb = lax.dynamic_slice_in_dim(q, j * Q_BLOCK, Q_BLOCK, axis=1)
        s = jnp.einsum('bqhd,bkhd->bhqk', qb, k, preferred_element_type=jnp.float32) * scale
        qpos = j * Q_BLOCK + jnp.arange(Q_BLOCK)
        s = jnp.where(kpos[None, :] <= qpos[:, None], s, -jnp.inf)
        p = jax.nn.softmax(s, axis=-1).astype(v.dtype)
        return jnp.einsum('bhqk,bkhd->bqhd', p, v)

    out = lax.map(one_block, jnp.arange(nb))
    return out.transpose(1, 0, 2, 3, 4).reshape(B, S, H, v.shape[-1])


def banded_causal_attention(q, k, v, band, slopes, stride, scale):
    N, L, H, dh = q.shape
    nb = -(-L // Q_BLOCK)
    pad = nb * Q_BLOCK - L
    q = jnp.pad(q, ((0, 0), (0, pad), (0, 0), (0, 0)))
    k = jnp.pad(k, ((0, 0), (Q_BLOCK, pad), (0, 0), (0, 0)))
    v = jnp.pad(v, ((0, 0), (Q_BLOCK, pad), (0, 0), (0, 0)))
    qi = jnp.arange(Q_BLOCK)
    ki = jnp.arange(2 * Q_BLOCK)
    rel = qi[:, None] - ki[None, :] + Q_BLOCK
    band_ok = (rel >= 0) & (rel <= band)
    bias = -(slopes.astype(jnp.float32) * stride)[:, None, None] * rel.astype(jnp.float32)

    def one_block(j):
        qb = lax.dynamic_slice_in_dim(q, j * Q_BLOCK, Q_BLOCK, axis=1)
        kb = lax.dynamic_slice_in_dim(k, j * Q_BLOCK, 2 * Q_BLOCK, axis=1)
        vb = lax.dynamic_slice_in_dim(v, j * Q_BLOCK, 2 * Q_BLOCK, axis=1)
        s = jnp.einsum('nqhd,nkhd->nhqk', qb, kb, preferred_element_type=jnp.float32) * scale + bias
        key_ok = band_ok & ((j * Q_BLOCK - Q_BLOCK + ki) >= 0)[None, :]
        s = jnp.where(key_ok, s, -jnp.inf)
        lse = jax.nn.logsumexp(s, axis=-1)
        p = jnp.exp(s - lse[..., None]).astype(v.dtype)
        o = jnp.einsum('nhqk,nkhd->nqhd', p, vb)
        return o, lse.transpose(0, 2, 1)

    o, lse = lax.map(one_block, jnp.arange(nb))
    o = o.transpose(1, 0, 2, 3, 4).reshape(N, nb * Q_BLOCK, H, dh)[:, :L]
    lse = lse.transpose(1, 0, 2, 3).reshape(N, nb * Q_BLOCK, H)[:, :L]
    return o, lse


def to_strided(t, dil):
    B, S, H, d = t.shape
    return t.reshape(B, S // dil, dil, H, d).transpose(0, 2, 1, 3, 4).reshape(B * dil, S // dil, H, d)


def from_strided(t, B, dil):
    L = t.shape[1]
    rest = t.shape[2:]
    t = t.reshape((B, dil, L) + rest)
    t = jnp.moveaxis(t, 1, 2)
    return t.reshape((B, L * dil) + rest)


def dilated_attention(q, k, v):
    B = q.shape[0]
    slopes = alibi_slopes(DIL_HEADS)
    outs, lses = [], []
    for g, (window, dil) in enumerate(DIL_PAIRS):
        lo, hi = g * DIL_HEADS_PER_GROUP, (g + 1) * DIL_HEADS_PER_GROUP
        o, lse = banded_causal_attention(
            to_strided(q[:, :, lo:hi], dil), to_strided(k[:, :, lo:hi], dil), to_strided(v[:, :, lo:hi], dil),
            window // dil, slopes[lo:hi], dil, DIL_HEAD_DIM ** -0.5)
        outs.append(from_strided(o, B, dil))
        lses.append(from_strided(lse, B, dil))
    w = jax.nn.softmax(jnp.stack(lses, axis=0), axis=0)
    o = jnp.stack(outs, axis=0)
    return jnp.einsum('gbsh,gbshd->bshd', w.astype(o.dtype), o)


def s5_ssm(u, lam_re, lam_im, log_dt, b_re, b_im, c_re, c_im, d_skip):
    B, S, _ = u.shape
    f32 = jnp.float32
    uf = u.astype(f32)
    lam = lax.complex(lam_re.astype(f32), lam_im.astype(f32))
    dt = jnp.exp(log_dt.astype(f32))[:, None]
    lam_bar = jnp.exp(lam * dt)
    b_bar = ((lam_bar - 1.0) / lam)[..., None] * lax.complex(b_re.astype(f32), b_im.astype(f32))
    c_mat = lax.complex(c_re.astype(f32), c_im.astype(f32))
    nc = S // SSM_CHUNK
    u_chunks = uf.reshape(B, nc, SSM_CHUNK, SSM_GROUPS, SSM_GROUP_SIZE).transpose(1, 2, 0, 3, 4)
    a = jnp.broadcast_to(lam_bar, (SSM_CHUNK, 1, SSM_GROUPS, SSM_STATE))

    def combine(e1, e2):
        a1, b1 = e1
        a2, b2 = e2
        return a2 * a1, a2 * b1 + b2

    def chunk_step(state, u_c):
        bu = jnp.einsum('gph,tbgh->tbgp', b_bar, u_c.astype(jnp.complex64))
        a_cum, xs = lax.associative_scan(combine, (a, bu), axis=0)
        xs = xs + a_cum * state[None]
        y = jnp.einsum('gkp,tbgp->tbgk', c_mat, xs).real
        return xs[-1], y

    state0 = jnp.zeros((B, SSM_GROUPS, SSM_STATE), jnp.complex64)
    _, y = lax.scan(chunk_step, state0, u_chunks)
    y = y.transpose(2, 0, 1, 3, 4).reshape(B, S, SSM_WIDTH)
    return y + d_skip.astype(f32) * uf


def token_mixer(h, positions, w_in, b_in, q_norm, kv_norm, w_qb, w_kvb,
                lam_re, lam_im, log_dt, b_re, b_im, c_re, c_im, d_skip, w_glu, b_glu, w_br, w_out):
    B, S, _ = h.shape
    proj = h @ w_in + b_in
    q_a, kv_a, k_rope, dil_qkv, u, gate_logits = jnp.split(proj, SPLIT_POINTS, axis=-1)

    q = (rms_norm(q_a, q_norm) @ w_qb).reshape(B, S, MLA_HEADS, MLA_NOPE_DIM + MLA_ROPE_DIM)
    q = jnp.concatenate([q[..., :MLA_NOPE_DIM], rope(q[..., MLA_NOPE_DIM:], positions)], axis=-1)
    kv = (rms_norm(kv_a, kv_norm) @ w_kvb).reshape(B, S, MLA_HEADS, MLA_NOPE_DIM + MLA_V_DIM)
    k_pe = jnp.broadcast_to(rope(k_rope, positions)[:, :, None, :], (B, S, MLA_HEADS, MLA_ROPE_DIM))
    k = jnp.concatenate([kv[..., :MLA_NOPE_DIM], k_pe], axis=-1)
    y_mla = causal_block_attention(q, k, kv[..., MLA_NOPE_DIM:], (MLA_NOPE_DIM + MLA_ROPE_DIM) ** -0.5)
    y_mla = y_mla.reshape(B, S, BRANCH_WIDTH)

    qkv = dil_qkv.reshape(B, S, 3, DIL_HEADS, DIL_HEAD_DIM)
    y_dil = dilated_attention(qkv[:, :, 0], qkv[:, :, 1], qkv[:, :, 2]).reshape(B, S, BRANCH_WIDTH)

    y_ssm = jax.nn.gelu(s5_ssm(u, lam_re, lam_im, log_dt, b_re, b_im, c_re, c_im, d_skip)).astype(h.dtype)
    y_ssm = y_ssm * jax.nn.sigmoid(y_ssm @ w_glu + b_glu)

    gates = jax.nn.sigmoid(gate_logits).reshape(B, S, N_BRANCH, D_MODEL)
    merged = (gates[:, :, 0] * (y_mla @ w_br[0])
              + gates[:, :, 1] * (y_dil @ w_br[1])
              + gates[:, :, 2] * (y_ssm @ w_br[2]))
    return merged @ w_out


def setup_inputs(seed: int = 0) -> dict:
    key = jax.random.key(seed)
    ks = iter(jax.random.split(key, 40))
    f32 = jnp.float32
    L, D = DEPTH, D_MODEL

    def nrm(shape, scale):
        return jax.random.normal(next(ks), shape, f32) * scale

    x = nrm((BATCH, SEQ, D), 1.0)
    c = nrm((BATCH, D), 1.0)
    offset = jax.random.randint(next(ks), (BATCH, 1), 0, 1024, dtype=jnp.int32)
    positions = offset + jnp.arange(SEQ, dtype=jnp.int32)[None, :]
    w_ada = nrm((L, D, 9 * D), D ** -0.5)
    b_ada = nrm((L, 9 * D), 0.02)
    ln_g = 1.0 + nrm((L, 3, D), 0.02)
    ln_b = nrm((L, 3, D), 0.02)
    ffn_w1 = nrm((L, 2, D, D_FF), D ** -0.5)
    ffn_w3 = nrm((L, 2, D, D_FF), D ** -0.5)
    ffn_w2 = nrm((L, 2, D_FF, D), D_FF ** -0.5 * DEEPNORM_BETA)
    w_in = nrm((L, D, N_IN), D ** -0.5)
    b_in = nrm((L, N_IN), 0.02)
    mla_q_norm = 1.0 + nrm((L, MLA_Q_RANK), 0.02)
    mla_kv_norm = 1.0 + nrm((L, MLA_KV_RANK), 0.02)
    mla_w_qb = nrm((L, MLA_Q_RANK, MLA_HEADS * (MLA_NOPE_DIM + MLA_ROPE_DIM)), MLA_Q_RANK ** -0.5)
    mla_w_kvb = nrm((L, MLA_KV_RANK, MLA_HEADS * (MLA_NOPE_DIM + MLA_V_DIM)), MLA_KV_RANK ** -0.5)
    n_idx = jnp.arange(SSM_STATE, dtype=f32)
    ssm_lambda_re = -0.5 + nrm((L, SSM_GROUPS, SSM_STATE), 0.01)
    ssm_lambda_im = math.pi * n_idx + nrm((L, SSM_GROUPS, SSM_STATE), 0.01)
    ssm_log_dt = jax.random.uniform(next(ks), (L, SSM_GROUPS), f32, math.log(1e-3), math.log(1e-1))
    ssm_b_re = nrm((L, SSM_GROUPS, SSM_STATE, SSM_GROUP_SIZE), (2 * SSM_GROUP_SIZE) ** -0.5)
    ssm_b_im = nrm((L, SSM_GROUPS, SSM_STATE, SSM_GROUP_SIZE), (2 * SSM_GROUP_SIZE) ** -0.5)
    ssm_c_re = nrm((L, SSM_GROUPS, SSM_GROUP_SIZE, SSM_STATE), (2 * SSM_STATE) ** -0.5)
    ssm_c_im = nrm((L, SSM_GROUPS, SSM_GROUP_SIZE, SSM_STATE), (2 * SSM_STATE) ** -0.5)
    ssm_d = nrm((L, SSM_WIDTH), 1.0)
    ssm_w_glu = nrm((L, SSM_WIDTH, SSM_WIDTH), SSM_WIDTH ** -0.5)
    ssm_b_glu = nrm((L, SSM_WIDTH), 0.02)
    w_br = nrm((L, N_BRANCH, BRANCH_WIDTH, D), BRANCH_WIDTH ** -0.5)
    w_out = nrm((L, D, D), D ** -0.5 * DEEPNORM_BETA)
    return {
        'x': x, 'c': c, 'positions': positions,
        'w_ada': w_ada, 'b_ada': b_ada, 'ln_g': ln_g, 'ln_b': ln_b,
        'ffn_w1': ffn_w1, 'ffn_w3': ffn_w3, 'ffn_w2': ffn_w2,
        'w_in': w_in, 'b_in': b_in,
        'mla_q_norm': mla_q_norm, 'mla_kv_norm': mla_kv_norm, 'mla_w_qb': mla_w_qb, 'mla_w_kvb': mla_w_kvb,
        'ssm_lambda_re': ssm_lambda_re, 'ssm_lambda_im': ssm_lambda_im, 'ssm_log_dt': ssm_log_dt,
        'ssm_b_re': ssm_b_re, 'ssm_b_im': ssm_b_im, 'ssm_c_re': ssm_c_re, 'ssm_c_im': ssm_c_im,
        'ssm_d': ssm_d, 'ssm_w_glu': ssm_w_glu, 'ssm_b_glu': ssm_b_glu,
        'w_br': w_br, 'w_out': w_out,
    }


def reference(x, c, positions, w_ada, b_ada, ln_g, ln_b, ffn_w1, ffn_w3, ffn_w2, w_in, b_in,
              mla_q_norm, mla_kv_norm, mla_w_qb, mla_w_kvb,
              ssm_lambda_re, ssm_lambda_im, ssm_log_dt, ssm_b_re, ssm_b_im, ssm_c_re, ssm_c_im,
              ssm_d, ssm_w_glu, ssm_b_glu, w_br, w_out):
    B, S, D = x.shape
    h = x
    cond = jax.nn.silu(c)
    for l in range(DEPTH):
        ada = (cond @ w_ada[l] + b_ada[l]).reshape(B, 3, 3, D)
        f = swiglu(modulate(h, ada[:, 0, 0], ada[:, 0, 1]), ffn_w1[l, 0], ffn_w3[l, 0], ffn_w2[l, 0])
        h = post_norm(h, f, MACARON_WEIGHT, ada[:, 0, 2], ln_g[l, 0], ln_b[l, 0])
        m = token_mixer(modulate(h, ada[:, 1, 0], ada[:, 1, 1]), positions, w_in[l], b_in[l],
                        mla_q_norm[l], mla_kv_norm[l], mla_w_qb[l], mla_w_kvb[l],
                        ssm_lambda_re[l], ssm_lambda_im[l], ssm_log_dt[l], ssm_b_re[l], ssm_b_im[l],
                        ssm_c_re[l], ssm_c_im[l], ssm_d[l], ssm_w_glu[l], ssm_b_glu[l], w_br[l], w_out[l])
        h = post_norm(h, m, 1.0, ada[:, 1, 2], ln_g[l, 1], ln_b[l, 1])
        f = swiglu(modulate(h, ada[:, 2, 0], ada[:, 2, 1]), ffn_w1[l, 1], ffn_w3[l, 1], ffn_w2[l, 1])
        h = post_norm(h, f, MACARON_WEIGHT, ada[:, 2, 2], ln_g[l, 2], ln_b[l, 2])
    return h
```

```python
from contextlib import ExitStack
import numpy as np
import concourse.bass as bass
import concourse.mybir as mybir
from concourse.bass_utils import run_bass_kernel_spmd

F32 = mybir.dt.float32
BF16 = mybir.dt.bfloat16
I32 = mybir.dt.int32
AF = mybir.ActivationFunctionType
ALU = mybir.AluOpType

NCORES = 8
D = 1024
SEQ = 4096
NB = 2
NTOK = NB * SEQ
DEPTH = 4
DFF = 2816
KC = D // 128
FC = DFF // 128
N_IN = 8864
ALPHA = (2 * DEPTH) ** 0.25
LN_EPS = 1e-5
RMS_EPS = 1e-6
NCST = 24
CH_QA, CH_KVA, CH_KR, CH_DIL, CH_U, CH_GATE = 0, 3, 5, 6, 42, 46
W_CHUNKS = ([(i * 128, 128) for i in range(5)] + [(576, 96)] + [(672 + 128 * i, 128) for i in range(36)]
            + [(5280 + 128 * i, 128) for i in range(4)] + [(5792 + 128 * i, 128) for i in range(24)])
NCH = len(W_CHUNKS)
DIL_PAIRS = ((128, 1), (512, 4), (2048, 16))
MAGIC = 12582912.0
POOL_DSEM = True
TWO_PI = 6.283185307179586


class Prog:
    ENGS = ("sync", "act", "pe", "dve", "pool")

    def __init__(self, nc, es):
        self.nc = nc
        self.es = es
        self.q = {e: [] for e in self.ENGS}
        self.cnt = {e: 0 for e in self.ENGS}
        self.sems = {}
        for e in ("act", "pe", "dve", "pool"):
            self.sems["c_" + e] = es.enter_context(nc.semaphore("c_" + e))
        self.dval = {}
        self.dlast = {}
        self.waited = {e: {} for e in self.ENGS}
        self.res = {}
        self.dkeymap = {}
        self.nblocks = 0
        self.nrot = 0

    def _dsem(self, key):
        sid = self.dkeymap.get(key)
        if sid is None and key.startswith("pc"):
            sid = "d_" + key
            if sid not in self.sems:
                self.sems[sid] = self.es.enter_context(self.nc.semaphore(sid))
                self.dval[sid] = 0
                self.dlast[sid] = None
            return sid
        if sid is None:
            n = len(self.dkeymap)
            sid = "d_%d" % n
            if sid not in self.sems:
                self.sems[sid] = self.es.enter_context(self.nc.semaphore(sid))
                self.dval[sid] = 0
                self.dlast[sid] = None
            self.dkeymap[key] = sid
        return sid

    def op(self, eng, fn, r=(), w=(), dma=None):
        deps = {}

        def add(tok):
            if tok is None:
                return
            cur = deps.get(tok[0])
            if cur is None or cur[1] < tok[1]:
                deps[tok[0]] = tok

        for k in r:
            st = self.res.get(k)
            if st is not None:
                add(st[0])
        for k in w:
            st = self.res.get(k)
            if st is not None:
                add(st[0])
                for t in st[1].values():
                    add(t)
        sid_d = None
        if dma is not None:
            sid_d = self._dsem(dma)
            add(self.dlast[sid_d])
        wq = self.q[eng]
        wd = self.waited[eng]
        for sid, tok in deps.items():
            _, val, e2, idx = tok
            if idx is not None and e2 == eng:
                if eng == "pe":
                    continue
                if self.cnt[eng] - idx >= 2:
                    continue
            if wd.get(sid, 0) >= val:
                continue
            wd[sid] = val
            wq.append((0, sid, val))
        if dma is not None:
            self.dval[sid_d] += 16
            tok = (sid_d, self.dval[sid_d], eng, None)
            self.dlast[sid_d] = tok
            wq.append((2, fn, sid_d))
        else:
            self.cnt[eng] += 1
            tok = ("c_" + eng, self.cnt[eng], eng, self.cnt[eng])
            wq.append((1, fn))
        for k in r:
            st = self.res.get(k)
            if st is None:
                st = [None, {}]
                self.res[k] = st
            st[1][tok[0]] = tok
        for k in w:
            self.res[k] = [tok, {}]

    def barrier(self):
        for e in self.ENGS:
            wd = self.waited[e]
            for sid in self.sems:
                if sid.startswith("c_"):
                    val = self.cnt[sid[2:]]
                else:
                    val = self.dval[sid]
                if val > 0 and wd.get(sid, 0) < val:
                    wd[sid] = val
                    self.q[e].append((0, sid, val))
        self.res = {}
        self.dkeymap = {}

    def rotate(self, limit=30000):
        for sid in list(self.sems):
            if sid.startswith("c_"):
                val = self.cnt[sid[2:]]
            else:
                val = self.dval[sid]
            if val > limit:
                self.nrot += 1
                self.sems[sid] = self.es.enter_context(self.nc.semaphore("%s_r%d" % (sid, self.nrot)))
                if sid.startswith("c_"):
                    self.cnt[sid[2:]] = 0
                else:
                    self.dval[sid] = 0
                    self.dlast[sid] = None
                for e in self.ENGS:
                    self.waited[e].pop(sid, None)

    def flush(self):
        nc = self.nc
        sems = self.sems
        self.nblocks += 1
        with nc.Block() as block:
            for e, deco in (("sync", block.sync), ("act", block.scalar), ("pe", block.tensor),
                            ("dve", block.vector), ("pool", block.gpsimd)):
                items = self.q[e]
                self.q[e] = []
                if not items:
                    continue
                csem = sems.get("c_" + e)

                def body(E, items=items, csem=csem):
                    for it in items:
                        if it[0] == 0:
                            E.wait_ge(sems[it[1]], it[2])
                        elif it[0] == 1:
                            it[1](E).then_inc(csem, 1)
                        elif it[0] == 2:
                            it[1](E).then_inc(sems[it[2]], 16)
                        elif it[0] == 3:
                            E.sem_inc(sems[it[1]], 1)
                        else:
                            E.sem_clear(sems[it[1]])
                deco(body)


class Builder:
    def __init__(self, layers=DEPTH, stop_after=None, skip_mix=False, skip_ffn=False, debug=(), mix_parts="all"):
        self.layers = layers
        self.stop_after = stop_after
        self.skip_mix = skip_mix
        self.skip_ffn = skip_ffn
        self.debug = tuple(debug)
        self.mix_parts = mix_parts
        self.in_names = []
        self.nc = bass.Bass("TRN2", target_bir_lowering=False)

    def sbt(self, name, shape, dt):
        self._u = getattr(self, "_u", 0) + 1
        return self.nc.sbuf_tensor("%s_%d" % (name, self._u), shape, dt)

    def din(self, name, shape, dt=F32):
        self.in_names.append(name)
        return self.nc.dram_tensor(name, list(shape), dt, kind="ExternalInput").ap()

    def dscr(self, name, shape, dt=F32):
        kind = "ExternalOutput" if name in self.debug else "Internal"
        return self.nc.dram_tensor(name, list(shape), dt, kind=kind).ap()

    def dout(self, name, shape, dt=F32):
        return self.nc.dram_tensor(name, list(shape), dt, kind="ExternalOutput").ap()

    def build(self):
        nc = self.nc
        L = self.layers
        self.L = L
        self.xT = self.din("xT", [D, NTOK])
        self.cT = self.din("cT", [128, KC, NB])
        self.w_ada = self.din("w_ada", [L, D, 9 * D])
        self.b_adaT = self.din("b_adaT", [128, L, 72])
        self.lnT = self.din("lnT", [128, L, 3, 2, KC])
        self.cst = self.din("cst", [128, NCST])
        if not self.skip_ffn:
            self.ffn_w1 = self.din("ffn_w1", [L, 2, D, DFF])
            self.ffn_w3 = self.din("ffn_w3", [L, 2, D, DFF])
            self.ffn_w2 = self.din("ffn_w2", [L, 2, DFF, D])
            self.w1q = self.dscr("w1q", [L, 2, FC, 128, KC, 128], BF16)
            self.w3q = self.dscr("w3q", [L, 2, FC, 128, KC, 128], BF16)
            self.w2q = self.dscr("w2q", [L, 2, KC, 128, FC, 128], BF16)
        if not self.skip_mix:
            self.mixer_decl()
        self.yT = self.dout("yT", [D, NTOK])
        self.hS = self.dscr("hS", [D, NTOK])

        with ExitStack() as es:
            self.es = es
            P = self.P = Prog(nc, es)
            self.psum = [es.enter_context(nc.psum_tensor("ps%d" % i, [128, 512], F32)) for i in range(8)]
            self.ones_bf = nc.alloc_sbuf_tensor("ones_bf", [128, 128], BF16)
            self.ada = nc.alloc_sbuf_tensor("ada", [128, 72, NB], F32)
            self.s1p = nc.alloc_sbuf_tensor("s1p", [128, 3, KC, NB], F32)
            self.gp = nc.alloc_sbuf_tensor("gp", [128, 3, KC, NB], F32)
            self.condT = nc.alloc_sbuf_tensor("condT", [128, KC, NB], F32)
            self.lnS = nc.alloc_sbuf_tensor("lnS", [128, L, 3, 2, KC], F32)
            self.b_adaS = nc.alloc_sbuf_tensor("b_adaS", [128, L, 72], F32)
            self.cstS = nc.alloc_sbuf_tensor("cstS", [128, NCST], F32)

            self.phase_init()
            self.phase_precast()
            src = self.xT
            stages = []
            for l in range(self.layers):
                for stg in ("ffn0", "mix", "ffn1"):
                    stages.append((l, stg))
                    if self.stop_after == (l, stg):
                        break
                else:
                    continue
                break
            if self.skip_mix:
                stages = [x for x in stages if x[1] != "mix"]
            if self.skip_ffn:
                stages = [x for x in stages if x[1] == "mix"]
            ada_done = set()
            for n, (l, stg) in enumerate(stages):
                dst = self.yT if n == len(stages) - 1 else self.hS
                if l not in ada_done:
                    self.phase_ada(l)
                    ada_done.add(l)
                if stg == "ffn0":
                    self.phase_ffn(l, 0, src, dst)
                elif stg == "mix":
                    self.phase_mixer(l, src, dst)
                else:
                    self.phase_ffn(l, 1, src, dst)
                src = self.hS
        return nc

    def end_phase(self):
        self.P.barrier()
        self.P.flush()
        self.P.rotate()

    def phase_init(self):
        P = self.P
        ones_bf, condT = self.ones_bf, self.condT
        P.op("pool", lambda E: E.memset(ones_bf[:], 1.0 / D), w=["ones"])
        P.op("sync", lambda E: E.dma_start(out=condT[:], in_=self.cT), w=["condT"], dma="c0")
        P.op("sync", lambda E: E.dma_start(out=self.lnS[:], in_=self.lnT), w=["lnS"], dma="c1")
        P.op("sync", lambda E: E.dma_start(out=self.b_adaS[:], in_=self.b_adaT), w=["b_adaS"], dma="c2")
        P.op("sync", lambda E: E.dma_start(out=self.cstS[:], in_=self.cst), w=["cst"], dma="c3")
        P.op("act", lambda E: E.activation(out=condT[:], in_=condT[:], func=AF.Silu), r=["condT"], w=["condT"])
        self.end_phase()

    def phase_precast(self):
        P = self.P
        n = 0
        if not self.skip_mix:
            n = self.mixer_precast(n)
        for l in range(0 if self.skip_ffn else self.layers):
            for f in range(2):
                for (src, dst, nout, nk) in ((self.ffn_w1, self.w1q, FC, KC), (self.ffn_w3, self.w3q, FC, KC),
                                            (self.ffn_w2, self.w2q, KC, FC)):
                    sv = src[l, f].rearrange("(kc p) (j n) -> j p kc n", p=128, n=128)
                    for j in range(nout):
                        P.op("pool", lambda E, o=dst[l, f, j], i=sv[j]: E.dma_start(out=o, in_=i),
                             w=[], dma="pc%d" % (n % 8))
                        n += 1
        self.end_phase()

    def phase_ada(self, l):
        P, nc = self.P, self.nc
        NPIECE = 8
        CW = 9 * D // NPIECE
        with ExitStack() as pst:
            wt = [pst.enter_context(self.sbt("adaw%d" % i, [128, KC, CW], F32)) for i in range(2)]
            ps = self.psum[0]
            wv = self.w_ada[l].rearrange("(kc p) n -> p kc n", p=128)
            for pc in range(NPIECE):
                t = wt[pc % 2]
                P.op("sync", lambda E, t=t, pc=pc: E.dma_start(out=t[:], in_=wv[:, :, pc * CW:(pc + 1) * CW]),
                     w=["adaw%d" % (pc % 2)], dma="adaw%d" % (pc % 2))
                for cc in range(CW // 128):
                    c = pc * (CW // 128) + cc
                    for kc in range(KC):
                        P.op("pe", lambda E, t=t, cc=cc, kc=kc, c=c: E.matmul(
                            ps[:, c * NB:(c + 1) * NB], lhsT=t[:, kc, cc * 128:(cc + 1) * 128],
                            rhs=self.condT[:, kc, :], start=(kc == 0), stop=(kc == KC - 1)),
                            r=["adaw%d" % (pc % 2), "condT"], w=["ps0"])
            ada = self.ada
            psv = ps[:, 0:72 * NB].rearrange("p (c b) -> p c b", b=NB)
            for b in range(NB):
                P.op("dve", lambda E, b=b: E.tensor_tensor(
                    out=ada[:, :, b], in0=psv[:, :, b], in1=self.b_adaS[:, l, :], op=ALU.add),
                    r=["ps0", "b_adaS"], w=["ada"])
            for s in range(3):
                res_w = 1.0 if s == 1 else 0.5
                P.op("dve", lambda E, s=s: E.tensor_scalar_add(
                    out=self.s1p[:, s], in0=ada[:, s * 24 + 8:s * 24 + 16, :], scalar1=1.0),
                    r=["ada"], w=["s1p"])
                P.op("dve", lambda E, s=s, rw=res_w: E.tensor_scalar_mul(
                    out=self.gp[:, s], in0=ada[:, s * 24 + 16:s * 24 + 24, :], scalar1=rw / ALPHA),
                    r=["ada"], w=["gp"])
            self.end_phase()

    def ln_finish(self, h, zkeys, ykeys, cols, mean_ps, ez_ps, mean_k, ez_k, g_ap, b_ap, bufs, eps):
        P = self.P
        mean, m2, rstd, tmp = bufs
        P.op("act", lambda E: E.activation(out=mean[:], in_=mean_ps[:], func=AF.Copy), r=[mean_k], w=["ln_mean"])
        P.op("pool", lambda E: E.tensor_tensor(out=m2[:], in0=mean[:], in1=mean[:], op=ALU.mult),
             r=["ln_mean"], w=["ln_m2"])
        P.op("dve", lambda E: E.tensor_tensor(out=m2[:], in0=ez_ps[:], in1=m2[:], op=ALU.subtract),
             r=[ez_k, "ln_m2"], w=["ln_m2"])
        P.op("dve", lambda E: E.tensor_scalar_add(out=m2[:], in0=m2[:], scalar1=eps), r=["ln_m2"], w=["ln_m2"])
        P.op("act", lambda E: E.activation(out=rstd[:], in_=m2[:], func=AF.Sqrt), r=["ln_m2"], w=["ln_rstd"])
        P.op("dve", lambda E: E.reciprocal(out=rstd[:], in_=rstd[:]), r=["ln_rstd"], w=["ln_rstd"])
        for i in range(KC):
            t = tmp[i % len(tmp)]
            tk = "ln_tmp%d" % (i % len(tmp))
            hs = h[:, i, cols]
            P.op("dve", lambda E, t=t, hs=hs: E.tensor_tensor(out=t[:], in0=hs, in1=mean[:], op=ALU.subtract),
                 r=[zkeys[i], "ln_mean"], w=[tk])
            P.op("pool", lambda E, t=t: E.tensor_tensor(out=t[:], in0=t[:], in1=rstd[:], op=ALU.mult),
                 r=[tk, "ln_rstd"], w=[tk])
            P.op("act", lambda E, t=t, hs=hs, i=i: E.activation(
                out=hs, in_=t[:], func=AF.Identity, bias=b_ap(i), scale=g_ap(i)),
                r=[tk, "lnS"], w=[ykeys[i]])

    def phase_ffn(self, l, f, src, dst):
        P, nc = self.P, self.nc
        s = 0 if f == 0 else 2
        TT = 1024
        NH = TT // 512
        NST = NTOK // TT
        eps = LN_EPS / (ALPHA * ALPHA)
        with ExitStack() as pst:
            def alloc(name, shape, dt):
                return pst.enter_context(self.sbt(name, shape, dt))
            ht = [alloc("ht%d" % i, [128, KC, TT], F32) for i in range(2)]
            hmod = [alloc("hmod%d" % i, [128, KC, TT], BF16) for i in range(2)]
            a = alloc("a_act", [128, FC, TT], BF16)
            NW = 3
            w1t = [alloc("w1t%d" % i, [128, KC, 128], BF16) for i in range(NW)]
            w3t = [alloc("w3t%d" % i, [128, KC, 128], BF16) for i in range(NW)]
            w2t = [alloc("w2t%d" % i, [128, FC, 128], BF16) for i in range(2)]
            sil = [alloc("sil%d" % i, [128, 512], F32) for i in range(2)]
            zb = [alloc("zb%d" % i, [128, 512], BF16) for i in range(2)]
            zq = [alloc("zq%d" % i, [128, 512], BF16) for i in range(2)]
            lnbufs = (alloc("mean", [128, 512], F32), alloc("m2", [128, 512], F32), alloc("rstd", [128, 512], F32),
                      [alloc("lntmp%d" % i, [128, 512], F32) for i in range(2)])
            ps = self.psum
            srcv = src.rearrange("(kc p) t -> p kc t", p=128)
            dstv = dst.rearrange("(kc p) t -> p kc t", p=128)
            lnS, s1p, gp, ada = self.lnS, self.s1p, self.gp, self.ada
            sh0 = s * 24
            statb = [6, 7, 0, 2]
            cnt = dict(u=0, w=0, w2=0, z=0)

            def load_tile(st):
                t = ht[st % 2]
                P.op("sync", lambda E: E.dma_start(out=t[:], in_=srcv[:, :, st * TT:(st + 1) * TT]),
                     w=["ht%d" % (st % 2)], dma="ht%d" % (st % 2))

            load_tile(0)
            for st in range(NST):
                bl = (st * TT) // SEQ
                h = ht[st % 2]
                hk = "ht%d" % (st % 2)
                hm = hmod[st % 2]
                hm_keys = ["hmod%d_%d" % (st % 2, kc) for kc in range(KC)]
                if st + 1 < NST:
                    load_tile(st + 1)
                for kc in range(KC):
                    P.op("act", lambda E, kc=kc, h=h, hm=hm, bl=bl: E.activation(
                        out=hm[:, kc, :], in_=h[:, kc, :], func=AF.Identity,
                        bias=ada[:, sh0 + kc, bl:bl + 1], scale=s1p[:, s, kc, bl:bl + 1]),
                        r=[hk, "ada", "s1p"], w=[hm_keys[kc]])
                for j in range(FC):
                    wi = cnt["w"] % NW
                    cnt["w"] += 1
                    P.op("sync", lambda E, wi=wi, j=j: E.dma_start(out=w1t[wi][:], in_=self.w1q[l, f, j]),
                         w=["w1t%d" % wi], dma="w1t%d" % wi)
                    P.op("sync", lambda E, wi=wi, j=j: E.dma_start(out=w3t[wi][:], in_=self.w3q[l, f, j]),
                         w=["w3t%d" % wi], dma="w3t%d" % wi)
                    for hf in range(NH):
                        pu = cnt["u"] % 2
                        pg = 2 + cnt["u"] % 2
                        cnt["u"] += 1
                        cols = slice(hf * 512, (hf + 1) * 512)
                        for kc in range(KC):
                            P.op("pe", lambda E, wi=wi, kc=kc, cols=cols, pu=pu, hm=hm: E.matmul(
                                ps[pu][:], lhsT=w1t[wi][:, kc, :], rhs=hm[:, kc, cols],
                                start=(kc == 0), stop=(kc == KC - 1)),
                                r=["w1t%d" % wi, hm_keys[kc]], w=["ps%d" % pu])
                        for kc in range(KC):
                            P.op("pe", lambda E, wi=wi, kc=kc, cols=cols, pg=pg, hm=hm: E.matmul(
                                ps[pg][:], lhsT=w3t[wi][:, kc, :], rhs=hm[:, kc, cols],
                                start=(kc == 0), stop=(kc == KC - 1)),
                                r=["w3t%d" % wi, hm_keys[kc]], w=["ps%d" % pg])
                        sl = sil[pu]
                        P.op("act", lambda E, sl=sl, pu=pu: E.activation(out=sl[:], in_=ps[pu][:], func=AF.Silu),
                             r=["ps%d" % pu], w=["sil%d" % pu])
                        P.op("dve", lambda E, sl=sl, pg=pg, j=j, cols=cols: E.tensor_tensor(
                            out=a[:, j, cols], in0=ps[pg][:], in1=sl[:], op=ALU.mult),
                            r=["ps%d" % pg, "sil%d" % pu], w=["a_%d_%d" % (j, hf)])
                for i in range(KC):
                    wi = cnt["w2"] % 2
                    cnt["w2"] += 1
                    P.op("sync", lambda E, wi=wi, i=i: E.dma_start(out=w2t[wi][:], in_=self.w2q[l, f, i]),
                         w=["w2t%d" % wi], dma="w2t%d" % wi)
                    for hf in range(NH):
                        pf = 4 + cnt["z"] % 2
                        zi = cnt["z"] % 2
                        cnt["z"] += 1
                        cols = slice(hf * 512, (hf + 1) * 512)
                        for j in range(FC):
                            P.op("pe", lambda E, wi=wi, j=j, cols=cols, pf=pf: E.matmul(
                                ps[pf][:], lhsT=w2t[wi][:, j, :], rhs=a[:, j, cols],
                                start=(j == 0), stop=(j == FC - 1)),
                                r=["w2t%d" % wi, "a_%d_%d" % (j, hf)], w=["ps%d" % pf])
                        hs = h[:, i, cols]
                        zk = hk + "_z%d_%d" % (i, hf)
                        P.op("dve", lambda E, pf=pf, hs=hs, i=i, bl=bl: E.scalar_tensor_tensor(
                            out=hs, in0=ps[pf][:], scalar=gp[:, s, i, bl:bl + 1], in1=hs,
                            op0=ALU.mult, op1=ALU.add),
                            r=["ps%d" % pf, "gp", hk], w=[zk])
                        P.op("act", lambda E, hs=hs, zi=zi: E.activation(out=zb[zi][:], in_=hs, func=AF.Copy),
                             r=[zk], w=["zb%d" % zi])
                        P.op("act", lambda E, hs=hs, zi=zi: E.activation(out=zq[zi][:], in_=hs, func=AF.Square),
                             r=[zk], w=["zq%d" % zi])
                        bm, be = statb[2 * hf], statb[2 * hf + 1]
                        P.op("pe", lambda E, zi=zi, bm=bm, i=i: E.matmul(
                            ps[bm][:], lhsT=self.ones_bf[:], rhs=zb[zi][:], start=(i == 0), stop=(i == KC - 1)),
                            r=["ones", "zb%d" % zi], w=["ps%d" % bm])
                        P.op("pe", lambda E, zi=zi, be=be, i=i: E.matmul(
                            ps[be][:], lhsT=self.ones_bf[:], rhs=zq[zi][:], start=(i == 0), stop=(i == KC - 1)),
                            r=["ones", "zq%d" % zi], w=["ps%d" % be])
                ykeys_all = []
                for hf in range(NH):
                    cols = slice(hf * 512, (hf + 1) * 512)
                    bm, be = statb[2 * hf], statb[2 * hf + 1]
                    zkeys = [hk + "_z%d_%d" % (i, hf) for i in range(KC)]
                    ykeys = [hk + "_y%d_%d" % (i, hf) for i in range(KC)]
                    ykeys_all += ykeys
                    self.ln_finish(h, zkeys, ykeys, cols, ps[bm], ps[be], "ps%d" % bm, "ps%d" % be,
                                   lambda i: lnS[:, l, s, 0, i:i + 1], lambda i: lnS[:, l, s, 1, i:i + 1],
                                   lnbufs, eps)
                P.op("sync", lambda E, st=st, h=h: E.dma_start(out=dstv[:, :, st * TT:(st + 1) * TT], in_=h[:]),
                     r=ykeys_all, w=[hk], dma="hst%d" % (st % 2))
            self.end_phase()


    def mixer_decl(self):
        L = self.L
        self.w_in = self.din("w_in", [L, D, N_IN])
        self.b_inT = self.din("b_inT", [128, L, NCH + 1])
        self.b_vrow = self.din("b_vrow", [L, 1, 1536])
        self.qnT = self.din("qnT", [128, L, 3])
        self.kvnT = self.din("kvnT", [128, L, 2])
        self.w_qb = self.din("w_qb", [L, 384, 768])
        self.w_kvb = self.din("w_kvb", [L, 256, 1024])
        self.w_glu = self.din("w_glu", [L, 512, 512])
        self.b_gluT = self.din("b_gluT", [128, L, 4])
        self.w_br = self.din("w_br", [L, 3, 512, D])
        self.w_out = self.din("w_out", [L, D, D])
        self.pos = self.din("pos", [NB, SEQ], I32)
        self.maskc = self.din("maskc", [128, 4, 512], BF16)
        self.dilmb = self.din("dilmb", [128, 12, 256])
        self.ident = self.din("ident", [128, 128])
        self.pswap = self.din("pswap", [128, 128])
        self.lamreT = self.din("lamreT", [128, L, 32])
        self.lamimT = self.din("lamimT", [128, L, 32])
        self.logdtB = self.din("logdtB", [128, L, 32])
        self.bX = self.din("bX", [128, L, 32, 16])
        self.bY = self.din("bY", [128, L, 32, 16])
        self.cX = self.din("cX", [128, L, 32, 16])
        self.cY = self.din("cY", [128, L, 32, 16])
        self.dT = self.din("dT", [128, L, 4])
        self.winq = self.dscr("winq", [L, NCH, 128, KC, 128], BF16)
        self.wksw = self.dscr("wksw", [L, 128, KC, 96], BF16)
        self.wvq = self.dscr("wvq", [L, 128, KC, 1536], BF16)
        self.wqbq = self.dscr("wqbq", [L, 128, 3, 768], BF16)
        self.wqbs = self.dscr("wqbs", [L, 128, 3, 768], BF16)
        self.wkvbq = self.dscr("wkvbq", [L, 128, 2, 1024], BF16)
        self.wgluq = self.dscr("wgluq", [L, 128, 4, 512], BF16)
        self.wbrq = self.dscr("wbrq", [L, 3, 128, 4, D], BF16)
        self.wbr0q = self.dscr("wbr0q", [L, 64, 8, D], BF16)
        self.woutq = self.dscr("woutq", [L, 128, KC, D], BF16)
        self.qhS = self.dscr("qhS", [8, 96, SEQ], BF16)
        self.khS = self.dscr("khS", [8, 96, SEQ], BF16)
        self.vS = self.dscr("vS", [SEQ, 512], BF16)
        self.ymlaS = self.dscr("ymlaS", [8, 64, SEQ], BF16)
        self.ydilS = self.dscr("ydilS", [4, 128, SEQ], BF16)
        self.yssmS = self.dscr("yssmS", [4, 128, SEQ], BF16)
        self.tabS = self.dscr("tabS", [4, 128, 8, 2, 128])
        self.wabS = self.dscr("wabS", [4, 128, 8, 2, 128], BF16)
        self.wcS = self.dscr("wcS", [4, 128, 8, 2, 128], BF16)

    def mixer_precast(self, n):
        P = self.P

        def cast(o, i):
            nonlocal n
            P.op("pool", lambda E, o=o, i=i: E.dma_start(out=o, in_=i), w=[], dma="pc%d" % (n % 8))
            n += 1
        for l in range(self.L):
            wi = self.w_in[l]
            for c, (cs, cw) in enumerate(W_CHUNKS):
                cast(self.winq[l, c, :, :, 0:cw], wi[:, cs:cs + cw].rearrange("(kc p) n -> p kc n", p=128))
            for (d0, s0, wdt) in ((0, 576, 64), (64, 656, 16), (80, 640, 16)):
                cast(self.wksw[l, :, :, d0:d0 + wdt], wi[:, s0:s0 + wdt].rearrange("(kc p) n -> p kc n", p=128))
            for kc in range(KC):
                cast(self.wvq[l, :, kc, :], wi[kc * 128:(kc + 1) * 128, 672 + 3072:672 + 4608])
            cast(self.wqbq[l], self.w_qb[l].rearrange("(kc p) n -> p kc n", p=128))
            qv = self.w_qb[l].rearrange("(kc p) (h n) -> p kc h n", p=128, n=96)
            qs = self.wqbs[l].rearrange("p kc (h n) -> p kc h n", n=96)
            for kc in range(3):
                for (d0, s0, wdt) in ((0, 0, 64), (64, 80, 16), (80, 64, 16)):
                    cast(qs[:, kc, :, d0:d0 + wdt], qv[:, kc, :, s0:s0 + wdt])
            cast(self.wkvbq[l], self.w_kvb[l].rearrange("(kc p) n -> p kc n", p=128))
            cast(self.wgluq[l], self.w_glu[l].rearrange("(kc p) n -> p kc n", p=128))
            for k in range(3):
                cast(self.wbrq[l, k], self.w_br[l, k].rearrange("(kc p) n -> p kc n", p=128))
            cast(self.wbr0q[l], self.w_br[l, 0].rearrange("(h p) n -> p h n", p=64))
            for kc in range(KC):
                cast(self.woutq[l, :, kc, :], self.w_out[l, kc * 128:(kc + 1) * 128, :])
        return n

    def phase_mixer(self, l, src, dst):
        nc = self.nc
        parts = self.mix_parts
        if parts == "all" or "ssm" in parts:
            self.mix_consts(l)
        for bl in range(NB):
            with ExitStack() as seq:
                hmod = seq.enter_context(self.sbt("hmodS", [128, KC, SEQ], BF16))
                self.mix_s1(l, bl, src, hmod)
                if parts == "all" or "dil" in parts:
                    self.mix_dil(l, bl, hmod)
                if parts == "all" or "ssm" in parts:
                    self.mix_ssm(l, bl, hmod)
            if parts == "all" or "mla" in parts:
                self.mix_mla(l, bl)
            if parts == "all" or "merge" in parts:
                self.mix_merge(l, bl, src, dst)

    def mix_s1(self, l, bl, src, hmod):
        P, nc, ps = self.P, self.nc, self.psum
        cst = self.cstS
        tok0 = bl * SEQ
        srcv = src.rearrange("(kc p) t -> p kc t", p=128)
        sh0 = 24
        with ExitStack() as st:
            def A(name, shape, dt):
                return st.enter_context(self.sbt(name, shape, dt))
            ht = [A("s1ht%d" % i, [128, KC, 512], F32) for i in range(2)]
            wlat = A("wlat", [128, 6, KC, 128], BF16)
            wks = A("wks", [128, KC, 96], BF16)
            wq = A("wq", [128, 3, 768], BF16)
            wqs = A("wqs", [128, 3, 768], BF16)
            wkv = A("wkv", [128, 2, 1024], BF16)
            bS = A("b_inS", [128, NCH + 1], F32)
            qn = A("qnS", [128, 3], F32)
            kvng = A("kvngS", [128, 2], F32)
            onesq = A("onesq", [128, 128], BF16)
            oneskv = A("oneskv", [128, 128], BF16)
            latf = [A("latf%d" % i, [128, 512], F32) for i in range(5)]
            sq = [A("sq%d" % i, [128, 512], BF16) for i in range(2)]
            rq = [A("rq%d" % i, [128, 512], F32) for i in range(2)]
            qan_t = A("qan_t", [128, 3, 512], BF16)
            kvn_t = A("kvn_t", [128, 2, 512], BF16)
            kpe_t = A("kpe_t", [128, 512], BF16)
            posi = A("posi", [128, 512], I32)
            ang = A("ang", [128, 512], F32)
            kk = A("kk", [128, 512], F32)
            CC = A("CC", [128, 512], F32)
            SS = A("SS", [128, 512], F32)
            tmp1 = [A("rt1_%d" % i, [128, 512], F32) for i in range(2)]
            tmp2 = [A("rt2_%d" % i, [128, 512], F32) for i in range(2)]
            qh_t = [A("qh_t%d" % i, [128, 512], BF16) for i in range(2)]
            kh_t = [A("kh_t%d" % i, [128, 512], BF16) for i in range(2)]
            v_t = [A("v_t%d" % i, [128, 4, 512], BF16) for i in range(2)]
            R = slice(64, 96)
            P.op("sync", lambda E: E.dma_start(out=wlat[:], in_=self.winq[l, 0:6].rearrange("c p k n -> p c k n")),
                 w=["wlat"], dma="s1w0")
            P.op("sync", lambda E: E.dma_start(out=wks[:], in_=self.wksw[l]), w=["wks"], dma="s1w1")
            P.op("sync", lambda E: E.dma_start(out=wq[:], in_=self.wqbq[l]), w=["wq"], dma="s1w2")
            P.op("sync", lambda E: E.dma_start(out=wqs[:], in_=self.wqbs[l]), w=["wqs"], dma="s1w3")
            P.op("sync", lambda E: E.dma_start(out=wkv[:], in_=self.wkvbq[l]), w=["wkv"], dma="s1w4")
            P.op("sync", lambda E: E.dma_start(out=bS[:], in_=self.b_inT[:, l, :]), w=["bS"], dma="s1w5")
            P.op("sync", lambda E: E.dma_start(out=qn[:], in_=self.qnT[:, l, :]), w=["qn"], dma="s1w6")
            P.op("sync", lambda E: E.dma_start(out=kvng[:], in_=self.kvnT[:, l, :]), w=["kvng"], dma="s1w7")
            P.op("pool", lambda E: E.memset(onesq[:], 1.0 / 384), w=["onesq"])
            P.op("pool", lambda E: E.memset(oneskv[:], 1.0 / 256), w=["oneskv"])
            wvv = wkv[:].rearrange("p k (h two d) -> p k h two d", two=2, d=64)
            cnt = dict(b=0, r=0, h=0)

            def bank2(base):
                b = base + cnt["b"] % 2
                cnt["b"] += 1
                return b
            for t in range(SEQ // 512):
                h = ht[t % 2]
                hk = "s1ht%d" % (t % 2)
                cols = slice(t * 512, (t + 1) * 512)
                P.op("sync", lambda E, h=h, t=t: E.dma_start(
                    out=h[:], in_=srcv[:, :, tok0 + t * 512:tok0 + (t + 1) * 512]), w=[hk], dma=hk)
                for kc in range(KC):
                    P.op("act", lambda E, kc=kc, h=h, cols=cols: E.activation(
                        out=hmod[:, kc, cols], in_=h[:, kc, :], func=AF.Identity,
                        bias=self.ada[:, sh0 + kc, bl:bl + 1], scale=self.s1p[:, 1, kc, bl:bl + 1]),
                        r=[hk, "ada", "s1p"], w=["hmod_%d_%d" % (t, kc)])
                hmk = ["hmod_%d_%d" % (t, kc) for kc in range(KC)]
                P.op("sync", lambda E, t=t: E.dma_start(
                    out=posi[R, :], in_=self.pos[bl:bl + 1, t * 512:(t + 1) * 512].partition_broadcast(32)),
                    w=["posi"], dma="posi")
                P.op("dve", lambda E: E.tensor_copy(out=ang[R, :], in_=posi[R, :]), r=["posi"], w=["ang"])
                P.op("dve", lambda E: E.tensor_scalar(out=ang[R, :], in0=ang[R, :], scalar1=cst[R, 0:1], scalar2=None,
                                                      op0=ALU.mult), r=["ang", "cst"], w=["ang"])
                P.op("dve", lambda E: E.tensor_scalar(out=kk[R, :], in0=ang[R, :], scalar1=1.0 / TWO_PI, scalar2=MAGIC,
                                                      op0=ALU.mult, op1=ALU.add), r=["ang"], w=["kk"])
                P.op("dve", lambda E: E.tensor_scalar_sub(out=kk[R, :], in0=kk[R, :], scalar1=MAGIC), r=["kk"], w=["kk"])
                P.op("dve", lambda E: E.scalar_tensor_tensor(out=ang[R, :], in0=kk[R, :], scalar=-TWO_PI, in1=ang[R, :],
                                                             op0=ALU.mult, op1=ALU.add), r=["kk", "ang"], w=["ang"])
                P.op("act", lambda E: E.activation(out=SS[R, :], in_=ang[R, :], func=AF.Sin, scale=cst[R, 1:2]),
                     r=["ang", "cst"], w=["SS"])
                P.op("act", lambda E: E.activation(out=kk[R, :], in_=ang[R, :], func=AF.Abs), r=["ang"], w=["kk"])
                P.op("act", lambda E: E.activation(out=CC[R, :], in_=kk[R, :], func=AF.Sin, scale=-1.0, bias=cst[R, 6:7]),
                     r=["kk", "cst"], w=["CC"])
                for c in range(5):
                    pb = bank2(0)
                    for kc in range(KC):
                        P.op("pe", lambda E, c=c, kc=kc, pb=pb, cols=cols: E.matmul(
                            ps[pb][:], lhsT=wlat[:, c, kc, :], rhs=hmod[:, kc, cols], start=(kc == 0), stop=(kc == KC - 1)),
                            r=["wlat", hmk[kc]], w=["ps%d" % pb])
                    P.op("act", lambda E, c=c, pb=pb: E.activation(out=latf[c][:], in_=ps[pb][:], func=AF.Identity,
                                                                   bias=bS[:, c:c + 1]), r=["ps%d" % pb, "bS"], w=["latf%d" % c])
                    sqi = c % 2
                    P.op("act", lambda E, c=c, pb=pb, sqi=sqi: E.activation(out=sq[sqi][:], in_=ps[pb][:], func=AF.Square,
                                                                            bias=bS[:, c:c + 1]), r=["ps%d" % pb, "bS"], w=["sq%d" % sqi])
                    if c < 3:
                        P.op("pe", lambda E, c=c, sqi=sqi: E.matmul(ps[2][:], lhsT=onesq[:], rhs=sq[sqi][:],
                                                                  start=(c == 0), stop=(c == 2)), r=["onesq", "sq%d" % sqi], w=["ps2"])
                    else:
                        P.op("pe", lambda E, c=c, sqi=sqi: E.matmul(ps[3][:], lhsT=oneskv[:], rhs=sq[sqi][:],
                                                                  start=(c == 3), stop=(c == 4)), r=["oneskv", "sq%d" % sqi], w=["ps3"])
                for wi_, (pbk, c0, nch, gain, outt, key) in enumerate(((2, 0, 3, qn, qan_t, "qan_t"), (3, 3, 2, kvng, kvn_t, "kvn_t"))):
                    rr = rq[wi_]
                    rk = "rq%d" % wi_
                    P.op("dve", lambda E, rr=rr, pbk=pbk: E.tensor_scalar_add(out=rr[:], in0=ps[pbk][:], scalar1=RMS_EPS),
                         r=["ps%d" % pbk], w=[rk])
                    P.op("act", lambda E, rr=rr: E.activation(out=rr[:], in_=rr[:], func=AF.Sqrt), r=[rk], w=[rk])
                    P.op("dve", lambda E, rr=rr: E.reciprocal(out=rr[:], in_=rr[:]), r=[rk], w=[rk])
                    for j in range(nch):
                        P.op("dve", lambda E, rr=rr, j=j, c0=c0, gain=gain, outt=outt: E.scalar_tensor_tensor(
                            out=outt[:, j, :], in0=latf[c0 + j][:], scalar=gain[:, j:j + 1], in1=rr[:],
                            op0=ALU.mult, op1=ALU.mult), r=["latf%d" % (c0 + j), rk, "qn", "kvng"], w=[key + "%d" % j])
                qank = ["qan_t%d" % j for j in range(3)]
                kvnk = ["kvn_t%d" % j for j in range(2)]
                for kc in range(KC):
                    P.op("pe", lambda E, kc=kc, cols=cols: E.matmul(ps[4][0:96, :], lhsT=wlat[:, 5, kc, 0:96], rhs=hmod[:, kc, cols],
                                                                    start=(kc == 0), stop=(kc == KC - 1)), r=["wlat", hmk[kc]], w=["ps4"])
                for kc in range(KC):
                    P.op("pe", lambda E, kc=kc, cols=cols: E.matmul(ps[5][0:96, :], lhsT=wks[:, kc, :], rhs=hmod[:, kc, cols],
                                                                    start=(kc == 0), stop=(kc == KC - 1)), r=["wks", hmk[kc]], w=["ps5"])
                P.op("dve", lambda E: E.scalar_tensor_tensor(out=tmp1[0][R, :], in0=ps[4][R, :], scalar=bS[R, CH_KR:CH_KR + 1],
                                                             in1=CC[R, :], op0=ALU.add, op1=ALU.mult), r=["ps4", "bS", "CC"], w=["rt1_0"])
                P.op("dve", lambda E: E.scalar_tensor_tensor(out=tmp2[0][R, :], in0=ps[5][R, :], scalar=bS[R, NCH:NCH + 1],
                                                             in1=SS[R, :], op0=ALU.add, op1=ALU.mult), r=["ps5", "bS", "SS"], w=["rt2_0"])
                P.op("pool", lambda E: E.tensor_tensor(out=kpe_t[R, :], in0=tmp1[0][R, :], in1=tmp2[0][R, :], op=ALU.add),
                     r=["rt1_0", "rt2_0"], w=["kpe_t"])
                for hd in range(8):
                    i2 = cnt["h"] % 2
                    cnt["h"] += 1
                    for kc in range(3):
                        P.op("pe", lambda E, kc=kc, hd=hd: E.matmul(ps[4][0:96, :], lhsT=wq[:, kc, hd * 96:(hd + 1) * 96], rhs=qan_t[:, kc, :],
                                                                   start=(kc == 0), stop=(kc == 2)), r=["wq", qank[kc]], w=["ps4"])
                    for kc in range(3):
                        P.op("pe", lambda E, kc=kc, hd=hd: E.matmul(ps[5][0:96, :], lhsT=wqs[:, kc, hd * 96:(hd + 1) * 96], rhs=qan_t[:, kc, :],
                                                                   start=(kc == 0), stop=(kc == 2)), r=["wqs", qank[kc]], w=["ps5"])
                    for kc in range(2):
                        P.op("pe", lambda E, kc=kc, hd=hd: E.matmul(ps[6][0:64, :], lhsT=wkv[:, kc, hd * 128:hd * 128 + 64], rhs=kvn_t[:, kc, :],
                                                                   start=(kc == 0), stop=(kc == 1)), r=["wkv", kvnk[kc]], w=["ps6"])
                    qt_, kt_ = qh_t[i2], kh_t[i2]
                    qk_, kk_ = "qh_t%d" % i2, "kh_t%d" % i2
                    P.op("act", lambda E, qt_=qt_: E.activation(out=qt_[0:64, :], in_=ps[4][0:64, :], func=AF.Copy),
                         r=["ps4"], w=[qk_ + "a"])
                    P.op("dve", lambda E, i2=i2: E.tensor_tensor(out=tmp1[i2][R, :], in0=ps[4][R, :], in1=CC[R, :], op=ALU.mult),
                         r=["ps4", "CC"], w=["rt1_%d" % i2])
                    P.op("dve", lambda E, i2=i2: E.tensor_tensor(out=tmp2[i2][R, :], in0=ps[5][R, :], in1=SS[R, :], op=ALU.mult),
                         r=["ps5", "SS"], w=["rt2_%d" % i2])
                    P.op("pool", lambda E, i2=i2, qt_=qt_: E.tensor_tensor(out=qt_[R, :], in0=tmp1[i2][R, :], in1=tmp2[i2][R, :], op=ALU.add),
                         r=["rt1_%d" % i2, "rt2_%d" % i2], w=[qk_ + "b"])
                    P.op("sync", lambda E, hd=hd, qt_=qt_, cols=cols: E.dma_start(out=self.qhS[hd, :, cols], in_=qt_[0:96, :]),
                         r=[qk_ + "a", qk_ + "b"], w=[], dma=qk_)
                    P.op("act", lambda E, kt_=kt_: E.activation(out=kt_[0:64, :], in_=ps[6][0:64, :], func=AF.Copy),
                         r=["ps6"], w=[kk_ + "a"])
                    P.op("pool", lambda E, kt_=kt_: E.tensor_copy(out=kt_[R, :], in_=kpe_t[R, :]), r=["kpe_t"], w=[kk_ + "b"])
                    P.op("sync", lambda E, hd=hd, kt_=kt_, cols=cols: E.dma_start(out=self.khS[hd, :, cols], in_=kt_[0:96, :]),
                         r=[kk_ + "a", kk_ + "b"], w=[], dma=kk_)
                vt = v_t[t % 2]
                vk = "v_t%d" % (t % 2)
                for blk in range(4):
                    for kc in range(2):
                        P.op("pe", lambda E, kc=kc, blk=blk: E.matmul(
                            ps[7][:].rearrange("p (h d) -> p h d", d=64), lhsT=kvn_t[:, kc, blk * 128:(blk + 1) * 128],
                            rhs=wvv[:, kc, :, 1, :], start=(kc == 0), stop=(kc == 1)), r=["wkv", kvnk[kc]], w=["ps7"])
                    P.op("act", lambda E, blk=blk, vt=vt: E.activation(out=vt[:, blk, :], in_=ps[7][:], func=AF.Copy),
                         r=["ps7"], w=[vk + "_%d" % blk])
                P.op("sync", lambda E, vt=vt, t=t: E.dma_start(
                    out=self.vS[t * 512:(t + 1) * 512, :].rearrange("(b p) n -> p b n", p=128), in_=vt[:]),
                    r=[vk + "_%d" % blk for blk in range(4)], w=[], dma=vk)
            self.end_phase()

    def mix_mla(self, l, bl):
        P, nc, ps = self.P, self.nc, self.psum
        scale = 96.0 ** -0.5
        with ExitStack() as st:
            def A(name, shape, dt):
                return st.enter_context(self.sbt(name, shape, dt))
            Vall = A("Vall", [128, 32, 512], BF16)
            masks = A("masks", [128, 4, 512], BF16)
            ones64 = A("ones64", [128, 64], BF16)
            qh = [A("qh%d" % i, [128, SEQ], BF16) for i in range(2)]
            kh = [A("kh%d" % i, [128, SEQ], BF16) for i in range(2)]
            pt = [A("pt%d" % i, [128, 512], BF16) for i in range(4)]
            rec = [A("rec%d" % i, [128, 512], F32) for i in range(2)]
            yh = [A("yh%d" % i, [128, SEQ], BF16) for i in range(2)]
            P.op("sync", lambda E: E.dma_start(out=Vall[:], in_=self.vS.rearrange("(b p) n -> p b n", p=128)),
                 w=["Vall"], dma="mlaw0")
            P.op("sync", lambda E: E.dma_start(out=masks[:], in_=self.maskc), w=["masks"], dma="mlaw1")
            P.op("pool", lambda E: E.memset(ones64[:], 1.0), w=["ones64"])
            m = 0
            n = 0
            for hd in range(8):
                q_, k_, y_ = qh[hd % 2], kh[hd % 2], yh[hd % 2]
                qk, kk_, yk = "qh%d" % (hd % 2), "kh%d" % (hd % 2), "yh%d" % (hd % 2)
                P.op("sync", lambda E, hd=hd, q_=q_: E.dma_start(out=q_[0:96, :], in_=self.qhS[hd]), w=[qk], dma=qk)
                P.op("sync", lambda E, hd=hd, k_=k_: E.dma_start(out=k_[0:96, :], in_=self.khS[hd]), w=[kk_], dma=kk_)
                for qt in range(SEQ // 512):
                    po, pd = 4 + n % 2, 6 + n % 2
                    ri = n % 2
                    n += 1
                    qcols = slice(qt * 512, (qt + 1) * 512)
                    nkb = 4 * (qt + 1)
                    for kb in range(nkb):
                        sb = m % 4
                        m += 1
                        P.op("pe", lambda E, kb=kb, sb=sb, q_=q_, k_=k_, qcols=qcols: E.matmul(
                            ps[sb][:], lhsT=k_[0:96, kb * 128:(kb + 1) * 128], rhs=q_[0:96, qcols], start=True, stop=True),
                            r=[qk, kk_], w=["ps%d" % sb])
                        P.op("act", lambda E, sb=sb: E.activation(out=pt[sb][:], in_=ps[sb][:], func=AF.Exp, scale=scale),
                             r=["ps%d" % sb], w=["pt%d" % sb])
                        if kb >= 4 * qt:
                            P.op("pool", lambda E, sb=sb, j=kb - 4 * qt: E.tensor_tensor(
                                out=pt[sb][:], in0=pt[sb][:], in1=masks[:, j, :], op=ALU.mult),
                                r=["pt%d" % sb, "masks"], w=["pt%d" % sb])
                        P.op("pe", lambda E, kb=kb, sb=sb, po=po, hd=hd, nkb=nkb: E.matmul(
                            ps[po][0:64, :], lhsT=Vall[:, kb, hd * 64:(hd + 1) * 64], rhs=pt[sb][:],
                            start=(kb == 0), stop=(kb == nkb - 1)), r=["Vall", "pt%d" % sb], w=["ps%d" % po])
                        P.op("pe", lambda E, kb=kb, sb=sb, pd=pd, nkb=nkb: E.matmul(
                            ps[pd][0:64, :], lhsT=ones64[:], rhs=pt[sb][:],
                            start=(kb == 0), stop=(kb == nkb - 1)), r=["ones64", "pt%d" % sb], w=["ps%d" % pd])
                    P.op("dve", lambda E, pd=pd, ri=ri: E.reciprocal(out=rec[ri][0:64, :], in_=ps[pd][0:64, :]),
                         r=["ps%d" % pd], w=["rec%d" % ri])
                    P.op("dve", lambda E, po=po, ri=ri, y_=y_, qcols=qcols: E.tensor_tensor(
                        out=y_[0:64, qcols], in0=ps[po][0:64, :], in1=rec[ri][0:64, :], op=ALU.mult),
                        r=["ps%d" % po, "rec%d" % ri], w=[yk + "_%d" % qt])
                P.op("sync", lambda E, hd=hd, y_=y_: E.dma_start(out=self.ymlaS[hd], in_=y_[0:64, :]),
                     r=[yk + "_%d" % qt for qt in range(SEQ // 512)], w=[yk], dma=yk)
            self.end_phase()
    def mix_dil(self, l, bl, hmod):
        P, nc, ps = self.P, self.nc, self.psum
        scale = 128.0 ** -0.5
        with ExitStack() as st:
            def A(name, shape, dt):
                return st.enter_context(self.sbt(name, shape, dt))
            MB = A("MB", [128, 12, 256], F32)
            bvb = A("bvb", [128, 1536], F32)
            wv = A("wvd", [128, KC, 1536], BF16)
            ones128 = A("ones128", [128, 128], BF16)
            bS = A("b_inD", [128, NCH + 1], F32)
            num = A("num", [128, SEQ], F32)
            den = A("den", [128, SEQ], F32)
            wqk = [A("wqk%d" % i, [128, 2, KC, 128], BF16) for i in range(2)]
            qd = A("qd", [128, SEQ], BF16)
            kd = A("kd", [128, SEQ], BF16)
            Vh = A("Vh", [128, 32, 128], BF16)
            stmp = [A("stmp%d" % i, [128, 256], F32) for i in range(2)]
            ptd = [A("ptd%d" % i, [128, 256], BF16) for i in range(2)]
            yd = A("yd", [128, SEQ], BF16)
            P.op("sync", lambda E: E.dma_start(out=MB[:], in_=self.dilmb), w=["MB"], dma="dw0")
            P.op("sync", lambda E: E.dma_start(out=bvb[:], in_=self.b_vrow[l].partition_broadcast(128)), w=["bvb"], dma="dw1")
            P.op("sync", lambda E: E.dma_start(out=wv[:], in_=self.wvq[l]), w=["wvd"], dma="dw2")
            P.op("sync", lambda E: E.dma_start(out=bS[:], in_=self.b_inT[:, l, :]), w=["bSd"], dma="dw3")
            P.op("pool", lambda E: E.memset(ones128[:], 1.0), w=["ones128"])
            hmk_all = "hmod_all"
            cnt = dict(s=0, q=0)
            for slot in range(4):
                for g, (window, d) in enumerate(DIL_PAIRS):
                    hd = g * 4 + slot
                    Ls = SEQ // d
                    nbk = Ls // 128
                    wi = cnt["q"] % 2
                    cnt["q"] += 1
                    wt = wqk[wi]
                    wk_ = "wqk%d" % wi
                    P.op("sync", lambda E, wt=wt, hd=hd: E.dma_start(out=wt[:, 0], in_=self.winq[l, CH_DIL + hd]),
                         w=[wk_ + "q"], dma=wk_ + "q")
                    P.op("sync", lambda E, wt=wt, hd=hd: E.dma_start(out=wt[:, 1], in_=self.winq[l, CH_DIL + 12 + hd]),
                         w=[wk_ + "k"], dma=wk_ + "k")
                    for t in range(SEQ // 512):
                        cols = slice(t * 512, (t + 1) * 512)
                        for which, dstT, key, bank in ((0, qd, "qd", 0), (1, kd, "kd", 1)):
                            for kc in range(KC):
                                P.op("pe", lambda E, wt=wt, which=which, kc=kc, cols=cols, bank=bank: E.matmul(
                                    ps[bank][:], lhsT=wt[:, which, kc, :], rhs=hmod[:, kc, cols],
                                    start=(kc == 0), stop=(kc == KC - 1)), r=[wk_ + "qk"[which], hmk_all], w=["ps%d" % bank])
                            n_i = 512 // d
                            dv = dstT[:].rearrange("p (r i) -> p r i", r=d)[:, :, t * n_i:(t + 1) * n_i]
                            sv = ps[bank][:].rearrange("p (i r) -> p r i", r=d)
                            bcol = CH_DIL + which * 12 + hd
                            P.op("act", lambda E, dv=dv, sv=sv, bcol=bcol: E.activation(
                                out=dv, in_=sv, func=AF.Identity, bias=bS[:, bcol:bcol + 1]),
                                r=["ps%d" % bank, "bSd"], w=[key])
                    for blk in range(32):
                        r_, kb = blk // nbk, blk % nbk
                        t0 = r_ + d * kb * 128
                        for kc in range(KC):
                            P.op("pe", lambda E, kc=kc, blk=blk, t0=t0, d=d, hd=hd: E.matmul(
                                ps[2][:, (blk % 4) * 128:(blk % 4 + 1) * 128],
                                lhsT=hmod[:, kc, t0:t0 + d * 127 + 1:d], rhs=wv[:, kc, hd * 128:(hd + 1) * 128],
                                start=(kc == 0), stop=(kc == KC - 1)), r=[hmk_all, "wvd"], w=["ps2"])
                        if blk % 4 == 3:
                            P.op("dve", lambda E, blk=blk, hd=hd: E.tensor_tensor(
                                out=Vh[:, blk - 3:blk + 1, :], in0=ps[2][:].rearrange("p (b n) -> p b n", n=128),
                                in1=bvb[:, hd * 128:(hd + 1) * 128].unsqueeze(1).broadcast_to([128, 4, 128]), op=ALU.add),
                                r=["ps2", "bvb"], w=["Vh"])
                    for r_ in range(d):
                        for kb in range(nbk):
                            ncol = 256 if kb < nbk - 1 else 128
                            si = cnt["s"] % 2
                            cnt["s"] += 1
                            sbank = 6 + si
                            c0 = r_ * Ls + kb * 128
                            P.op("pe", lambda E, c0=c0, ncol=ncol, sbank=sbank: E.matmul(
                                ps[sbank][:, 0:ncol], lhsT=kd[:, c0:c0 + 128], rhs=qd[:, c0:c0 + ncol], start=True, stop=True),
                                r=["qd", "kd"], w=["ps%d" % sbank])
                            P.op("dve", lambda E, si=si, ncol=ncol, sbank=sbank, hd=hd: E.scalar_tensor_tensor(
                                out=stmp[si][:, 0:ncol], in0=ps[sbank][:, 0:ncol], scalar=scale, in1=MB[:, hd, 0:ncol],
                                op0=ALU.mult, op1=ALU.add), r=["ps%d" % sbank, "MB"], w=["stmp%d" % si])
                            P.op("act", lambda E, si=si, ncol=ncol: E.activation(
                                out=ptd[si][:, 0:ncol], in_=stmp[si][:, 0:ncol], func=AF.Exp), r=["stmp%d" % si], w=["ptd%d" % si])
                            vblk = r_ * nbk + kb
                            ob, db = 2 + kb % 2, 4 + kb % 2
                            ob2, db2 = 2 + (kb + 1) % 2, 4 + (kb + 1) % 2
                            for (bank, lhs) in ((ob, None), (db, ones128)):
                                lt = Vh[:, vblk, :] if lhs is None else lhs[:]
                                P.op("pe", lambda E, bank=bank, lt=lt, si=si, kb=kb: E.matmul(
                                    ps[bank][:, 0:128], lhsT=lt, rhs=ptd[si][:, 0:128], start=(kb == 0), stop=True),
                                    r=["Vh", "ones128", "ptd%d" % si], w=["ps%d" % bank])
                            if ncol == 256:
                                for (bank, lhs) in ((ob2, None), (db2, ones128)):
                                    lt = Vh[:, vblk, :] if lhs is None else lhs[:]
                                    P.op("pe", lambda E, bank=bank, lt=lt, si=si: E.matmul(
                                        ps[bank][:, 0:128], lhsT=lt, rhs=ptd[si][:, 128:256], start=True, stop=False),
                                        r=["Vh", "ones128", "ptd%d" % si], w=["ps%d" % bank])
                            tq = r_ + d * kb * 128
                            tsl = slice(tq, tq + d * 127 + 1, d)
                            if g == 0:
                                P.op("dve", lambda E, ob=ob, tsl=tsl: E.tensor_copy(out=num[:, tsl], in_=ps[ob][:, 0:128]),
                                     r=["ps%d" % ob], w=["num"])
                                P.op("dve", lambda E, db=db, tsl=tsl: E.tensor_copy(out=den[:, tsl], in_=ps[db][:, 0:128]),
                                     r=["ps%d" % db], w=["den"])
                            else:
                                P.op("dve", lambda E, ob=ob, tsl=tsl: E.tensor_tensor(
                                    out=num[:, tsl], in0=ps[ob][:, 0:128], in1=num[:, tsl], op=ALU.add),
                                    r=["ps%d" % ob, "num"], w=["num"])
                                P.op("dve", lambda E, db=db, tsl=tsl: E.tensor_tensor(
                                    out=den[:, tsl], in0=ps[db][:, 0:128], in1=den[:, tsl], op=ALU.add),
                                    r=["ps%d" % db, "den"], w=["den"])
                for t in range(SEQ // 1024):
                    cols = slice(t * 1024, (t + 1) * 1024)
                    P.op("dve", lambda E, cols=cols: E.reciprocal(out=den[:, cols], in_=den[:, cols]), r=["den"], w=["den"])
                    P.op("pool", lambda E, cols=cols: E.tensor_tensor(out=yd[:, cols], in0=num[:, cols], in1=den[:, cols], op=ALU.mult),
                         r=["num", "den"], w=["yd"])
                P.op("sync", lambda E, slot=slot: E.dma_start(out=self.ydilS[slot], in_=yd[:]), r=["yd"], w=[], dma="ydst")
            self.end_phase()

    def mix_consts(self, l):
        P, nc, ps = self.P, self.nc, self.psum
        cst = self.cstS
        if not hasattr(self, "ssm_r"):
            self.ssm_r = nc.alloc_sbuf_tensor("ssm_r", [128, 32], F32)
            self.ssm_c128 = nc.alloc_sbuf_tensor("ssm_c128", [128, 32], F32)
            self.ssm_s128 = nc.alloc_sbuf_tensor("ssm_s128", [128, 32], F32)
            self.ssm_d = nc.alloc_sbuf_tensor("ssm_d", [128, 4], F32)
            self.pswS = nc.alloc_sbuf_tensor("pswS", [128, 128], F32)
        with ExitStack() as st:
            def A(name, shape, dt=F32):
                return st.enter_context(self.sbt(name, shape, dt))
            lre, lim, dt_ = A("lre", [128, 32]), A("lim", [128, 32]), A("dt_", [128, 32])
            th, kk, c1, s1, ab = A("th", [128, 32]), A("kkc", [128, 32]), A("c1", [128, 32]), A("s1", [128, 32]), A("abc", [128, 32])
            t1, t2, t3 = A("t1c", [128, 32]), A("t2c", [128, 32]), A("t3c", [128, 32])
            nr, ni, fre, fim, fimA, freB = (A("nr", [128, 32]), A("ni", [128, 32]), A("fre", [128, 32]), A("fim", [128, 32]),
                                            A("fimA", [128, 32]), A("freB", [128, 32]))
            bX, bY, cX, cY = A("bXs", [128, 32, 16]), A("bYs", [128, 32, 16]), A("cXs", [128, 32, 16]), A("cYs", [128, 32, 16])
            XA, XB, tq = A("XA", [128, 32, 16]), A("XB", [128, 32, 16]), A("tq", [128, 32, 16])
            CS, SN = A("CSt", [128, 32, 128]), A("SNt", [128, 32, 128])
            u1, u2 = A("u1", [128, 32, 64]), A("u2", [128, 32, 64])
            ident = A("identS", [128, 128])
            wab = A("wab", [128, 8, 2, 128], BF16)
            wc = A("wcs", [128, 8, 2, 128], BF16)
            tab = A("tabb", [128, 8, 2, 128])
            loads = ((lre, self.lamreT[:, l, :]), (lim, self.lamimT[:, l, :]), (dt_, self.logdtB[:, l, :]),
                     (bX, self.bX[:, l]), (bY, self.bY[:, l]), (cX, self.cX[:, l]), (cY, self.cY[:, l]),
                     (ident, self.ident), (self.pswS, self.pswap), (self.ssm_d, self.dT[:, l, :]))
            for i, (t_, src) in enumerate(loads):
                P.op("sync", lambda E, t_=t_, src=src: E.dma_start(out=t_[:], in_=src), w=["cl%d" % i], dma="cl%d" % i)
            RL = ["cl%d" % i for i in range(len(loads))]
            K = "cc"

            def dve(fn):
                P.op("dve", fn, r=RL + [K], w=[K])

            def act(fn):
                P.op("act", fn, r=RL + [K], w=[K])
            act(lambda E: E.activation(out=dt_[:], in_=dt_[:], func=AF.Exp))
            dve(lambda E: E.tensor_tensor(out=t1[:], in0=lre[:], in1=dt_[:], op=ALU.mult))
            act(lambda E: E.activation(out=self.ssm_r[:], in_=t1[:], func=AF.Exp))
            dve(lambda E: E.tensor_tensor(out=th[:], in0=lim[:], in1=dt_[:], op=ALU.mult))
            dve(lambda E: E.tensor_scalar(out=kk[:], in0=th[:], scalar1=1.0 / TWO_PI, scalar2=MAGIC, op0=ALU.mult, op1=ALU.add))
            dve(lambda E: E.tensor_scalar_sub(out=kk[:], in0=kk[:], scalar1=MAGIC))
            dve(lambda E: E.scalar_tensor_tensor(out=th[:], in0=kk[:], scalar=-TWO_PI, in1=th[:], op0=ALU.mult, op1=ALU.add))
            act(lambda E: E.activation(out=s1[:], in_=th[:], func=AF.Sin))
            act(lambda E: E.activation(out=ab[:], in_=th[:], func=AF.Abs))
            act(lambda E: E.activation(out=c1[:], in_=ab[:], func=AF.Sin, scale=-1.0, bias=cst[:, 6:7]))
            dve(lambda E: E.tensor_tensor(out=nr[:], in0=self.ssm_r[:], in1=c1[:], op=ALU.mult))
            dve(lambda E: E.tensor_scalar_add(out=nr[:], in0=nr[:], scalar1=-1.0))
            dve(lambda E: E.tensor_tensor(out=ni[:], in0=self.ssm_r[:], in1=s1[:], op=ALU.mult))
            dve(lambda E: E.tensor_tensor(out=t1[:], in0=lre[:], in1=lre[:], op=ALU.mult))
            dve(lambda E: E.tensor_tensor(out=t2[:], in0=lim[:], in1=lim[:], op=ALU.mult))
            dve(lambda E: E.tensor_tensor(out=t1[:], in0=t1[:], in1=t2[:], op=ALU.add))
            dve(lambda E: E.reciprocal(out=t3[:], in_=t1[:]))
            dve(lambda E: E.tensor_tensor(out=t1[:], in0=nr[:], in1=lre[:], op=ALU.mult))
            dve(lambda E: E.tensor_tensor(out=t2[:], in0=ni[:], in1=lim[:], op=ALU.mult))
            dve(lambda E: E.tensor_tensor(out=t1[:], in0=t1[:], in1=t2[:], op=ALU.add))
            dve(lambda E: E.tensor_tensor(out=fre[:], in0=t1[:], in1=t3[:], op=ALU.mult))
            dve(lambda E: E.tensor_tensor(out=t1[:], in0=ni[:], in1=lre[:], op=ALU.mult))
            dve(lambda E: E.tensor_tensor(out=t2[:], in0=nr[:], in1=lim[:], op=ALU.mult))
            dve(lambda E: E.tensor_tensor(out=t1[:], in0=t1[:], in1=t2[:], op=ALU.subtract))
            dve(lambda E: E.tensor_tensor(out=fim[:], in0=t1[:], in1=t3[:], op=ALU.mult))
            dve(lambda E: E.tensor_scalar(out=fimA[:], in0=fim[:], scalar1=cst[:, 2:3], scalar2=None, op0=ALU.mult))
            dve(lambda E: E.tensor_scalar(out=freB[:], in0=fre[:], scalar1=cst[:, 3:4], scalar2=None, op0=ALU.mult))

            def bc(t_):
                return t_[:].unsqueeze(2).broadcast_to([128, 32, 16])
            dve(lambda E: E.tensor_tensor(out=XA[:], in0=bX[:], in1=bc(fre), op=ALU.mult))
            dve(lambda E: E.tensor_tensor(out=tq[:], in0=bY[:], in1=bc(fimA), op=ALU.mult))
            dve(lambda E: E.tensor_tensor(out=XA[:], in0=XA[:], in1=tq[:], op=ALU.add))
            dve(lambda E: E.tensor_tensor(out=XB[:], in0=bY[:], in1=bc(freB), op=ALU.mult))
            dve(lambda E: E.tensor_tensor(out=tq[:], in0=bX[:], in1=bc(fim), op=ALU.mult))
            dve(lambda E: E.tensor_tensor(out=XB[:], in0=XB[:], in1=tq[:], op=ALU.add))
            dve(lambda E: E.memset(CS[:, :, 0:1], 1.0))
            dve(lambda E: E.memset(SN[:, :, 0:1], 0.0))
            cj, sj = c1, s1
            pw = [(A("cp%d" % j, [128, 32]), A("sp%d" % j, [128, 32])) for j in range(7)]
            for j in range(7):
                n = 1 << j

                def bcn(t_, n=n):
                    return t_[:].unsqueeze(2).broadcast_to([128, 32, n])
                dve(lambda E, n=n, cj=cj, bcn=bcn: E.tensor_tensor(out=u1[:, :, 0:n], in0=CS[:, :, 0:n], in1=bcn(cj), op=ALU.mult))
                dve(lambda E, n=n, sj=sj, bcn=bcn: E.tensor_tensor(out=u2[:, :, 0:n], in0=SN[:, :, 0:n], in1=bcn(sj), op=ALU.mult))
                dve(lambda E, n=n: E.tensor_tensor(out=CS[:, :, n:2 * n], in0=u1[:, :, 0:n], in1=u2[:, :, 0:n], op=ALU.subtract))
                dve(lambda E, n=n, sj=sj, bcn=bcn: E.tensor_tensor(out=u1[:, :, 0:n], in0=CS[:, :, 0:n], in1=bcn(sj), op=ALU.mult))
                dve(lambda E, n=n, cj=cj, bcn=bcn: E.tensor_tensor(out=u2[:, :, 0:n], in0=SN[:, :, 0:n], in1=bcn(cj), op=ALU.mult))
                dve(lambda E, n=n: E.tensor_tensor(out=SN[:, :, n:2 * n], in0=u1[:, :, 0:n], in1=u2[:, :, 0:n], op=ALU.add))
                cn, sn = pw[j]
                dve(lambda E, cj=cj: E.tensor_tensor(out=t1[:], in0=cj[:], in1=cj[:], op=ALU.mult))
                dve(lambda E, sj=sj: E.tensor_tensor(out=t2[:], in0=sj[:], in1=sj[:], op=ALU.mult))
                dve(lambda E, cn=cn: E.tensor_tensor(out=cn[:], in0=t1[:], in1=t2[:], op=ALU.subtract))
                dve(lambda E, cj=cj, sj=sj: E.tensor_tensor(out=t3[:], in0=cj[:], in1=sj[:], op=ALU.mult))
                dve(lambda E, sn=sn: E.tensor_scalar_mul(out=sn[:], in0=t3[:], scalar1=2.0))
                cj, sj = cn, sn
            dve(lambda E, cj=cj: E.tensor_copy(out=self.ssm_c128[:], in_=cj[:]))
            dve(lambda E, sj=sj: E.tensor_scalar(out=self.ssm_s128[:], in0=sj[:], scalar1=cst[:, 2:3], scalar2=None, op0=ALU.mult))
            for gb in range(4):
                gs = slice(gb * 8, (gb + 1) * 8)
                dve(lambda E, gs=gs: E.tensor_copy(out=tab[:, :, 0, :], in_=CS[:, gs, :]))
                dve(lambda E, gs=gs: E.tensor_copy(out=tab[:, :, 1, :], in_=SN[:, gs, :]))
                P.op("sync", lambda E, gb=gb: E.dma_start(out=self.tabS[gb], in_=tab[:]), r=[K], w=[K], dma="cst0")
                for ab_i, X in enumerate((XA, XB)):
                    P.op("pe", lambda E, X=X, gs=gs: E.transpose(ps[0][:, 0:128], X[:, gs, :].rearrange("p g h -> p (g h)"), ident[:]),
                         r=RL + [K], w=[K])
                    for gl in range(8):
                        dve(lambda E, gl=gl, ab_i=ab_i: E.tensor_scalar(
                            out=wab[:, gl, ab_i, :], in0=ps[0][:, 0:128], scalar1=cst[:, 8 + gl:9 + gl], scalar2=None, op0=ALU.mult))
                P.op("sync", lambda E, gb=gb: E.dma_start(out=self.wabS[gb], in_=wab[:]), r=[K], w=[K], dma="cst1")
                dve(lambda E: E.memset(wc[:], 0.0))
                for gl in range(8):
                    g_ = gb * 8 + gl
                    dve(lambda E, gl=gl, g_=g_: E.tensor_scalar(
                        out=wc[:, gl, 0, gl * 16:(gl + 1) * 16], in0=cX[:, g_, :], scalar1=cst[:, 3:4], scalar2=None, op0=ALU.mult))
                    dve(lambda E, gl=gl, g_=g_: E.tensor_scalar(
                        out=wc[:, gl, 1, gl * 16:(gl + 1) * 16], in0=cY[:, g_, :], scalar1=cst[:, 5:6], scalar2=None, op0=ALU.mult))
                P.op("sync", lambda E, gb=gb: E.dma_start(out=self.wcS[gb], in_=wc[:]), r=[K], w=[K], dma="cst2")
            self.end_phase()

    def mix_ssm(self, l, bl, hmod):
        P, nc, ps = self.P, self.nc, self.psum
        with ExitStack() as st:
            def A(name, shape, dt=F32):
                return st.enter_context(self.sbt(name, shape, dt))
            wu = A("wu", [128, 4, KC, 128], BF16)
            bS = A("b_inU", [128, NCH + 1])
            wglu = A("wgluS", [128, 4, 512], BF16)
            bglu = A("bgluS", [128, 4])
            tab = [A("tabS%d" % i, [128, 8, 2, 128]) for i in range(2)]
            wab = [A("wabS%d" % i, [128, 8, 2, 128], BF16) for i in range(2)]
            wc = [A("wcS%d" % i, [128, 8, 2, 128], BF16) for i in range(2)]
            uf = A("uf", [128, 4, 512])
            ub = A("ub", [128, 4, 512], BF16)
            ta = [A("ta%d" % i, [128, 512]) for i in range(2)]
            tb = [A("tb%d" % i, [128, 512]) for i in range(2)]
            zall = A("zall", [128, 8, 512])
            vall = A("vall", [128, 8, 512])
            init = A("sinit", [128, 32])
            ctmp = A("ctmp", [128, 8])
            e1 = [A("e1_%d" % i, [128, 512], BF16) for i in range(2)]
            e2 = [A("e2_%d" % i, [128, 512], BF16) for i in range(2)]
            yf = A("yf", [128, 4, 512])
            g1 = [A("g1_%d" % i, [128, 512]) for i in range(2)]
            g2 = [A("g2_%d" % i, [128, 512]) for i in range(2)]
            ygb = A("ygb", [128, 4, 512], BF16)
            sg = [A("sg%d" % i, [128, 512]) for i in range(2)]
            yo = [A("yo%d" % i, [128, 4, 512], BF16) for i in range(2)]
            P.op("sync", lambda E: E.dma_start(out=wu[:], in_=self.winq[l, CH_U:CH_U + 4].rearrange("c p k n -> p c k n")),
                 w=["wu"], dma="sw0")
            P.op("sync", lambda E: E.dma_start(out=bS[:], in_=self.b_inT[:, l, :]), w=["bSu"], dma="sw1")
            P.op("sync", lambda E: E.dma_start(out=wglu[:], in_=self.wgluq[l]), w=["wglu"], dma="sw2")
            P.op("sync", lambda E: E.dma_start(out=bglu[:], in_=self.b_gluT[:, l, :]), w=["bglu"], dma="sw3")
            P.op("dve", lambda E: E.memset(init[:], 0.0), w=["init"])
            r2, c128, s128 = self.ssm_r, self.ssm_c128, self.ssm_s128
            cnt = dict(w=0, t=0, e=0, g=0)
            for t in range(SEQ // 512):
                cols = slice(t * 512, (t + 1) * 512)
                for c in range(4):
                    pb = c % 2
                    for kc in range(KC):
                        P.op("pe", lambda E, c=c, kc=kc, pb=pb, cols=cols: E.matmul(
                            ps[pb][:], lhsT=wu[:, c, kc, :], rhs=hmod[:, kc, cols], start=(kc == 0), stop=(kc == KC - 1)),
                            r=["wu", "hmod_all"], w=["ps%d" % pb])
                    P.op("act", lambda E, c=c, pb=pb: E.activation(out=uf[:, c, :], in_=ps[pb][:], func=AF.Identity,
                                                                   bias=bS[:, CH_U + c:CH_U + c + 1]), r=["ps%d" % pb, "bSu"], w=["uf%d" % c])
                    P.op("act", lambda E, c=c, pb=pb: E.activation(out=ub[:, c, :], in_=ps[pb][:], func=AF.Identity,
                                                                   bias=bS[:, CH_U + c:CH_U + c + 1]), r=["ps%d" % pb, "bSu"], w=["ub%d" % c])
                for gb in range(4):
                    wi = cnt["w"] % 2
                    cnt["w"] += 1
                    tb_, wa_, wc_ = tab[wi], wab[wi], wc[wi]
                    kt, ka, kc_ = "tabS%d" % wi, "wabS%d" % wi, "wcS%d" % wi
                    P.op("sync", lambda E, gb=gb, tb_=tb_: E.dma_start(out=tb_[:], in_=self.tabS[gb]), w=[kt], dma=kt)
                    P.op("sync", lambda E, gb=gb, wa_=wa_: E.dma_start(out=wa_[:], in_=self.wabS[gb]), w=[ka], dma=ka)
                    P.op("sync", lambda E, gb=gb, wc_=wc_: E.dma_start(out=wc_[:], in_=self.wcS[gb]), w=[kc_], dma=kc_)
                    for gl in range(8):
                        ti = cnt["t"] % 2
                        cnt["t"] += 1
                        pa, pb = 2 + ti, 4 + ti
                        P.op("pe", lambda E, gl=gl, gb=gb, pa=pa, wa_=wa_: E.matmul(
                            ps[pa][:], lhsT=wa_[:, gl, 0, :], rhs=ub[:, gb, :], start=True, stop=True), r=[ka, "ub%d" % gb], w=["ps%d" % pa])
                        P.op("pe", lambda E, gl=gl, gb=gb, pb=pb, wa_=wa_: E.matmul(
                            ps[pb][:], lhsT=wa_[:, gl, 1, :], rhs=ub[:, gb, :], start=True, stop=True), r=[ka, "ub%d" % gb], w=["ps%d" % pb])
                        csb = tb_[:, gl, 0, :].unsqueeze(1).broadcast_to([128, 4, 128])
                        snb = tb_[:, gl, 1, :].unsqueeze(1).broadcast_to([128, 4, 128])
                        P.op("dve", lambda E, ti=ti, pa=pa, csb=csb: E.tensor_tensor(
                            out=ta[ti][:].rearrange("p (k n) -> p k n", n=128), in0=ps[pa][:].rearrange("p (k n) -> p k n", n=128),
                            in1=csb, op=ALU.mult), r=["ps%d" % pa, kt], w=["ta%d" % ti])
                        P.op("dve", lambda E, ti=ti, pb=pb, snb=snb: E.tensor_tensor(
                            out=tb[ti][:].rearrange("p (k n) -> p k n", n=128), in0=ps[pb][:].rearrange("p (k n) -> p k n", n=128),
                            in1=snb, op=ALU.mult), r=["ps%d" % pb, kt], w=["tb%d" % ti])
                        P.op("pool", lambda E, ti=ti, gl=gl: E.tensor_tensor(out=zall[:, gl, :], in0=ta[ti][:], in1=tb[ti][:], op=ALU.add),
                             r=["ta%d" % ti, "tb%d" % ti], w=["z%d" % gl])
                    for k in range(4):
                        sc = slice(k * 128, (k + 1) * 128)
                        for gl in range(8):
                            g_ = gb * 8 + gl
                            P.op("dve", lambda E, gl=gl, g_=g_, sc=sc: E.tensor_tensor_scan(
                                out=vall[:, gl, sc], data0=r2[:, g_:g_ + 1].broadcast_to([128, 128]), data1=zall[:, gl, sc],
                                initial=init[:, g_:g_ + 1], op0=ALU.mult, op1=ALU.add),
                                r=["z%d" % gl, "init", "cc_r"], w=["v%d" % gl])
                        vlast = vall[:, :, k * 128 + 127]
                        gsl = slice(gb * 8, (gb + 1) * 8)
                        P.op("pe", lambda E, vlast=vlast: E.matmul(ps[6][:, 0:8], lhsT=self.pswS[:], rhs=vlast, start=True, stop=True),
                             r=["v%d" % gl for gl in range(8)], w=["ps6"])
                        P.op("dve", lambda E, gsl=gsl: E.tensor_tensor(out=ctmp[:], in0=ps[6][:, 0:8], in1=s128[:, gsl], op=ALU.mult),
                             r=["ps6"], w=["ctmp"])
                        P.op("dve", lambda E, gsl=gsl, vlast=vlast: E.tensor_tensor(out=init[:, gsl], in0=vlast, in1=c128[:, gsl], op=ALU.mult),
                             r=["v%d" % gl for gl in range(8)], w=["init"])
                        P.op("dve", lambda E, gsl=gsl: E.tensor_tensor(out=init[:, gsl], in0=init[:, gsl], in1=ctmp[:], op=ALU.add),
                             r=["ctmp", "init"], w=["init"])
                    for gl in range(8):
                        ei = cnt["e"] % 2
                        cnt["e"] += 1
                        csb = tb_[:, gl, 0, :].unsqueeze(1).broadcast_to([128, 4, 128])
                        snb = tb_[:, gl, 1, :].unsqueeze(1).broadcast_to([128, 4, 128])
                        vv = vall[:, gl, :].rearrange("p (k n) -> p k n", n=128)
                        P.op("pool", lambda E, ei=ei, vv=vv, csb=csb: E.tensor_tensor(
                            out=e1[ei][:].rearrange("p (k n) -> p k n", n=128), in0=vv, in1=csb, op=ALU.mult),
                            r=["v%d" % gl, kt], w=["e1_%d" % ei])
                        P.op("pool", lambda E, ei=ei, vv=vv, snb=snb: E.tensor_tensor(
                            out=e2[ei][:].rearrange("p (k n) -> p k n", n=128), in0=vv, in1=snb, op=ALU.mult),
                            r=["v%d" % gl, kt], w=["e2_%d" % ei])
                        P.op("pe", lambda E, ei=ei, gl=gl, wc_=wc_: E.matmul(ps[7][:], lhsT=wc_[:, gl, 0, :], rhs=e1[ei][:],
                                                                          start=(gl == 0), stop=False), r=[kc_, "e1_%d" % ei], w=["ps7"])
                        P.op("pe", lambda E, ei=ei, gl=gl, wc_=wc_: E.matmul(ps[7][:], lhsT=wc_[:, gl, 1, :], rhs=e2[ei][:],
                                                                          start=False, stop=(gl == 7)), r=[kc_, "e2_%d" % ei], w=["ps7"])
                    P.op("dve", lambda E, gb=gb: E.scalar_tensor_tensor(
                        out=yf[:, gb, :], in0=uf[:, gb, :], scalar=self.ssm_d[:, gb:gb + 1], in1=ps[7][:], op0=ALU.mult, op1=ALU.add),
                        r=["ps7", "uf%d" % gb, "cc_r"], w=["yf%d" % gb])
                for c in range(4):
                    gi = cnt["g"] % 2
                    cnt["g"] += 1
                    x_ = yf[:, c, :]
                    P.op("act", lambda E, gi=gi, x_=x_: E.activation(out=g1[gi][:], in_=x_, func=AF.Square), r=["yf%d" % c], w=["g1_%d" % gi])
                    P.op("pool", lambda E, gi=gi: E.tensor_scalar(out=g1[gi][:], in0=g1[gi][:], scalar1=0.044715, scalar2=1.0,
                                                                  op0=ALU.mult, op1=ALU.add), r=["g1_%d" % gi], w=["g1_%d" % gi])
                    P.op("pool", lambda E, gi=gi, x_=x_: E.tensor_tensor(out=g1[gi][:], in0=g1[gi][:], in1=x_, op=ALU.mult),
                         r=["g1_%d" % gi, "yf%d" % c], w=["g1_%d" % gi])
                    P.op("act", lambda E, gi=gi: E.activation(out=g2[gi][:], in_=g1[gi][:], func=AF.Sigmoid, scale=1.5957691),
                         r=["g1_%d" % gi], w=["g2_%d" % gi])
                    P.op("pool", lambda E, gi=gi, x_=x_, c=c: E.tensor_tensor(out=yf[:, c, :], in0=g2[gi][:], in1=x_, op=ALU.mult),
                         r=["g2_%d" % gi, "yf%d" % c], w=["yg%d" % c])
                    P.op("act", lambda E, c=c: E.activation(out=ygb[:, c, :], in_=yf[:, c, :], func=AF.Copy), r=["yg%d" % c], w=["ygb%d" % c])
                yo_ = yo[t % 2]
                yok = "yo%d" % (t % 2)
                for c in range(4):
                    pb = c % 2
                    for kc in range(4):
                        P.op("pe", lambda E, c=c, kc=kc, pb=pb: E.matmul(
                            ps[pb][:], lhsT=wglu[:, kc, c * 128:(c + 1) * 128], rhs=ygb[:, kc, :], start=(kc == 0), stop=(kc == 3)),
                            r=["wglu", "ygb%d" % kc], w=["ps%d" % pb])
                    si = c % 2
                    P.op("act", lambda E, c=c, pb=pb, si=si: E.activation(out=sg[si][:], in_=ps[pb][:], func=AF.Sigmoid,
                                                                           bias=bglu[:, c:c + 1]), r=["ps%d" % pb, "bglu"], w=["sg%d" % si])
                    P.op("dve", lambda E, c=c, si=si, yo_=yo_: E.tensor_tensor(out=yo_[:, c, :], in0=yf[:, c, :], in1=sg[si][:], op=ALU.mult),
                         r=["sg%d" % si, "yg%d" % c], w=[yok + "_%d" % c])
                P.op("sync", lambda E, yo_=yo_, cols=cols: E.dma_start(out=self.yssmS[:, :, cols].rearrange("c p t -> p c t"), in_=yo_[:]),
                     r=[yok + "_%d" % c for c in range(4)], w=[yok], dma=yok)
            self.end_phase()

    def mix_merge(self, l, bl, src, dst):
        P, nc, ps = self.P, self.nc, self.psum
        tok0 = bl * SEQ
        eps = LN_EPS / (ALPHA * ALPHA)
        srcv = src.rearrange("(kc p) t -> p kc t", p=128)
        dstv = dst.rearrange("(kc p) t -> p kc t", p=128)
        lnS = self.lnS
        with ExitStack() as st:
            def A(name, shape, dt=F32):
                return st.enter_context(self.sbt(name, shape, dt))
            wgs = [A("wg%d" % i, [128, 3, KC, 128], BF16) for i in range(2)]
            wbr = A("wbr", [128, 2, 4, D], BF16)
            wbr0 = A("wbr0", [64, 8, D], BF16)
            wo = A("wo", [128, KC, D], BF16)
            bS = A("b_inM", [128, NCH + 1])
            ht = [A("mht%d" % i, [128, KC, 512]) for i in range(2)]
            hm = A("mhm", [128, KC, 512], BF16)
            ym = [A("ym%d" % i, [64, 8, 512], BF16) for i in range(2)]
            ydl = [A("ydl%d" % i, [128, 4, 512], BF16) for i in range(2)]
            ys = [A("ys%d" % i, [128, 4, 512], BF16) for i in range(2)]
            gt = [A("gt%d" % i, [128, 512]) for i in range(3)]
            acc = [A("acc%d" % i, [128, 512]) for i in range(2)]
            tmpm = [A("tmpm%d" % i, [128, 512]) for i in range(2)]
            mg = A("mg", [128, KC, 512], BF16)
            zb = [A("mzb%d" % i, [128, 512], BF16) for i in range(2)]
            zq = [A("mzq%d" % i, [128, 512], BF16) for i in range(2)]
            lnbufs = (A("mmean", [128, 512]), A("mm2", [128, 512]), A("mrstd", [128, 512]),
                      [A("mlntmp%d" % i, [128, 512]) for i in range(2)])
            for k in range(2):
                P.op("sync", lambda E, k=k: E.dma_start(out=wbr[:, k], in_=self.wbrq[l, k + 1]), w=["wbr"], dma="mw1")
            P.op("sync", lambda E: E.dma_start(out=wbr0[:], in_=self.wbr0q[l]), w=["wbr0"], dma="mw2")
            P.op("sync", lambda E: E.dma_start(out=wo[:], in_=self.woutq[l]), w=["wo"], dma="mw3")
            P.op("sync", lambda E: E.dma_start(out=bS[:], in_=self.b_inT[:, l, :]), w=["bSm"], dma="mw4")
            cnt = dict(b=0, z=0, g=0)
            for t in range(SEQ // 512):
                cols = slice(t * 512, (t + 1) * 512)
                gcols = slice(tok0 + t * 512, tok0 + (t + 1) * 512)
                i2 = t % 2
                h = ht[i2]
                hk = "mht%d" % i2
                P.op("sync", lambda E, h=h, gcols=gcols: E.dma_start(out=h[:], in_=srcv[:, :, gcols]), w=[hk], dma=hk)
                P.op("sync", lambda E, i2=i2, cols=cols: E.dma_start(out=ym[i2][:], in_=self.ymlaS[:, :, cols].rearrange("h p t -> p h t")),
                     w=["ym%d" % i2], dma="ym%d" % i2)
                P.op("sync", lambda E, i2=i2, cols=cols: E.dma_start(out=ydl[i2][:], in_=self.ydilS[:, :, cols].rearrange("c p t -> p c t")),
                     w=["ydl%d" % i2], dma="ydl%d" % i2)
                P.op("sync", lambda E, i2=i2, cols=cols: E.dma_start(out=ys[i2][:], in_=self.yssmS[:, :, cols].rearrange("c p t -> p c t")),
                     w=["ys%d" % i2], dma="ys%d" % i2)
                for kc in range(KC):
                    P.op("act", lambda E, kc=kc, h=h: E.activation(
                        out=hm[:, kc, :], in_=h[:, kc, :], func=AF.Identity,
                        bias=self.ada[:, 24 + kc, bl:bl + 1], scale=self.s1p[:, 1, kc, bl:bl + 1]),
                        r=[hk, "ada", "s1p"], w=["mhm%d" % kc])
                hmk = ["mhm%d" % kc for kc in range(KC)]
                for i in range(KC):
                    wgi = cnt["g"] % 2
                    cnt["g"] += 1
                    wg = wgs[wgi]
                    wgk = "wg%d" % wgi
                    P.op("sync", lambda E, wg=wg, i=i: E.dma_start(
                        out=wg[:], in_=self.winq[l, CH_GATE + i:CH_GATE + 24:8].rearrange("c p k n -> p c k n")),
                        w=[wgk], dma=wgk)
                    for k in range(3):
                        pb = cnt["b"] % 3
                        cnt["b"] += 1
                        gc = k * 8 + i
                        for kc in range(KC):
                            P.op("pe", lambda E, k=k, kc=kc, pb=pb, wg=wg: E.matmul(
                                ps[pb][:], lhsT=wg[:, k, kc, :], rhs=hm[:, kc, :], start=(kc == 0), stop=(kc == KC - 1)),
                                r=[wgk, hmk[kc]], w=["ps%d" % pb])
                        P.op("act", lambda E, k=k, pb=pb, gc=gc: E.activation(
                            out=gt[k][:], in_=ps[pb][:], func=AF.Sigmoid, bias=bS[:, CH_GATE + gc:CH_GATE + gc + 1]),
                            r=["ps%d" % pb, "bSm"], w=["gt%d" % k])
                    for hd in range(8):
                        P.op("pe", lambda E, hd=hd, i=i, i2=i2: E.matmul(
                            ps[3][:], lhsT=wbr0[:, hd, i * 128:(i + 1) * 128], rhs=ym[i2][:, hd, :], start=(hd == 0), stop=(hd == 7)),
                            r=["wbr0", "ym%d" % i2], w=["ps3"])
                    for k, yk_, ysrc in ((0, "ydl%d" % i2, ydl[i2]), (1, "ys%d" % i2, ys[i2])):
                        for kc in range(4):
                            P.op("pe", lambda E, k=k, kc=kc, i=i, ysrc=ysrc: E.matmul(
                                ps[4 + k][:], lhsT=wbr[:, k, kc, i * 128:(i + 1) * 128], rhs=ysrc[:, kc, :], start=(kc == 0), stop=(kc == 3)),
                                r=["wbr", yk_], w=["ps%d" % (4 + k)])
                    ai = i % 2
                    P.op("dve", lambda E, ai=ai: E.tensor_tensor(out=acc[ai][:], in0=ps[3][:], in1=gt[0][:], op=ALU.mult),
                         r=["ps3", "gt0"], w=["acc%d" % ai])
                    P.op("dve", lambda E, ai=ai: E.tensor_tensor(out=tmpm[0][:], in0=ps[4][:], in1=gt[1][:], op=ALU.mult),
                         r=["ps4", "gt1"], w=["tmpm0"])
                    P.op("dve", lambda E, ai=ai: E.tensor_tensor(out=tmpm[1][:], in0=ps[5][:], in1=gt[2][:], op=ALU.mult),
                         r=["ps5", "gt2"], w=["tmpm1"])
                    P.op("pool", lambda E, ai=ai: E.tensor_tensor(out=acc[ai][:], in0=acc[ai][:], in1=tmpm[0][:], op=ALU.add),
                         r=["acc%d" % ai, "tmpm0"], w=["acc%d" % ai])
                    P.op("pool", lambda E, ai=ai, i=i: E.tensor_tensor(out=mg[:, i, :], in0=acc[ai][:], in1=tmpm[1][:], op=ALU.add),
                         r=["acc%d" % ai, "tmpm1"], w=["mg%d" % i])
                zkeys, ykeys = [], []
                for i in range(KC):
                    pf = 6
                    zi = cnt["z"] % 2
                    cnt["z"] += 1
                    for kc in range(KC):
                        P.op("pe", lambda E, i=i, kc=kc: E.matmul(
                            ps[6][:], lhsT=wo[:, kc, i * 128:(i + 1) * 128], rhs=mg[:, kc, :], start=(kc == 0), stop=(kc == KC - 1)),
                            r=["wo", "mg%d" % kc], w=["ps6"])
                    hs = h[:, i, :]
                    zk = hk + "_z%d" % i
                    zkeys.append(zk)
                    ykeys.append(hk + "_y%d" % i)
                    P.op("dve", lambda E, hs=hs, i=i: E.scalar_tensor_tensor(
                        out=hs, in0=ps[6][:], scalar=self.gp[:, 1, i, bl:bl + 1], in1=hs, op0=ALU.mult, op1=ALU.add),
                        r=["ps6", "gp", hk], w=[zk])
                    P.op("act", lambda E, hs=hs, zi=zi: E.activation(out=zb[zi][:], in_=hs, func=AF.Copy), r=[zk], w=["mzb%d" % zi])
                    P.op("act", lambda E, hs=hs, zi=zi: E.activation(out=zq[zi][:], in_=hs, func=AF.Square), r=[zk], w=["mzq%d" % zi])
                    P.op("pe", lambda E, zi=zi, i=i: E.matmul(ps[7][:], lhsT=self.ones_bf[:], rhs=zb[zi][:], start=(i == 0), stop=(i == KC - 1)),
                         r=["ones", "mzb%d" % zi], w=["ps7"])
                    P.op("pe", lambda E, zi=zi, i=i: E.matmul(ps[0][:], lhsT=self.ones_bf[:], rhs=zq[zi][:], start=(i == 0), stop=(i == KC - 1)),
                         r=["ones", "mzq%d" % zi], w=["ps0"])
                self.ln_finish(h, zkeys, ykeys, slice(0, 512), ps[7], ps[0], "ps7", "ps0",
                               lambda i: lnS[:, l, 1, 0, i:i + 1], lambda i: lnS[:, l, 1, 1, i:i + 1], lnbufs, eps)
                P.op("sync", lambda E, h=h, gcols=gcols: E.dma_start(out=dstv[:, :, gcols], in_=h[:]),
                     r=ykeys, w=[hk], dma="mst%d" % i2)
            self.end_phase()


def _consts():
    import ml_dtypes
    cst = np.zeros((128, NCST), np.float32)
    half = 16
    invf = (10000.0 ** (-(np.arange(half, dtype=np.float32)) / half)).astype(np.float32)
    cst[64:96, 0] = np.concatenate([invf, invf])
    cst[64:80, 1] = -1.0
    cst[80:96, 1] = 1.0
    cst[0:64, 2] = -1.0
    cst[64:128, 2] = 1.0
    cst[0:64, 3] = 1.0
    cst[64:128, 3] = -1.0
    cst[:, 5] = -1.0
    cst[:, 6] = np.pi / 2
    for j in range(8):
        cst[16 * j:16 * (j + 1), 8 + j] = 1.0
    ki = np.arange(128)[:, None, None]
    jj = np.arange(4)[None, :, None]
    nn = np.arange(512)[None, None, :]
    maskc = (nn >= jj * 128 + ki).astype(np.float32).astype(ml_dtypes.bfloat16)
    slopes = np.exp2(-8.0 * (np.arange(12, dtype=np.float32) + 1.0) / 12).astype(np.float32)
    kk = np.arange(128)[:, None]
    qq = np.arange(128)[None, :]
    dilmb = np.zeros((128, 12, 256), np.float32)
    for hd in range(12):
        d = DIL_PAIRS[hd // 4][1]
        relL = (qq - kk).astype(np.float32)
        relR = (qq - kk + 128).astype(np.float32)
        dilmb[:, hd, 0:128] = np.where(qq >= kk, -(slopes[hd] * d) * relL, -30000.0)
        dilmb[:, hd, 128:256] = np.where(qq <= kk, -(slopes[hd] * d) * relR, -30000.0)
    ident = np.eye(128, dtype=np.float32)
    pswap = np.zeros((128, 128), np.float32)
    for mcol in range(128):
        pswap[(mcol + 64) % 128, mcol] = 1.0
    return dict(cst=cst, maskc=np.ascontiguousarray(maskc), dilmb=dilmb, ident=ident, pswap=pswap)


def prep_inputs(inp, names, L, cores):
    f32 = np.float32
    g = lambda k: np.asarray(inp[k])[:L]
    sh = {}
    sh.update(_consts())
    sh["w_ada"] = np.ascontiguousarray(g("w_ada"), dtype=f32)
    sh["b_adaT"] = np.ascontiguousarray(g("b_ada").reshape(L, 72, 128).transpose(2, 0, 1))
    ln = np.stack([g("ln_g"), g("ln_b")], axis=2)
    sh["lnT"] = np.ascontiguousarray(ln.reshape(L, 3, 2, KC, 128).transpose(4, 0, 1, 2, 3))
    for k in ("ffn_w1", "ffn_w3", "ffn_w2", "w_in"):
        if k in names:
            sh[k] = np.ascontiguousarray(g(k), dtype=f32)
    if "w_in" in names:
        b_in = g("b_in")
        bT = np.zeros((128, L, NCH + 1), f32)
        for c, (cs, cw) in enumerate(W_CHUNKS):
            bT[:cw, :, c] = b_in[:, cs:cs + cw].T
        bT[64:80, :, NCH] = b_in[:, 656:672].T
        bT[80:96, :, NCH] = b_in[:, 640:656].T
        sh["b_inT"] = bT
        sh["b_vrow"] = np.ascontiguousarray(b_in[:, None, 672 + 3072:672 + 4608])
        sh["qnT"] = np.ascontiguousarray(g("mla_q_norm").reshape(L, 3, 128).transpose(2, 0, 1))
        sh["kvnT"] = np.ascontiguousarray(g("mla_kv_norm").reshape(L, 2, 128).transpose(2, 0, 1))
        sh["w_qb"] = np.ascontiguousarray(g("mla_w_qb"), dtype=f32)
        sh["w_kvb"] = np.ascontiguousarray(g("mla_w_kvb"), dtype=f32)
        sh["w_glu"] = np.ascontiguousarray(g("ssm_w_glu"), dtype=f32)
        sh["b_gluT"] = np.ascontiguousarray(g("ssm_b_glu").reshape(L, 4, 128).transpose(2, 0, 1))
        sh["w_br"] = np.ascontiguousarray(g("w_br"), dtype=f32)
        sh["w_out"] = np.ascontiguousarray(g("w_out"), dtype=f32)
        dup = lambda a: np.ascontiguousarray(np.concatenate([a, a], axis=0))
        sh["lamreT"] = dup(g("ssm_lambda_re").transpose(2, 0, 1))
        sh["lamimT"] = dup(g("ssm_lambda_im").transpose(2, 0, 1))
        sh["logdtB"] = np.ascontiguousarray(np.broadcast_to(g("ssm_log_dt")[None], (128, L, 32)))
        bre = g("ssm_b_re").transpose(2, 0, 1, 3)
        bim = g("ssm_b_im").transpose(2, 0, 1, 3)
        sh["bX"] = np.ascontiguousarray(np.concatenate([bre, bim], axis=0))
        sh["bY"] = np.ascontiguousarray(np.concatenate([bim, bre], axis=0))
        cre = g("ssm_c_re").transpose(3, 0, 1, 2)
        cim = g("ssm_c_im").transpose(3, 0, 1, 2)
        sh["cX"] = np.ascontiguousarray(np.concatenate([cre, cim], axis=0))
        sh["cY"] = np.ascontiguousarray(np.concatenate([cim, cre], axis=0))
        sh["dT"] = np.ascontiguousarray(g("ssm_d").reshape(L, 4, 128).transpose(2, 0, 1))
    x = np.asarray(inp["x"], f32)
    c = np.asarray(inp["c"], f32)
    pos = np.asarray(inp["positions"]).astype(np.int32)
    maps = []
    for i in cores:
        m = {k: v for k, v in sh.items() if k in names}
        xs = x[NB * i:NB * (i + 1)]
        m["xT"] = np.ascontiguousarray(xs.transpose(2, 0, 1).reshape(D, NTOK))
        cs_ = c[NB * i:NB * (i + 1)]
        m["cT"] = np.ascontiguousarray(cs_.reshape(NB, KC, 128).transpose(2, 1, 0))
        if "pos" in names:
            m["pos"] = np.ascontiguousarray(pos[NB * i:NB * (i + 1)])
        maps.append(m)
    return maps


def gather_output(results, cores):
    out = np.zeros((NCORES * NB, SEQ, D), np.float32)
    for i, r in zip(cores, results):
        yT = np.asarray(r["yT"]).reshape(D, NB, SEQ)
        out[NB * i:NB * (i + 1)] = yT.transpose(1, 2, 0)
    return out


_NC_CACHE = {}


def run(inp, cores=None, raw=False, **kw):
    cores = list(range(NCORES)) if cores is None else list(cores)
    key = tuple(sorted((k, str(v)) for k, v in kw.items()))
    if key not in _NC_CACHE:
        b = Builder(**kw)
        b.build()
        _NC_CACHE[key] = b
    b = _NC_CACHE[key]
    maps = prep_inputs(inp, set(b.in_names), b.L, cores)
    res = run_bass_kernel_spmd(b.nc, maps, core_ids=list(range(len(cores))))
    if raw:
        return res.results
    return gather_output(res.results, cores)


def kernel(**inputs):
    return run(inputs)
```

```python
from contextlib import ExitStack
import numpy as np
import concourse.bass as bass
import concourse.mybir as mybir
from concourse.bass_utils import run_bass_kernel_spmd

F32 = mybir.dt.float32
BF16 = mybir.dt.bfloat16
I32 = mybir.dt.int32
AF = mybir.ActivationFunctionType
ALU = mybir.AluOpType

NCORES = 8
D = 1024
SEQ = 4096
NB = 2
NTOK = NB * SEQ
DEPTH = 4
DFF = 2816
KC = D // 128
FC = DFF // 128
N_IN = 8864
ALPHA = (2 * DEPTH) ** 0.25
LN_EPS = 1e-5
RMS_EPS = 1e-6
NCST = 24
CH_QA, CH_KVA, CH_KR, CH_DIL, CH_U, CH_GATE = 0, 3, 5, 6, 42, 46
W_CHUNKS = ([(i * 128, 128) for i in range(5)] + [(576, 96)] + [(672 + 128 * i, 128) for i in range(36)]
            + [(5280 + 128 * i, 128) for i in range(4)] + [(5792 + 128 * i, 128) for i in range(24)])
NCH = len(W_CHUNKS)
DIL_PAIRS = ((128, 1), (512, 4), (2048, 16))
MAGIC = 12582912.0
POOL_DSEM = True
TWO_PI = 6.283185307179586


class Prog:
    ENGS = ("sync", "act", "pe", "dve", "pool")

    def __init__(self, nc, es):
        self.nc = nc
        self.es = es
        self.q = {e: [] for e in self.ENGS}
        self.cnt = {e: 0 for e in self.ENGS}
        self.sems = {}
        for e in ("act", "pe", "dve", "pool"):
            self.sems["c_" + e] = es.enter_context(nc.semaphore("c_" + e))
        self.dval = {}
        self.dlast = {}
        self.waited = {e: {} for e in self.ENGS}
        self.res = {}
        self.dkeymap = {}
        self.nblocks = 0
        self.nrot = 0

    def _dsem(self, key):
        sid = self.dkeymap.get(key)
        if sid is None and key.startswith("pc"):
            sid = "d_" + key
            if sid not in self.sems:
                self.sems[sid] = self.es.enter_context(self.nc.semaphore(sid))
                self.dval[sid] = 0
                self.dlast[sid] = None
            return sid
        if sid is None:
            n = len(self.dkeymap)
            sid = "d_%d" % n
            if sid not in self.sems:
                self.sems[sid] = self.es.enter_context(self.nc.semaphore(sid))
                self.dval[sid] = 0
                self.dlast[sid] = None
            self.dkeymap[key] = sid
        return sid

    def op(self, eng, fn, r=(), w=(), dma=None):
        deps = {}

        def add(tok):
            if tok is None:
                return
            cur = deps.get(tok[0])
            if cur is None or cur[1] < tok[1]:
                deps[tok[0]] = tok

        for k in r:
            st = self.res.get(k)
            if st is not None:
                add(st[0])
        for k in w:
            st = self.res.get(k)
            if st is not None:
                add(st[0])
                for t in st[1].values():
                    add(t)
        sid_d = None
        if dma is not None:
            sid_d = self._dsem(dma)
            add(self.dlast[sid_d])
        wq = self.q[eng]
        wd = self.waited[eng]
        for sid, tok in deps.items():
            _, val, e2, idx = tok
            if idx is not None and e2 == eng:
                if eng == "pe":
                    continue
                if self.cnt[eng] - idx >= 2:
                    continue
            if wd.get(sid, 0) >= val:
                continue
            wd[sid] = val
            wq.append((0, sid, val))
        if dma is not None:
            self.dval[sid_d] += 16
            tok = (sid_d, self.dval[sid_d], eng, None)
            self.dlast[sid_d] = tok
            wq.append((2, fn, sid_d))
        else:
            self.cnt[eng] += 1
            tok = ("c_" + eng, self.cnt[eng], eng, self.cnt[eng])
            wq.append((1, fn))
        for k in r:
            st = self.res.get(k)
            if st is None:
                st = [None, {}]
                self.res[k] = st
            st[1][tok[0]] = tok
        for k in w:
            self.res[k] = [tok, {}]

    def barrier(self):
        for e in self.ENGS:
            wd = self.waited[e]
            for sid in self.sems:
                if sid.startswith("c_"):
                    val = self.cnt[sid[2:]]
                else:
                    val = self.dval[sid]
                if val > 0 and wd.get(sid, 0) < val:
                    wd[sid] = val
                    self.q[e].append((0, sid, val))
        self.res = {}
        self.dkeymap = {}

    def rotate(self, limit=30000):
        for sid in list(self.sems):
            if sid.startswith("c_"):
                val = self.cnt[sid[2:]]
            else:
                val = self.dval[sid]
            if val > limit:
                self.nrot += 1
                self.sems[sid] = self.es.enter_context(self.nc.semaphore("%s_r%d" % (sid, self.nrot)))
                if sid.startswith("c_"):
                    self.cnt[sid[2:]] = 0
                else:
                    self.dval[sid] = 0
                    self.dlast[sid] = None
                for e in self.ENGS:
                    self.waited[e].pop(sid, None)

    def flush(self):
        nc = self.nc
        sems = self.sems
        self.nblocks += 1
        with nc.Block() as block:
            for e, deco in (("sync", block.sync), ("act", block.scalar), ("pe", block.tensor),
                            ("dve", block.vector), ("pool", block.gpsimd)):
                items = self.q[e]
                self.q[e] = []
                if not items:
                    continue
                csem = sems.get("c_" + e)

                def body(E, items=items, csem=csem):
                    for it in items:
                        if it[0] == 0:
                            E.wait_ge(sems[it[1]], it[2])
                        elif it[0] == 1:
                            it[1](E).then_inc(csem, 1)
                        elif it[0] == 2:
                            it[1](E).then_inc(sems[it[2]], 16)
                        elif it[0] == 3:
                            E.sem_inc(sems[it[1]], 1)
                        else:
                            E.sem_clear(sems[it[1]])
                deco(body)


class Builder:
    def __init__(self, layers=DEPTH, stop_after=None, skip_mix=False, skip_ffn=False, debug=(), mix_parts="all"):
        self.layers = layers
        self.stop_after = stop_after
        self.skip_mix = skip_mix
        self.skip_ffn = skip_ffn
        self.debug = tuple(debug)
        self.mix_parts = mix_parts
        self.in_names = []
        self.nc = bass.Bass("TRN2", target_bir_lowering=False)

    def sbt(self, name, shape, dt):
        self._u = getattr(self, "_u", 0) + 1
        return self.nc.sbuf_tensor("%s_%d" % (name, self._u), shape, dt)

    def din(self, name, shape, dt=F32):
        self.in_names.append(name)
        return self.nc.dram_tensor(name, list(shape), dt, kind="ExternalInput").ap()

    def dscr(self, name, shape, dt=F32):
        kind = "ExternalOutput" if name in self.debug else "Internal"
        return self.nc.dram_tensor(name, list(shape), dt, kind=kind).ap()

    def dout(self, name, shape, dt=F32):
        return self.nc.dram_tensor(name, list(shape), dt, kind="ExternalOutput").ap()

    def build(self):
        nc = self.nc
        L = self.layers
        self.L = L
        self.xT = self.din("xT", [D, NTOK])
        self.cT = self.din("cT", [128, KC, NB])
        self.w_ada = self.din("w_ada", [L, D, 9 * D])
        self.b_adaT = self.din("b_adaT", [128, L, 72])
        self.lnT = self.din("lnT", [128, L, 3, 2, KC])
        self.cst = self.din("cst", [128, NCST])
        if not self.skip_ffn:
            self.ffn_w1 = self.din("ffn_w1", [L, 2, D, DFF])
            self.ffn_w3 = self.din("ffn_w3", [L, 2, D, DFF])
            self.ffn_w2 = self.din("ffn_w2", [L, 2, DFF, D])
            self.w1q = self.dscr("w1q", [L, 2, FC, 128, KC, 128], BF16)
            self.w3q = self.dscr("w3q", [L, 2, FC, 128, KC, 128], BF16)
            self.w2q = self.dscr("w2q", [L, 2, KC, 128, FC, 128], BF16)
        if not self.skip_mix:
            self.mixer_decl()
        self.yT = self.dout("yT", [D, NTOK])
        self.hS = self.dscr("hS", [D, NTOK])

        with ExitStack() as es:
            self.es = es
            P = self.P = Prog(nc, es)
            self.psum = [es.enter_context(nc.psum_tensor("ps%d" % i, [128, 512], F32)) for i in range(8)]
            self.ones_bf = nc.alloc_sbuf_tensor("ones_bf", [128, 128], BF16)
            self.ada = nc.alloc_sbuf_tensor("ada", [128, 72, NB], F32)
            self.s1p = nc.alloc_sbuf_tensor("s1p", [128, 3, KC, NB], F32)
            self.gp = nc.alloc_sbuf_tensor("gp", [128, 3, KC, NB], F32)
            self.condT = nc.alloc_sbuf_tensor("condT", [128, KC, NB], F32)
            self.lnS = nc.alloc_sbuf_tensor("lnS", [128, L, 3, 2, KC], F32)
            self.b_adaS = nc.alloc_sbuf_tensor("b_adaS", [128, L, 72], F32)
            self.cstS = nc.alloc_sbuf_tensor("cstS", [128, NCST], F32)

            self.phase_init()
            self.phase_precast()
            src = self.xT
            stages = []
            for l in range(self.layers):
                for stg in ("ffn0", "mix", "ffn1"):
                    stages.append((l, stg))
                    if self.stop_after == (l, stg):
                        break
                else:
                    continue
                break
            if self.skip_mix:
                stages = [x for x in stages if x[1] != "mix"]
            if self.skip_ffn:
                stages = [x for x in stages if x[1] == "mix"]
            ada_done = set()
            for n, (l, stg) in enumerate(stages):
                dst = self.yT if n == len(stages) - 1 else self.hS
                if l not in ada_done:
                    self.phase_ada(l)
                    ada_done.add(l)
                if stg == "ffn0":
                    self.phase_ffn(l, 0, src, dst)
                elif stg == "mix":
                    self.phase_mixer(l, src, dst)
                else:
                    self.phase_ffn(l, 1, src, dst)
                src = self.hS
        return nc

    def end_phase(self):
        self.P.barrier()
        self.P.flush()
        self.P.rotate()

    def phase_init(self):
        P = self.P
        ones_bf, condT = self.ones_bf, self.condT
        P.op("pool", lambda E: E.memset(ones_bf[:], 1.0 / D), w=["ones"])
        P.op("sync", lambda E: E.dma_start(out=condT[:], in_=self.cT), w=["condT"], dma="c0")
        P.op("sync", lambda E: E.dma_start(out=self.lnS[:], in_=self.lnT), w=["lnS"], dma="c1")
        P.op("sync", lambda E: E.dma_start(out=self.b_adaS[:], in_=self.b_adaT), w=["b_adaS"], dma="c2")
        P.op("sync", lambda E: E.dma_start(out=self.cstS[:], in_=self.cst), w=["cst"], dma="c3")
        P.op("act", lambda E: E.activation(out=condT[:], in_=condT[:], func=AF.Silu), r=["condT"], w=["condT"])
        self.end_phase()

    def phase_precast(self):
        P = self.P
        n = 0
        if not self.skip_mix:
            n = self.mixer_precast(n)
        for l in range(0 if self.skip_ffn else self.layers):
            for f in range(2):
                for (src, dst, nout, nk) in ((self.ffn_w1, self.w1q, FC, KC), (self.ffn_w3, self.w3q, FC, KC),
                                            (self.ffn_w2, self.w2q, KC, FC)):
                    sv = src[l, f].rearrange("(kc p) (j n) -> j p kc n", p=128, n=128)
                    for j in range(nout):
                        P.op("pool", lambda E, o=dst[l, f, j], i=sv[j]: E.dma_start(out=o, in_=i),
                             w=[], dma="pc%d" % (n % 8))
                        n += 1
        self.end_phase()

    def phase_ada(self, l):
        P, nc = self.P, self.nc
        NPIECE = 8
        CW = 9 * D // NPIECE
        with ExitStack() as pst:
            wt = [pst.enter_context(self.sbt("adaw%d" % i, [128, KC, CW], F32)) for i in range(2)]
            ps = self.psum[0]
            wv = self.w_ada[l].rearrange("(kc p) n -> p kc n", p=128)
            for pc in range(NPIECE):
                t = wt[pc % 2]
                P.op("sync", lambda E, t=t, pc=pc: E.dma_start(out=t[:], in_=wv[:, :, pc * CW:(pc + 1) * CW]),
                     w=["adaw%d" % (pc % 2)], dma="adaw%d" % (pc % 2))
                for cc in range(CW // 128):
                    c = pc * (CW // 128) + cc
                    for kc in range(KC):
                        P.op("pe", lambda E, t=t, cc=cc, kc=kc, c=c: E.matmul(
                            ps[:, c * NB:(c + 1) * NB], lhsT=t[:, kc, cc * 128:(cc + 1) * 128],
                            rhs=self.condT[:, kc, :], start=(kc == 0), stop=(kc == KC - 1)),
                            r=["adaw%d" % (pc % 2), "condT"], w=["ps0"])
            ada = self.ada
            psv = ps[:, 0:72 * NB].rearrange("p (c b) -> p c b", b=NB)
            for b in range(NB):
                P.op("dve", lambda E, b=b: E.tensor_tensor(
                    out=ada[:, :, b], in0=psv[:, :, b], in1=self.b_adaS[:, l, :], op=ALU.add),
                    r=["ps0", "b_adaS"], w=["ada"])
            for s in range(3):
                res_w = 1.0 if s == 1 else 0.5
                P.op("dve", lambda E, s=s: E.tensor_scalar_add(
                    out=self.s1p[:, s], in0=ada[:, s * 24 + 8:s * 24 + 16, :], scalar1=1.0),
                    r=["ada"], w=["s1p"])
                P.op("dve", lambda E, s=s, rw=res_w: E.tensor_scalar_mul(
                    out=self.gp[:, s], in0=ada[:, s * 24 + 16:s * 24 + 24, :], scalar1=rw / ALPHA),
                    r=["ada"], w=["gp"])
            self.end_phase()

    def ln_finish(self, h, zkeys, ykeys, cols, mean_ps, ez_ps, mean_k, ez_k, g_ap, b_ap, bufs, eps):
        P = self.P
        mean, m2, rstd, tmp = bufs
        P.op("act", lambda E: E.activation(out=mean[:], in_=mean_ps[:], func=AF.Copy), r=[mean_k], w=["ln_mean"])
        P.op("pool", lambda E: E.tensor_tensor(out=m2[:], in0=mean[:], in1=mean[:], op=ALU.mult),
             r=["ln_mean"], w=["ln_m2"])
        P.op("dve", lambda E: E.tensor_tensor(out=m2[:], in0=ez_ps[:], in1=m2[:], op=ALU.subtract),
             r=[ez_k, "ln_m2"], w=["ln_m2"])
        P.op("dve", lambda E: E.tensor_scalar_add(out=m2[:], in0=m2[:], scalar1=eps), r=["ln_m2"], w=["ln_m2"])
        P.op("act", lambda E: E.activation(out=rstd[:], in_=m2[:], func=AF.Sqrt), r=["ln_m2"], w=["ln_rstd"])
        P.op("dve", lambda E: E.reciprocal(out=rstd[:], in_=rstd[:]), r=["ln_rstd"], w=["ln_rstd"])
        for i in range(KC):
            t = tmp[i % len(tmp)]
            tk = "ln_tmp%d" % (i % len(tmp))
            hs = h[:, i, cols]
            P.op("dve", lambda E, t=t, hs=hs: E.tensor_tensor(out=t[:], in0=hs, in1=mean[:], op=ALU.subtract),
                 r=[zkeys[i], "ln_mean"], w=[tk])
            P.op("pool", lambda E, t=t: E.tensor_tensor(out=t[:], in0=t[:], in1=rstd[:], op=ALU.mult),
                 r=[tk, "ln_rstd"], w=[tk])
            P.op("act", lambda E, t=t, hs=hs, i=i: E.activation(
                out=hs, in_=t[:], func=AF.Identity, bias=b_ap(i), scale=g_ap(i)),
                r=[tk, "lnS"], w=[ykeys[i]])

    def phase_ffn(self, l, f, src, dst):
        P, nc = self.P, self.nc
        s = 0 if f == 0 else 2
        TT = 1024
        NH = TT // 512
        NST = NTOK // TT
        eps = LN_EPS / (ALPHA * ALPHA)
        with ExitStack() as pst:
            def alloc(name, shape, dt):
                return pst.enter_context(self.sbt(name, shape, dt))
            ht = [alloc("ht%d" % i, [128, KC, TT], F32) for i in range(2)]
            hmod = [alloc("hmod%d" % i, [128, KC, TT], BF16) for i in range(2)]
            a = alloc("a_act", [128, FC, TT], BF16)
            NW = 3
            w1t = [alloc("w1t%d" % i, [128, KC, 128], BF16) for i in range(NW)]
            w3t = [alloc("w3t%d" % i, [128, KC, 128], BF16) for i in range(NW)]
            w2t = [alloc("w2t%d" % i, [128, FC, 128], BF16) for i in range(3)]
            sil = [alloc("sil%d" % i, [128, 512], F32) for i in range(2)]
            zb = [alloc("zb%d" % i, [128, 512], BF16) for i in range(2)]
            zq = [alloc("zq%d" % i, [128, 512], BF16) for i in range(2)]
            lnbufs = (alloc("mean", [128, 512], F32), alloc("m2", [128, 512], F32), alloc("rstd", [128, 512], F32),
                      [alloc("lntmp%d" % i, [128, 512], F32) for i in range(2)])
            ps = self.psum
            srcv = src.rearrange("(kc p) t -> p kc t", p=128)
            dstv = dst.rearrange("(kc p) t -> p kc t", p=128)
            lnS, s1p, gp, ada = self.lnS, self.s1p, self.gp, self.ada
            sh0 = s * 24
            statb = [6, 7, 0, 2]
            cnt = dict(u=0, w=0, w2=0, z=0)

            def load_tile(st):
                t = ht[st % 2]
                P.op("sync", lambda E: E.dma_start(out=t[:], in_=srcv[:, :, st * TT:(st + 1) * TT]),
                     w=["ht%d" % (st % 2)], dma="ht%d" % (st % 2))

            load_tile(0)
            for st in range(NST):
                bl = (st * TT) // SEQ
                h = ht[st % 2]
                hk = "ht%d" % (st % 2)
                hm = hmod[st % 2]
                hm_keys = ["hmod%d_%d" % (st % 2, kc) for kc in range(KC)]
                if st + 1 < NST:
                    load_tile(st + 1)
                for kc in range(KC):
                    P.op("act", lambda E, kc=kc, h=h, hm=hm, bl=bl: E.activation(
                        out=hm[:, kc, :], in_=h[:, kc, :], func=AF.Identity,
                        bias=ada[:, sh0 + kc, bl:bl + 1], scale=s1p[:, s, kc, bl:bl + 1]),
                        r=[hk, "ada", "s1p"], w=[hm_keys[kc]])
                for j in range(FC):
                    wi = cnt["w"] % NW
                    cnt["w"] += 1
                    P.op("sync", lambda E, wi=wi, j=j: E.dma_start(out=w1t[wi][:], in_=self.w1q[l, f, j]),
                         w=["w1t%d" % wi], dma="w1t%d" % wi)
                    P.op("sync", lambda E, wi=wi, j=j: E.dma_start(out=w3t[wi][:], in_=self.w3q[l, f, j]),
                         w=["w3t%d" % wi], dma="w3t%d" % wi)
                    for hf in range(NH):
                        pu = cnt["u"] % 2
                        pg = 2 + cnt["u"] % 2
                        cnt["u"] += 1
                        cols = slice(hf * 512, (hf + 1) * 512)
                        for kc in range(KC):
                            P.op("pe", lambda E, wi=wi, kc=kc, cols=cols, pu=pu, hm=hm: E.matmul(
                                ps[pu][:], lhsT=w1t[wi][:, kc, :], rhs=hm[:, kc, cols],
                                start=(kc == 0), stop=(kc == KC - 1)),
                                r=["w1t%d" % wi, hm_keys[kc]], w=["ps%d" % pu])
                        for kc in range(KC):
                            P.op("pe", lambda E, wi=wi, kc=kc, cols=cols, pg=pg, hm=hm: E.matmul(
                                ps[pg][:], lhsT=w3t[wi][:, kc, :], rhs=hm[:, kc, cols],
                                start=(kc == 0), stop=(kc == KC - 1)),
                                r=["w3t%d" % wi, hm_keys[kc]], w=["ps%d" % pg])
                        sl = sil[pu]
                        P.op("act", lambda E, sl=sl, pu=pu: E.activation(out=sl[:], in_=ps[pu][:], func=AF.Silu),
                             r=["ps%d" % pu], w=["sil%d" % pu])
                        P.op("dve", lambda E, sl=sl, pg=pg, j=j, cols=cols: E.tensor_tensor(
                            out=a[:, j, cols], in0=ps[pg][:], in1=sl[:], op=ALU.mult),
                            r=["ps%d" % pg, "sil%d" % pu], w=["a_%d_%d" % (j, hf)])
                ykeys_all = []
                for hf in range(NH):
                    cols = slice(hf * 512, (hf + 1) * 512)
                    bm, be = 6, 7
                    for i in range(KC):
                        wi = cnt["w2"] % 3
                        cnt["w2"] += 1
                        P.op("sync", lambda E, wi=wi, i=i: E.dma_start(out=w2t[wi][:], in_=self.w2q[l, f, i]),
                             w=["w2t%d" % wi], dma="w2t%d" % wi)
                        pf = 4 + cnt["z"] % 2
                        zi = cnt["z"] % 2
                        cnt["z"] += 1
                        for j in range(FC):
                            P.op("pe", lambda E, wi=wi, j=j, cols=cols, pf=pf: E.matmul(
                                ps[pf][:], lhsT=w2t[wi][:, j, :], rhs=a[:, j, cols],
                                start=(j == 0), stop=(j == FC - 1)),
                                r=["w2t%d" % wi, "a_%d_%d" % (j, hf)], w=["ps%d" % pf])
                        hs = h[:, i, cols]
                        zk = hk + "_z%d_%d" % (i, hf)
                        P.op("dve", lambda E, pf=pf, hs=hs, i=i, bl=bl: E.scalar_tensor_tensor(
                            out=hs, in0=ps[pf][:], scalar=gp[:, s, i, bl:bl + 1], in1=hs,
                            op0=ALU.mult, op1=ALU.add),
                            r=["ps%d" % pf, "gp", hk], w=[zk])
                        P.op("act", lambda E, hs=hs, zi=zi: E.activation(out=zb[zi][:], in_=hs, func=AF.Copy),
                             r=[zk], w=["zb%d" % zi])
                        P.op("act", lambda E, hs=hs, zi=zi: E.activation(out=zq[zi][:], in_=hs, func=AF.Square),
                             r=[zk], w=["zq%d" % zi])
                        P.op("pe", lambda E, zi=zi, bm=bm, i=i: E.matmul(
                            ps[bm][:], lhsT=self.ones_bf[:], rhs=zb[zi][:], start=(i == 0), stop=(i == KC - 1)),
                            r=["ones", "zb%d" % zi], w=["ps%d" % bm])
                        P.op("pe", lambda E, zi=zi, be=be, i=i: E.matmul(
                            ps[be][:], lhsT=self.ones_bf[:], rhs=zq[zi][:], start=(i == 0), stop=(i == KC - 1)),
                            r=["ones", "zq%d" % zi], w=["ps%d" % be])
                    zkeys = [hk + "_z%d_%d" % (i, hf) for i in range(KC)]
                    ykeys = [hk + "_y%d_%d" % (i, hf) for i in range(KC)]
                    ykeys_all += ykeys
                    self.ln_finish(h, zkeys, ykeys, cols, ps[bm], ps[be], "ps%d" % bm, "ps%d" % be,
                                   lambda i: lnS[:, l, s, 0, i:i + 1], lambda i: lnS[:, l, s, 1, i:i + 1],
                                   lnbufs, eps)
                P.op("sync", lambda E, st=st, h=h: E.dma_start(out=dstv[:, :, st * TT:(st + 1) * TT], in_=h[:]),
                     r=ykeys_all, w=[hk], dma="hst%d" % (st % 2))
            self.end_phase()


    def mixer_decl(self):
        L = self.L
        self.w_in = self.din("w_in", [L, D, N_IN])
        self.b_inT = self.din("b_inT", [128, L, NCH + 1])
        self.b_vrow = self.din("b_vrow", [L, 1, 1536])
        self.qnT = self.din("qnT", [128, L, 3])
        self.kvnT = self.din("kvnT", [128, L, 2])
        self.w_qb = self.din("w_qb", [L, 384, 768])
        self.w_kvb = self.din("w_kvb", [L, 256, 1024])
        self.w_glu = self.din("w_glu", [L, 512, 512])
        self.b_gluT = self.din("b_gluT", [128, L, 4])
        self.w_br = self.din("w_br", [L, 3, 512, D])
        self.w_out = self.din("w_out", [L, D, D])
        self.pos = self.din("pos", [NB, SEQ], I32)
        self.maskc = self.din("maskc", [128, 4, 512], BF16)
        self.dilmb = self.din("dilmb", [128, 12, 256])
        self.ident = self.din("ident", [128, 128])
        self.pswap = self.din("pswap", [128, 128])
        self.lamreT = self.din("lamreT", [128, L, 32])
        self.lamimT = self.din("lamimT", [128, L, 32])
        self.logdtB = self.din("logdtB", [128, L, 32])
        self.bX = self.din("bX", [128, L, 32, 16])
        self.bY = self.din("bY", [128, L, 32, 16])
        self.cX = self.din("cX", [128, L, 32, 16])
        self.cY = self.din("cY", [128, L, 32, 16])
        self.dT = self.din("dT", [128, L, 4])
        self.winq = self.dscr("winq", [L, NCH, 128, KC, 128], BF16)
        self.wksw = self.dscr("wksw", [L, 128, KC, 96], BF16)
        self.wvq = self.dscr("wvq", [L, 128, KC, 1536], BF16)
        self.wqbq = self.dscr("wqbq", [L, 128, 3, 768], BF16)
        self.wqbs = self.dscr("wqbs", [L, 128, 3, 768], BF16)
        self.wkvbq = self.dscr("wkvbq", [L, 128, 2, 1024], BF16)
        self.wgluq = self.dscr("wgluq", [L, 128, 4, 512], BF16)
        self.wbrq = self.dscr("wbrq", [L, 3, 128, 4, D], BF16)
        self.wbr0q = self.dscr("wbr0q", [L, 64, 8, D], BF16)
        self.woutq = self.dscr("woutq", [L, 128, KC, D], BF16)
        self.qhS = self.dscr("qhS", [8, 96, SEQ], BF16)
        self.khS = self.dscr("khS", [8, 96, SEQ], BF16)
        self.vS = self.dscr("vS", [SEQ, 512], BF16)
        self.ymlaS = self.dscr("ymlaS", [8, 64, SEQ], BF16)
        self.ydilS = self.dscr("ydilS", [4, 128, SEQ], BF16)
        self.yssmS = self.dscr("yssmS", [4, 128, SEQ], BF16)
        self.tabS = self.dscr("tabS", [4, 128, 8, 2, 128])
        self.wabS = self.dscr("wabS", [4, 128, 8, 2, 128], BF16)
        self.wcS = self.dscr("wcS", [4, 128, 8, 2, 128], BF16)

    def mixer_precast(self, n):
        P = self.P

        def cast(o, i):
            nonlocal n
            P.op("pool", lambda E, o=o, i=i: E.dma_start(out=o, in_=i), w=[], dma="pc%d" % (n % 8))
            n += 1
        for l in range(self.L):
            wi = self.w_in[l]
            for c, (cs, cw) in enumerate(W_CHUNKS):
                cast(self.winq[l, c, :, :, 0:cw], wi[:, cs:cs + cw].rearrange("(kc p) n -> p kc n", p=128))
            for (d0, s0, wdt) in ((0, 576, 64), (64, 656, 16), (80, 640, 16)):
                cast(self.wksw[l, :, :, d0:d0 + wdt], wi[:, s0:s0 + wdt].rearrange("(kc p) n -> p kc n", p=128))
            for kc in range(KC):
                cast(self.wvq[l, :, kc, :], wi[kc * 128:(kc + 1) * 128, 672 + 3072:672 + 4608])
            cast(self.wqbq[l], self.w_qb[l].rearrange("(kc p) n -> p kc n", p=128))
            qv = self.w_qb[l].rearrange("(kc p) (h n) -> p kc h n", p=128, n=96)
            qs = self.wqbs[l].rearrange("p kc (h n) -> p kc h n", n=96)
            for kc in range(3):
                for (d0, s0, wdt) in ((0, 0, 64), (64, 80, 16), (80, 64, 16)):
                    cast(qs[:, kc, :, d0:d0 + wdt], qv[:, kc, :, s0:s0 + wdt])
            cast(self.wkvbq[l], self.w_kvb[l].rearrange("(kc p) n -> p kc n", p=128))
            cast(self.wgluq[l], self.w_glu[l].rearrange("(kc p) n -> p kc n", p=128))
            for k in range(3):
                cast(self.wbrq[l, k], self.w_br[l, k].rearrange("(kc p) n -> p kc n", p=128))
            cast(self.wbr0q[l], self.w_br[l, 0].rearrange("(h p) n -> p h n", p=64))
            for kc in range(KC):
                cast(self.woutq[l, :, kc, :], self.w_out[l, kc * 128:(kc + 1) * 128, :])
        return n

    def phase_mixer(self, l, src, dst):
        nc = self.nc
        parts = self.mix_parts
        if parts == "all" or "ssm" in parts:
            self.mix_consts(l)
        for bl in range(NB):
            with ExitStack() as seq:
                hmod = seq.enter_context(self.sbt("hmodS", [128, KC, SEQ], BF16))
                self.mix_s1(l, bl, src, hmod)
                if parts == "all" or "dil" in parts:
                    self.mix_dil(l, bl, hmod)
                if parts == "all" or "ssm" in parts:
                    self.mix_ssm(l, bl, hmod)
            if parts == "all" or "mla" in parts:
                self.mix_mla(l, bl)
            if parts == "all" or "merge" in parts:
                self.mix_merge(l, bl, src, dst)

    def mix_s1(self, l, bl, src, hmod):
        P, nc, ps = self.P, self.nc, self.psum
        cst = self.cstS
        tok0 = bl * SEQ
        srcv = src.rearrange("(kc p) t -> p kc t", p=128)
        sh0 = 24
        with ExitStack() as st:
            def A(name, shape, dt):
                return st.enter_context(self.sbt(name, shape, dt))
            ht = [A("s1ht%d" % i, [128, KC, 512], F32) for i in range(2)]
            wlat = A("wlat", [128, 6, KC, 128], BF16)
            wks = A("wks", [128, KC, 96], BF16)
            wq = A("wq", [128, 3, 768], BF16)
            wqs = A("wqs", [128, 3, 768], BF16)
            wkv = A("wkv", [128, 2, 1024], BF16)
            bS = A("b_inS", [128, NCH + 1], F32)
            qn = A("qnS", [128, 3], F32)
            kvng = A("kvngS", [128, 2], F32)
            onesq = A("onesq", [128, 128], BF16)
            oneskv = A("oneskv", [128, 128], BF16)
            latf = [A("latf%d" % i, [128, 512], F32) for i in range(5)]
            sq = [A("sq%d" % i, [128, 512], BF16) for i in range(2)]
            rq = [A("rq%d" % i, [128, 512], F32) for i in range(2)]
            qan_t = A("qan_t", [128, 3, 512], BF16)
            kvn_t = A("kvn_t", [128, 2, 512], BF16)
            kpe_t = A("kpe_t", [128, 512], BF16)
            posi = A("posi", [128, 512], I32)
            ang = A("ang", [128, 512], F32)
            kk = A("kk", [128, 512], F32)
            CC = A("CC", [128, 512], F32)
            SS = A("SS", [128, 512], F32)
            tmp1 = [A("rt1_%d" % i, [128, 512], F32) for i in range(2)]
            tmp2 = [A("rt2_%d" % i, [128, 512], F32) for i in range(2)]
            qh_t = [A("qh_t%d" % i, [128, 512], BF16) for i in range(2)]
            kh_t = [A("kh_t%d" % i, [128, 512], BF16) for i in range(2)]
            v_t = [A("v_t%d" % i, [128, 4, 512], BF16) for i in range(2)]
            R = slice(64, 96)
            P.op("sync", lambda E: E.dma_start(out=wlat[:], in_=self.winq[l, 0:6].rearrange("c p k n -> p c k n")),
                 w=["wlat"], dma="s1w0")
            P.op("sync", lambda E: E.dma_start(out=wks[:], in_=self.wksw[l]), w=["wks"], dma="s1w1")
            P.op("sync", lambda E: E.dma_start(out=wq[:], in_=self.wqbq[l]), w=["wq"], dma="s1w2")
            P.op("sync", lambda E: E.dma_start(out=wqs[:], in_=self.wqbs[l]), w=["wqs"], dma="s1w3")
            P.op("sync", lambda E: E.dma_start(out=wkv[:], in_=self.wkvbq[l]), w=["wkv"], dma="s1w4")
            P.op("sync", lambda E: E.dma_start(out=bS[:], in_=self.b_inT[:, l, :]), w=["bS"], dma="s1w5")
            P.op("sync", lambda E: E.dma_start(out=qn[:], in_=self.qnT[:, l, :]), w=["qn"], dma="s1w6")
            P.op("sync", lambda E: E.dma_start(out=kvng[:], in_=self.kvnT[:, l, :]), w=["kvng"], dma="s1w7")
            P.op("pool", lambda E: E.memset(onesq[:], 1.0 / 384), w=["onesq"])
            P.op("pool", lambda E: E.memset(oneskv[:], 1.0 / 256), w=["oneskv"])
            wvv = wkv[:].rearrange("p k (h two d) -> p k h two d", two=2, d=64)
            cnt = dict(b=0, r=0, h=0)

            def bank2(base):
                b = base + cnt["b"] % 2
                cnt["b"] += 1
                return b
            for t in range(SEQ // 512):
                h = ht[t % 2]
                hk = "s1ht%d" % (t % 2)
                cols = slice(t * 512, (t + 1) * 512)
                P.op("sync", lambda E, h=h, t=t: E.dma_start(
                    out=h[:], in_=srcv[:, :, tok0 + t * 512:tok0 + (t + 1) * 512]), w=[hk], dma=hk)
                for kc in range(KC):
                    P.op("act", lambda E, kc=kc, h=h, cols=cols: E.activation(
                        out=hmod[:, kc, cols], in_=h[:, kc, :], func=AF.Identity,
                        bias=self.ada[:, sh0 + kc, bl:bl + 1], scale=self.s1p[:, 1, kc, bl:bl + 1]),
                        r=[hk, "ada", "s1p"], w=["hmod_%d_%d" % (t, kc)])
                hmk = ["hmod_%d_%d" % (t, kc) for kc in range(KC)]
                P.op("sync", lambda E, t=t: E.dma_start(
                    out=posi[R, :], in_=self.pos[bl:bl + 1, t * 512:(t + 1) * 512].partition_broadcast(32)),
                    w=["posi"], dma="posi")
                P.op("dve", lambda E: E.tensor_copy(out=ang[R, :], in_=posi[R, :]), r=["posi"], w=["ang"])
                P.op("dve", lambda E: E.tensor_scalar(out=ang[R, :], in0=ang[R, :], scalar1=cst[R, 0:1], scalar2=None,
                                                      op0=ALU.mult), r=["ang", "cst"], w=["ang"])
                P.op("dve", lambda E: E.tensor_scalar(out=kk[R, :], in0=ang[R, :], scalar1=1.0 / TWO_PI, scalar2=MAGIC,
                                                      op0=ALU.mult, op1=ALU.add), r=["ang"], w=["kk"])
                P.op("dve", lambda E: E.tensor_scalar_sub(out=kk[R, :], in0=kk[R, :], scalar1=MAGIC), r=["kk"], w=["kk"])
                P.op("dve", lambda E: E.scalar_tensor_tensor(out=ang[R, :], in0=kk[R, :], scalar=-TWO_PI, in1=ang[R, :],
                                                             op0=ALU.mult, op1=ALU.add), r=["kk", "ang"], w=["ang"])
                P.op("act", lambda E: E.activation(out=SS[R, :], in_=ang[R, :], func=AF.Sin, scale=cst[R, 1:2]),
                     r=["ang", "cst"], w=["SS"])
                P.op("act", lambda E: E.activation(out=kk[R, :], in_=ang[R, :], func=AF.Abs), r=["ang"], w=["kk"])
                P.op("act", lambda E: E.activation(out=CC[R, :], in_=kk[R, :], func=AF.Sin, scale=-1.0, bias=cst[R, 6:7]),
                     r=["kk", "cst"], w=["CC"])
                for c in range(5):
                    pb = bank2(0)
                    for kc in range(KC):
                        P.op("pe", lambda E, c=c, kc=kc, pb=pb, cols=cols: E.matmul(
                            ps[pb][:], lhsT=wlat[:, c, kc, :], rhs=hmod[:, kc, cols], start=(kc == 0), stop=(kc == KC - 1)),
                            r=["wlat", hmk[kc]], w=["ps%d" % pb])
                    P.op("act", lambda E, c=c, pb=pb: E.activation(out=latf[c][:], in_=ps[pb][:], func=AF.Identity,
                                                                   bias=bS[:, c:c + 1]), r=["ps%d" % pb, "bS"], w=["latf%d" % c])
                    sqi = c % 2
                    P.op("act", lambda E, c=c, pb=pb, sqi=sqi: E.activation(out=sq[sqi][:], in_=ps[pb][:], func=AF.Square,
                                                                            bias=bS[:, c:c + 1]), r=["ps%d" % pb, "bS"], w=["sq%d" % sqi])
                    if c < 3:
                        P.op("pe", lambda E, c=c, sqi=sqi: E.matmul(ps[2][:], lhsT=onesq[:], rhs=sq[sqi][:],
                                                                  start=(c == 0), stop=(c == 2)), r=["onesq", "sq%d" % sqi], w=["ps2"])
                    else:
                        P.op("pe", lambda E, c=c, sqi=sqi: E.matmul(ps[3][:], lhsT=oneskv[:], rhs=sq[sqi][:],
                                                                  start=(c == 3), stop=(c == 4)), r=["oneskv", "sq%d" % sqi], w=["ps3"])
                for wi_, (pbk, c0, nch, gain, outt, key) in enumerate(((2, 0, 3, qn, qan_t, "qan_t"), (3, 3, 2, kvng, kvn_t, "kvn_t"))):
                    rr = rq[wi_]
                    rk = "rq%d" % wi_
                    P.op("dve", lambda E, rr=rr, pbk=pbk: E.tensor_scalar_add(out=rr[:], in0=ps[pbk][:], scalar1=RMS_EPS),
                         r=["ps%d" % pbk], w=[rk])
                    P.op("act", lambda E, rr=rr: E.activation(out=rr[:], in_=rr[:], func=AF.Sqrt), r=[rk], w=[rk])
                    P.op("dve", lambda E, rr=rr: E.reciprocal(out=rr[:], in_=rr[:]), r=[rk], w=[rk])
                    for j in range(nch):
                        P.op("dve", lambda E, rr=rr, j=j, c0=c0, gain=gain, outt=outt: E.scalar_tensor_tensor(
                            out=outt[:, j, :], in0=latf[c0 + j][:], scalar=gain[:, j:j + 1], in1=rr[:],
                            op0=ALU.mult, op1=ALU.mult), r=["latf%d" % (c0 + j), rk, "qn", "kvng"], w=[key + "%d" % j])
                qank = ["qan_t%d" % j for j in range(3)]
                kvnk = ["kvn_t%d" % j for j in range(2)]
                for kc in range(KC):
                    P.op("pe", lambda E, kc=kc, cols=cols: E.matmul(ps[4][0:96, :], lhsT=wlat[:, 5, kc, 0:96], rhs=hmod[:, kc, cols],
                                                                    start=(kc == 0), stop=(kc == KC - 1)), r=["wlat", hmk[kc]], w=["ps4"])
                for kc in range(KC):
                    P.op("pe", lambda E, kc=kc, cols=cols: E.matmul(ps[5][0:96, :], lhsT=wks[:, kc, :], rhs=hmod[:, kc, cols],
                                                                    start=(kc == 0), stop=(kc == KC - 1)), r=["wks", hmk[kc]], w=["ps5"])
                P.op("dve", lambda E: E.scalar_tensor_tensor(out=tmp1[0][R, :], in0=ps[4][R, :], scalar=bS[R, CH_KR:CH_KR + 1],
                                                             in1=CC[R, :], op0=ALU.add, op1=ALU.mult), r=["ps4", "bS", "CC"], w=["rt1_0"])
                P.op("dve", lambda E: E.scalar_tensor_tensor(out=tmp2[0][R, :], in0=ps[5][R, :], scalar=bS[R, NCH:NCH + 1],
                                                             in1=SS[R, :], op0=ALU.add, op1=ALU.mult), r=["ps5", "bS", "SS"], w=["rt2_0"])
                P.op("pool", lambda E: E.tensor_tensor(out=kpe_t[R, :], in0=tmp1[0][R, :], in1=tmp2[0][R, :], op=ALU.add),
                     r=["rt1_0", "rt2_0"], w=["kpe_t"])
                for hd in range(8):
                    i2 = cnt["h"] % 2
                    cnt["h"] += 1
                    for kc in range(3):
                        P.op("pe", lambda E, kc=kc, hd=hd: E.matmul(ps[4][0:96, :], lhsT=wq[:, kc, hd * 96:(hd + 1) * 96], rhs=qan_t[:, kc, :],
                                                                   start=(kc == 0), stop=(kc == 2)), r=["wq", qank[kc]], w=["ps4"])
                    for kc in range(3):
                        P.op("pe", lambda E, kc=kc, hd=hd: E.matmul(ps[5][0:96, :], lhsT=wqs[:, kc, hd * 96:(hd + 1) * 96], rhs=qan_t[:, kc, :],
                                                                   start=(kc == 0), stop=(kc == 2)), r=["wqs", qank[kc]], w=["ps5"])
                    for kc in range(2):
                        P.op("pe", lambda E, kc=kc, hd=hd: E.matmul(ps[6][0:64, :], lhsT=wkv[:, kc, hd * 128:hd * 128 + 64], rhs=kvn_t[:, kc, :],
                                                                   start=(kc == 0), stop=(kc == 1)), r=["wkv", kvnk[kc]], w=["ps6"])
                    qt_, kt_ = qh_t[i2], kh_t[i2]
                    qk_, kk_ = "qh_t%d" % i2, "kh_t%d" % i2
                    P.op("act", lambda E, qt_=qt_: E.activation(out=qt_[0:64, :], in_=ps[4][0:64, :], func=AF.Copy),
                         r=["ps4"], w=[qk_ + "a"])
                    P.op("dve", lambda E, i2=i2: E.tensor_tensor(out=tmp1[i2][R, :], in0=ps[4][R, :], in1=CC[R, :], op=ALU.mult),
                         r=["ps4", "CC"], w=["rt1_%d" % i2])
                    P.op("dve", lambda E, i2=i2: E.tensor_tensor(out=tmp2[i2][R, :], in0=ps[5][R, :], in1=SS[R, :], op=ALU.mult),
                         r=["ps5", "SS"], w=["rt2_%d" % i2])
                    P.op("pool", lambda E, i2=i2, qt_=qt_: E.tensor_tensor(out=qt_[R, :], in0=tmp1[i2][R, :], in1=tmp2[i2][R, :], op=ALU.add),
                         r=["rt1_%d" % i2, "rt2_%d" % i2], w=[qk_ + "b"])
                    P.op("sync", lambda E, hd=hd, qt_=qt_, cols=cols: E.dma_start(out=self.qhS[hd, :, cols], in_=qt_[0:96, :]),
                         r=[qk_ + "a", qk_ + "b"], w=[], dma=qk_)
                    P.op("act", lambda E, kt_=kt_: E.activation(out=kt_[0:64, :], in_=ps[6][0:64, :], func=AF.Copy),
                         r=["ps6"], w=[kk_ + "a"])
                    P.op("pool", lambda E, kt_=kt_: E.tensor_copy(out=kt_[R, :], in_=kpe_t[R, :]), r=["kpe_t"], w=[kk_ + "b"])
                    P.op("sync", lambda E, hd=hd, kt_=kt_, cols=cols: E.dma_start(out=self.khS[hd, :, cols], in_=kt_[0:96, :]),
                         r=[kk_ + "a", kk_ + "b"], w=[], dma=kk_)
                vt = v_t[t % 2]
                vk = "v_t%d" % (t % 2)
                for blk in range(4):
                    for kc in range(2):
                        P.op("pe", lambda E, kc=kc, blk=blk: E.matmul(
                            ps[7][:].rearrange("p (h d) -> p h d", d=64), lhsT=kvn_t[:, kc, blk * 128:(blk + 1) * 128],
                            rhs=wvv[:, kc, :, 1, :], start=(kc == 0), stop=(kc == 1)), r=["wkv", kvnk[kc]], w=["ps7"])
                    P.op("act", lambda E, blk=blk, vt=vt: E.activation(out=vt[:, blk, :], in_=ps[7][:], func=AF.Copy),
                         r=["ps7"], w=[vk + "_%d" % blk])
                P.op("sync", lambda E, vt=vt, t=t: E.dma_start(
                    out=self.vS[t * 512:(t + 1) * 512, :].rearrange("(b p) n -> p b n", p=128), in_=vt[:]),
                    r=[vk + "_%d" % blk for blk in range(4)], w=[], dma=vk)
            self.end_phase()

    def mix_mla(self, l, bl):
        P, nc, ps = self.P, self.nc, self.psum
        scale = 96.0 ** -0.5
        with ExitStack() as st:
            def A(name, shape, dt):
                return st.enter_context(self.sbt(name, shape, dt))
            Vall = A("Vall", [128, 32, 512], BF16)
            masks = A("masks", [128, 4, 512], BF16)
            ones64 = A("ones64", [128, 64], BF16)
            qh = [A("qh%d" % i, [128, SEQ], BF16) for i in range(2)]
            kh = [A("kh%d" % i, [128, SEQ], BF16) for i in range(2)]
            pt = [A("pt%d" % i, [128, 512], BF16) for i in range(4)]
            rec = [A("rec%d" % i, [128, 512], F32) for i in range(2)]
            yh = [A("yh%d" % i, [128, SEQ], BF16) for i in range(2)]
            P.op("sync", lambda E: E.dma_start(out=Vall[:], in_=self.vS.rearrange("(b p) n -> p b n", p=128)),
                 w=["Vall"], dma="mlaw0")
            P.op("sync", lambda E: E.dma_start(out=masks[:], in_=self.maskc), w=["masks"], dma="mlaw1")
            P.op("pool", lambda E: E.memset(ones64[:], 1.0), w=["ones64"])
            LA = 2
            stages = []
            m = 0
            n = 0
            for hd in range(8):
                q_, k_, y_ = qh[hd % 2], kh[hd % 2], yh[hd % 2]
                qk, kk_, yk = "qh%d" % (hd % 2), "kh%d" % (hd % 2), "yh%d" % (hd % 2)
                first_of_head = True
                for qt in range(SEQ // 512):
                    po, pd = 4 + n % 2, 6 + n % 2
                    ri = n % 2
                    n += 1
                    qcols = slice(qt * 512, (qt + 1) * 512)
                    nkb = 4 * (qt + 1)
                    for kb in range(nkb):
                        sb = m % 4
                        m += 1

                        def stA(hd=hd, q_=q_, k_=k_, qk=qk, kk_=kk_, kb=kb, sb=sb, qcols=qcols, qt=qt, load=first_of_head):
                            if load:
                                P.op("sync", lambda E: E.dma_start(out=q_[0:96, :], in_=self.qhS[hd]), w=[qk], dma=qk)
                                P.op("sync", lambda E: E.dma_start(out=k_[0:96, :], in_=self.khS[hd]), w=[kk_], dma=kk_)
                            P.op("pe", lambda E: E.matmul(
                                ps[sb][:], lhsT=k_[0:96, kb * 128:(kb + 1) * 128], rhs=q_[0:96, qcols], start=True, stop=True),
                                r=[qk, kk_], w=["ps%d" % sb])
                            P.op("act", lambda E: E.activation(out=pt[sb][:], in_=ps[sb][:], func=AF.Exp, scale=scale),
                                 r=["ps%d" % sb], w=["pt%d" % sb])
                            if kb >= 4 * qt:
                                j = kb - 4 * qt
                                P.op("pool", lambda E: E.tensor_tensor(
                                    out=pt[sb][:], in0=pt[sb][:], in1=masks[:, j, :], op=ALU.mult),
                                    r=["pt%d" % sb, "masks"], w=["pt%d" % sb])

                        def stB(hd=hd, y_=y_, yk=yk, kb=kb, sb=sb, po=po, pd=pd, ri=ri, nkb=nkb, qcols=qcols, qt=qt):
                            P.op("pe", lambda E: E.matmul(
                                ps[po][0:64, :], lhsT=Vall[:, kb, hd * 64:(hd + 1) * 64], rhs=pt[sb][:],
                                start=(kb == 0), stop=(kb == nkb - 1)), r=["Vall", "pt%d" % sb], w=["ps%d" % po])
                            P.op("pe", lambda E: E.matmul(
                                ps[pd][0:64, :], lhsT=ones64[:], rhs=pt[sb][:],
                                start=(kb == 0), stop=(kb == nkb - 1)), r=["ones64", "pt%d" % sb], w=["ps%d" % pd])
                            if kb == nkb - 1:
                                P.op("dve", lambda E: E.reciprocal(out=rec[ri][0:64, :], in_=ps[pd][0:64, :]),
                                     r=["ps%d" % pd], w=["rec%d" % ri])
                                P.op("dve", lambda E: E.tensor_tensor(
                                    out=y_[0:64, qcols], in0=ps[po][0:64, :], in1=rec[ri][0:64, :], op=ALU.mult),
                                    r=["ps%d" % po, "rec%d" % ri], w=[yk + "_%d" % qt])
                                if qt == SEQ // 512 - 1:
                                    P.op("sync", lambda E: E.dma_start(out=self.ymlaS[hd], in_=y_[0:64, :]),
                                         r=[yk + "_%d" % q2 for q2 in range(SEQ // 512)], w=[yk], dma=yk)
                        stages.append((stA, stB))
                        first_of_head = False
            for i in range(len(stages) + LA):
                if i < len(stages):
                    stages[i][0]()
                if i - LA >= 0:
                    stages[i - LA][1]()
            self.end_phase()
    def mix_dil(self, l, bl, hmod):
        P, nc, ps = self.P, self.nc, self.psum
        scale = 128.0 ** -0.5
        with ExitStack() as st:
            def A(name, shape, dt):
                return st.enter_context(self.sbt(name, shape, dt))
            MB = A("MB", [128, 12, 256], F32)
            bvb = A("bvb", [128, 1536], F32)
            wv = A("wvd", [128, KC, 1536], BF16)
            ones128 = A("ones128", [128, 128], BF16)
            bS = A("b_inD", [128, NCH + 1], F32)
            num = A("num", [128, SEQ], F32)
            den = A("den", [128, SEQ], F32)
            wqk = [A("wqk%d" % i, [128, 2, KC, 128], BF16) for i in range(2)]
            qd = A("qd", [128, SEQ], BF16)
            kd = A("kd", [128, SEQ], BF16)
            Vh = A("Vh", [128, 32, 128], BF16)
            stmp = [A("stmp%d" % i, [128, 256], F32) for i in range(4)]
            ptd = [A("ptd%d" % i, [128, 256], BF16) for i in range(4)]
            yd = A("yd", [128, SEQ], BF16)
            P.op("sync", lambda E: E.dma_start(out=MB[:], in_=self.dilmb), w=["MB"], dma="dw0")
            P.op("sync", lambda E: E.dma_start(out=bvb[:], in_=self.b_vrow[l].partition_broadcast(128)), w=["bvb"], dma="dw1")
            P.op("sync", lambda E: E.dma_start(out=wv[:], in_=self.wvq[l]), w=["wvd"], dma="dw2")
            P.op("sync", lambda E: E.dma_start(out=bS[:], in_=self.b_inT[:, l, :]), w=["bSd"], dma="dw3")
            P.op("pool", lambda E: E.memset(ones128[:], 1.0), w=["ones128"])
            hmk_all = "hmod_all"
            cnt = dict(s=0, q=0)
            for slot in range(4):
                for g, (window, d) in enumerate(DIL_PAIRS):
                    hd = g * 4 + slot
                    Ls = SEQ // d
                    nbk = Ls // 128
                    wi = cnt["q"] % 2
                    cnt["q"] += 1
                    wt = wqk[wi]
                    wk_ = "wqk%d" % wi
                    P.op("sync", lambda E, wt=wt, hd=hd: E.dma_start(out=wt[:, 0], in_=self.winq[l, CH_DIL + hd]),
                         w=[wk_ + "q"], dma=wk_ + "q")
                    P.op("sync", lambda E, wt=wt, hd=hd: E.dma_start(out=wt[:, 1], in_=self.winq[l, CH_DIL + 12 + hd]),
                         w=[wk_ + "k"], dma=wk_ + "k")
                    for t in range(SEQ // 512):
                        cols = slice(t * 512, (t + 1) * 512)
                        for which, dstT, key, bank in ((0, qd, "qd", 0), (1, kd, "kd", 1)):
                            for kc in range(KC):
                                P.op("pe", lambda E, wt=wt, which=which, kc=kc, cols=cols, bank=bank: E.matmul(
                                    ps[bank][:], lhsT=wt[:, which, kc, :], rhs=hmod[:, kc, cols],
                                    start=(kc == 0), stop=(kc == KC - 1)), r=[wk_ + "qk"[which], hmk_all], w=["ps%d" % bank])
                            n_i = 512 // d
                            dv = dstT[:].rearrange("p (r i) -> p r i", r=d)[:, :, t * n_i:(t + 1) * n_i]
                            sv = ps[bank][:].rearrange("p (i r) -> p r i", r=d)
                            bcol = CH_DIL + which * 12 + hd
                            P.op("act", lambda E, dv=dv, sv=sv, bcol=bcol: E.activation(
                                out=dv, in_=sv, func=AF.Identity, bias=bS[:, bcol:bcol + 1]),
                                r=["ps%d" % bank, "bSd"], w=[key])
                    for blk in range(32):
                        r_, kb = blk // nbk, blk % nbk
                        t0 = r_ + d * kb * 128
                        for kc in range(KC):
                            P.op("pe", lambda E, kc=kc, blk=blk, t0=t0, d=d, hd=hd: E.matmul(
                                ps[2][:, (blk % 4) * 128:(blk % 4 + 1) * 128],
                                lhsT=hmod[:, kc, t0:t0 + d * 127 + 1:d], rhs=wv[:, kc, hd * 128:(hd + 1) * 128],
                                start=(kc == 0), stop=(kc == KC - 1)), r=[hmk_all, "wvd"], w=["ps2"])
                        if blk % 4 == 3:
                            P.op("dve", lambda E, blk=blk, hd=hd: E.tensor_tensor(
                                out=Vh[:, blk - 3:blk + 1, :], in0=ps[2][:].rearrange("p (b n) -> p b n", n=128),
                                in1=bvb[:, hd * 128:(hd + 1) * 128].unsqueeze(1).broadcast_to([128, 4, 128]), op=ALU.add),
                                r=["ps2", "bvb"], w=["Vh"])
                    LA = 2
                    SB = [6, 7, 0, 1]
                    stages = []
                    for r_ in range(d):
                        for kb in range(nbk):
                            ncol = 256 if kb < nbk - 1 else 128
                            si = cnt["s"] % 4
                            cnt["s"] += 1
                            sbank = SB[si]
                            c0 = r_ * Ls + kb * 128

                            def stA(c0=c0, ncol=ncol, sbank=sbank, si=si, hd=hd):
                                P.op("pe", lambda E: E.matmul(
                                    ps[sbank][:, 0:ncol], lhsT=kd[:, c0:c0 + 128], rhs=qd[:, c0:c0 + ncol], start=True, stop=True),
                                    r=["qd", "kd"], w=["ps%d" % sbank])
                                P.op("dve", lambda E: E.scalar_tensor_tensor(
                                    out=stmp[si][:, 0:ncol], in0=ps[sbank][:, 0:ncol], scalar=scale, in1=MB[:, hd, 0:ncol],
                                    op0=ALU.mult, op1=ALU.add), r=["ps%d" % sbank, "MB"], w=["stmp%d" % si])
                                P.op("act", lambda E: E.activation(
                                    out=ptd[si][:, 0:ncol], in_=stmp[si][:, 0:ncol], func=AF.Exp), r=["stmp%d" % si], w=["ptd%d" % si])

                            def stB(r_=r_, kb=kb, ncol=ncol, si=si, g=g, d=d, nbk=nbk):
                                vblk = r_ * nbk + kb
                                ob, db = 2 + kb % 2, 4 + kb % 2
                                ob2, db2 = 2 + (kb + 1) % 2, 4 + (kb + 1) % 2
                                for (bank, lhs) in ((ob, None), (db, ones128)):
                                    lt = Vh[:, vblk, :] if lhs is None else lhs[:]
                                    P.op("pe", lambda E, bank=bank, lt=lt: E.matmul(
                                        ps[bank][:, 0:128], lhsT=lt, rhs=ptd[si][:, 0:128], start=(kb == 0), stop=True),
                                        r=["Vh", "ones128", "ptd%d" % si], w=["ps%d" % bank])
                                if ncol == 256:
                                    for (bank, lhs) in ((ob2, None), (db2, ones128)):
                                        lt = Vh[:, vblk, :] if lhs is None else lhs[:]
                                        P.op("pe", lambda E, bank=bank, lt=lt: E.matmul(
                                            ps[bank][:, 0:128], lhsT=lt, rhs=ptd[si][:, 128:256], start=True, stop=False),
                                            r=["Vh", "ones128", "ptd%d" % si], w=["ps%d" % bank])
                                tq = r_ + d * kb * 128
                                tsl = slice(tq, tq + d * 127 + 1, d)
                                if g == 0:
                                    P.op("dve", lambda E: E.tensor_copy(out=num[:, tsl], in_=ps[ob][:, 0:128]),
                                         r=["ps%d" % ob], w=["num"])
                                    P.op("dve", lambda E: E.tensor_copy(out=den[:, tsl], in_=ps[db][:, 0:128]),
                                         r=["ps%d" % db], w=["den"])
                                else:
                                    P.op("dve", lambda E: E.tensor_tensor(
                                        out=num[:, tsl], in0=ps[ob][:, 0:128], in1=num[:, tsl], op=ALU.add),
                                        r=["ps%d" % ob, "num"], w=["num"])
                                    P.op("dve", lambda E: E.tensor_tensor(
                                        out=den[:, tsl], in0=ps[db][:, 0:128], in1=den[:, tsl], op=ALU.add),
                                        r=["ps%d" % db, "den"], w=["den"])
                            stages.append((stA, stB))
                    for i in range(len(stages) + LA):
                        if i < len(stages):
                            stages[i][0]()
                        if i - LA >= 0:
                            stages[i - LA][1]()
                for t in range(SEQ // 1024):
                    cols = slice(t * 1024, (t + 1) * 1024)
                    P.op("dve", lambda E, cols=cols: E.reciprocal(out=den[:, cols], in_=den[:, cols]), r=["den"], w=["den"])
                    P.op("pool", lambda E, cols=cols: E.tensor_tensor(out=yd[:, cols], in0=num[:, cols], in1=den[:, cols], op=ALU.mult),
                         r=["num", "den"], w=["yd"])
                P.op("sync", lambda E, slot=slot: E.dma_start(out=self.ydilS[slot], in_=yd[:]), r=["yd"], w=[], dma="ydst")
            self.end_phase()

    def mix_consts(self, l):
        P, nc, ps = self.P, self.nc, self.psum
        cst = self.cstS
        if not hasattr(self, "ssm_r"):
            self.ssm_r = nc.alloc_sbuf_tensor("ssm_r", [128, 32], F32)
            self.ssm_c128 = nc.alloc_sbuf_tensor("ssm_c128", [128, 32], F32)
            self.ssm_s128 = nc.alloc_sbuf_tensor("ssm_s128", [128, 32], F32)
            self.ssm_d = nc.alloc_sbuf_tensor("ssm_d", [128, 4], F32)
            self.pswS = nc.alloc_sbuf_tensor("pswS", [128, 128], F32)
        with ExitStack() as st:
            def A(name, shape, dt=F32):
                return st.enter_context(self.sbt(name, shape, dt))
            lre, lim, dt_ = A("lre", [128, 32]), A("lim", [128, 32]), A("dt_", [128, 32])
            th, kk, c1, s1, ab = A("th", [128, 32]), A("kkc", [128, 32]), A("c1", [128, 32]), A("s1", [128, 32]), A("abc", [128, 32])
            t1, t2, t3 = A("t1c", [128, 32]), A("t2c", [128, 32]), A("t3c", [128, 32])
            nr, ni, fre, fim, fimA, freB = (A("nr", [128, 32]), A("ni", [128, 32]), A("fre", [128, 32]), A("fim", [128, 32]),
                                            A("fimA", [128, 32]), A("freB", [128, 32]))
            bX, bY, cX, cY = A("bXs", [128, 32, 16]), A("bYs", [128, 32, 16]), A("cXs", [128, 32, 16]), A("cYs", [128, 32, 16])
            XA, XB, tq = A("XA", [128, 32, 16]), A("XB", [128, 32, 16]), A("tq", [128, 32, 16])
            CS, SN = A("CSt", [128, 32, 128]), A("SNt", [128, 32, 128])
            u1, u2 = A("u1", [128, 32, 64]), A("u2", [128, 32, 64])
            ident = A("identS", [128, 128])
            wab = A("wab", [128, 8, 2, 128], BF16)
            wc = A("wcs", [128, 8, 2, 128], BF16)
            tab = A("tabb", [128, 8, 2, 128])
            loads = ((lre, self.lamreT[:, l, :]), (lim, self.lamimT[:, l, :]), (dt_, self.logdtB[:, l, :]),
                     (bX, self.bX[:, l]), (bY, self.bY[:, l]), (cX, self.cX[:, l]), (cY, self.cY[:, l]),
                     (ident, self.ident), (self.pswS, self.pswap), (self.ssm_d, self.dT[:, l, :]))
            for i, (t_, src) in enumerate(loads):
                P.op("sync", lambda E, t_=t_, src=src: E.dma_start(out=t_[:], in_=src), w=["cl%d" % i], dma="cl%d" % i)
            RL = ["cl%d" % i for i in range(len(loads))]
            K = "cc"

            def dve(fn):
                P.op("dve", fn, r=RL + [K], w=[K])

            def act(fn):
                P.op("act", fn, r=RL + [K], w=[K])
            act(lambda E: E.activation(out=dt_[:], in_=dt_[:], func=AF.Exp))
            dve(lambda E: E.tensor_tensor(out=t1[:], in0=lre[:], in1=dt_[:], op=ALU.mult))
            act(lambda E: E.activation(out=self.ssm_r[:], in_=t1[:], func=AF.Exp))
            dve(lambda E: E.tensor_tensor(out=th[:], in0=lim[:], in1=dt_[:], op=ALU.mult))
            dve(lambda E: E.tensor_scalar(out=kk[:], in0=th[:], scalar1=1.0 / TWO_PI, scalar2=MAGIC, op0=ALU.mult, op1=ALU.add))
            dve(lambda E: E.tensor_scalar_sub(out=kk[:], in0=kk[:], scalar1=MAGIC))
            dve(lambda E: E.scalar_tensor_tensor(out=th[:], in0=kk[:], scalar=-TWO_PI, in1=th[:], op0=ALU.mult, op1=ALU.add))
            act(lambda E: E.activation(out=s1[:], in_=th[:], func=AF.Sin))
            act(lambda E: E.activation(out=ab[:], in_=th[:], func=AF.Abs))
            act(lambda E: E.activation(out=c1[:], in_=ab[:], func=AF.Sin, scale=-1.0, bias=cst[:, 6:7]))
            dve(lambda E: E.tensor_tensor(out=nr[:], in0=self.ssm_r[:], in1=c1[:], op=ALU.mult))
            dve(lambda E: E.tensor_scalar_add(out=nr[:], in0=nr[:], scalar1=-1.0))
            dve(lambda E: E.tensor_tensor(out=ni[:], in0=self.ssm_r[:], in1=s1[:], op=ALU.mult))
            dve(lambda E: E.tensor_tensor(out=t1[:], in0=lre[:], in1=lre[:], op=ALU.mult))
            dve(lambda E: E.tensor_tensor(out=t2[:], in0=lim[:], in1=lim[:], op=ALU.mult))
            dve(lambda E: E.tensor_tensor(out=t1[:], in0=t1[:], in1=t2[:], op=ALU.add))
            dve(lambda E: E.reciprocal(out=t3[:], in_=t1[:]))
            dve(lambda E: E.tensor_tensor(out=t1[:], in0=nr[:], in1=lre[:], op=ALU.mult))
            dve(lambda E: E.tensor_tensor(out=t2[:], in0=ni[:], in1=lim[:], op=ALU.mult))
            dve(lambda E: E.tensor_tensor(out=t1[:], in0=t1[:], in1=t2[:], op=ALU.add))
            dve(lambda E: E.tensor_tensor(out=fre[:], in0=t1[:], in1=t3[:], op=ALU.mult))
            dve(lambda E: E.tensor_tensor(out=t1[:], in0=ni[:], in1=lre[:], op=ALU.mult))
            dve(lambda E: E.tensor_tensor(out=t2[:], in0=nr[:], in1=lim[:], op=ALU.mult))
            dve(lambda E: E.tensor_tensor(out=t1[:], in0=t1[:], in1=t2[:], op=ALU.subtract))
            dve(lambda E: E.tensor_tensor(out=fim[:], in0=t1[:], in1=t3[:], op=ALU.mult))
            dve(lambda E: E.tensor_scalar(out=fimA[:], in0=fim[:], scalar1=cst[:, 2:3], scalar2=None, op0=ALU.mult))
            dve(lambda E: E.tensor_scalar(out=freB[:], in0=fre[:], scalar1=cst[:, 3:4], scalar2=None, op0=ALU.mult))

            def bc(t_):
                return t_[:].unsqueeze(2).broadcast_to([128, 32, 16])
            dve(lambda E: E.tensor_tensor(out=XA[:], in0=bX[:], in1=bc(fre), op=ALU.mult))
            dve(lambda E: E.tensor_tensor(out=tq[:], in0=bY[:], in1=bc(fimA), op=ALU.mult))
            dve(lambda E: E.tensor_tensor(out=XA[:], in0=XA[:], in1=tq[:], op=ALU.add))
            dve(lambda E: E.tensor_tensor(out=XB[:], in0=bY[:], in1=bc(freB), op=ALU.mult))
            dve(lambda E: E.tensor_tensor(out=tq[:], in0=bX[:], in1=bc(fim), op=ALU.mult))
            dve(lambda E: E.tensor_tensor(out=XB[:], in0=XB[:], in1=tq[:], op=ALU.add))
            dve(lambda E: E.memset(CS[:, :, 0:1], 1.0))
            dve(lambda E: E.memset(SN[:, :, 0:1], 0.0))
            cj, sj = c1, s1
            pw = [(A("cp%d" % j, [128, 32]), A("sp%d" % j, [128, 32])) for j in range(7)]
            for j in range(7):
                n = 1 << j

                def bcn(t_, n=n):
                    return t_[:].unsqueeze(2).broadcast_to([128, 32, n])
                dve(lambda E, n=n, cj=cj, bcn=bcn: E.tensor_tensor(out=u1[:, :, 0:n], in0=CS[:, :, 0:n], in1=bcn(cj), op=ALU.mult))
                dve(lambda E, n=n, sj=sj, bcn=bcn: E.tensor_tensor(out=u2[:, :, 0:n], in0=SN[:, :, 0:n], in1=bcn(sj), op=ALU.mult))
                dve(lambda E, n=n: E.tensor_tensor(out=CS[:, :, n:2 * n], in0=u1[:, :, 0:n], in1=u2[:, :, 0:n], op=ALU.subtract))
                dve(lambda E, n=n, sj=sj, bcn=bcn: E.tensor_tensor(out=u1[:, :, 0:n], in0=CS[:, :, 0:n], in1=bcn(sj), op=ALU.mult))
                dve(lambda E, n=n, cj=cj, bcn=bcn: E.tensor_tensor(out=u2[:, :, 0:n], in0=SN[:, :, 0:n], in1=bcn(cj), op=ALU.mult))
                dve(lambda E, n=n: E.tensor_tensor(out=SN[:, :, n:2 * n], in0=u1[:, :, 0:n], in1=u2[:, :, 0:n], op=ALU.add))
                cn, sn = pw[j]
                dve(lambda E, cj=cj: E.tensor_tensor(out=t1[:], in0=cj[:], in1=cj[:], op=ALU.mult))
                dve(lambda E, sj=sj: E.tensor_tensor(out=t2[:], in0=sj[:], in1=sj[:], op=ALU.mult))
                dve(lambda E, cn=cn: E.tensor_tensor(out=cn[:], in0=t1[:], in1=t2[:], op=ALU.subtract))
                dve(lambda E, cj=cj, sj=sj: E.tensor_tensor(out=t3[:], in0=cj[:], in1=sj[:], op=ALU.mult))
                dve(lambda E, sn=sn: E.tensor_scalar_mul(out=sn[:], in0=t3[:], scalar1=2.0))
                cj, sj = cn, sn
            dve(lambda E, cj=cj: E.tensor_copy(out=self.ssm_c128[:], in_=cj[:]))
            dve(lambda E, sj=sj: E.tensor_scalar(out=self.ssm_s128[:], in0=sj[:], scalar1=cst[:, 2:3], scalar2=None, op0=ALU.mult))
            for gb in range(4):
                gs = slice(gb * 8, (gb + 1) * 8)
                dve(lambda E, gs=gs: E.tensor_copy(out=tab[:, :, 0, :], in_=CS[:, gs, :]))
                dve(lambda E, gs=gs: E.tensor_copy(out=tab[:, :, 1, :], in_=SN[:, gs, :]))
                P.op("sync", lambda E, gb=gb: E.dma_start(out=self.tabS[gb], in_=tab[:]), r=[K], w=[K], dma="cst0")
                for ab_i, X in enumerate((XA, XB)):
                    P.op("pe", lambda E, X=X, gs=gs: E.transpose(ps[0][:, 0:128], X[:, gs, :].rearrange("p g h -> p (g h)"), ident[:]),
                         r=RL + [K], w=[K])
                    for gl in range(8):
                        dve(lambda E, gl=gl, ab_i=ab_i: E.tensor_scalar(
                            out=wab[:, gl, ab_i, :], in0=ps[0][:, 0:128], scalar1=cst[:, 8 + gl:9 + gl], scalar2=None, op0=ALU.mult))
                P.op("sync", lambda E, gb=gb: E.dma_start(out=self.wabS[gb], in_=wab[:]), r=[K], w=[K], dma="cst1")
                dve(lambda E: E.memset(wc[:], 0.0))
                for gl in range(8):
                    g_ = gb * 8 + gl
                    dve(lambda E, gl=gl, g_=g_: E.tensor_scalar(
                        out=wc[:, gl, 0, gl * 16:(gl + 1) * 16], in0=cX[:, g_, :], scalar1=cst[:, 3:4], scalar2=None, op0=ALU.mult))
                    dve(lambda E, gl=gl, g_=g_: E.tensor_scalar(
                        out=wc[:, gl, 1, gl * 16:(gl + 1) * 16], in0=cY[:, g_, :], scalar1=cst[:, 5:6], scalar2=None, op0=ALU.mult))
                P.op("sync", lambda E, gb=gb: E.dma_start(out=self.wcS[gb], in_=wc[:]), r=[K], w=[K], dma="cst2")
            self.end_phase()

    def mix_ssm(self, l, bl, hmod):
        P, nc, ps = self.P, self.nc, self.psum
        with ExitStack() as st:
            def A(name, shape, dt=F32):
                return st.enter_context(self.sbt(name, shape, dt))
            wu = A("wu", [128, 4, KC, 128], BF16)
            bS = A("b_inU", [128, NCH + 1])
            wglu = A("wgluS", [128, 4, 512], BF16)
            bglu = A("bgluS", [128, 4])
            tab = [A("tabS%d" % i, [128, 8, 2, 128]) for i in range(2)]
            wab = [A("wabS%d" % i, [128, 8, 2, 128], BF16) for i in range(2)]
            wc = [A("wcS%d" % i, [128, 8, 2, 128], BF16) for i in range(2)]
            uf = A("uf", [128, 4, 512])
            ub = A("ub", [128, 4, 512], BF16)
            ta = [A("ta%d" % i, [128, 512]) for i in range(2)]
            tb = [A("tb%d" % i, [128, 512]) for i in range(2)]
            zall = A("zall", [128, 8, 512])
            vall = A("vall", [128, 8, 512])
            init = A("sinit", [128, 32])
            ctmp = A("ctmp", [128, 8])
            e1 = [A("e1_%d" % i, [128, 512], BF16) for i in range(2)]
            e2 = [A("e2_%d" % i, [128, 512], BF16) for i in range(2)]
            yf = A("yf", [128, 4, 512])
            g1 = [A("g1_%d" % i, [128, 512]) for i in range(2)]
            g2 = [A("g2_%d" % i, [128, 512]) for i in range(2)]
            ygb = A("ygb", [128, 4, 512], BF16)
            sg = [A("sg%d" % i, [128, 512]) for i in range(2)]
            yo = [A("yo%d" % i, [128, 4, 512], BF16) for i in range(2)]
            P.op("sync", lambda E: E.dma_start(out=wu[:], in_=self.winq[l, CH_U:CH_U + 4].rearrange("c p k n -> p c k n")),
                 w=["wu"], dma="sw0")
            P.op("sync", lambda E: E.dma_start(out=bS[:], in_=self.b_inT[:, l, :]), w=["bSu"], dma="sw1")
            P.op("sync", lambda E: E.dma_start(out=wglu[:], in_=self.wgluq[l]), w=["wglu"], dma="sw2")
            P.op("sync", lambda E: E.dma_start(out=bglu[:], in_=self.b_gluT[:, l, :]), w=["bglu"], dma="sw3")
            P.op("dve", lambda E: E.memset(init[:], 0.0), w=["init"])
            r2, c128, s128 = self.ssm_r, self.ssm_c128, self.ssm_s128
            cnt = dict(w=0, t=0, e=0, g=0)
            for t in range(SEQ // 512):
                cols = slice(t * 512, (t + 1) * 512)
                for c in range(4):
                    pb = c % 2
                    for kc in range(KC):
                        P.op("pe", lambda E, c=c, kc=kc, pb=pb, cols=cols: E.matmul(
                            ps[pb][:], lhsT=wu[:, c, kc, :], rhs=hmod[:, kc, cols], start=(kc == 0), stop=(kc == KC - 1)),
                            r=["wu", "hmod_all"], w=["ps%d" % pb])
                    P.op("act", lambda E, c=c, pb=pb: E.activation(out=uf[:, c, :], in_=ps[pb][:], func=AF.Identity,
                                                                   bias=bS[:, CH_U + c:CH_U + c + 1]), r=["ps%d" % pb, "bSu"], w=["uf%d" % c])
                    P.op("act", lambda E, c=c, pb=pb: E.activation(out=ub[:, c, :], in_=ps[pb][:], func=AF.Identity,
                                                                   bias=bS[:, CH_U + c:CH_U + c + 1]), r=["ps%d" % pb, "bSu"], w=["ub%d" % c])
                for gb in range(4):
                    wi = cnt["w"] % 2
                    cnt["w"] += 1
                    tb_, wa_, wc_ = tab[wi], wab[wi], wc[wi]
                    kt, ka, kc_ = "tabS%d" % wi, "wabS%d" % wi, "wcS%d" % wi
                    P.op("sync", lambda E, gb=gb, tb_=tb_: E.dma_start(out=tb_[:], in_=self.tabS[gb]), w=[kt], dma=kt)
                    P.op("sync", lambda E, gb=gb, wa_=wa_: E.dma_start(out=wa_[:], in_=self.wabS[gb]), w=[ka], dma=ka)
                    P.op("sync", lambda E, gb=gb, wc_=wc_: E.dma_start(out=wc_[:], in_=self.wcS[gb]), w=[kc_], dma=kc_)
                    for gl in range(8):
                        ti = cnt["t"] % 2
                        cnt["t"] += 1
                        pa, pb = 2 + ti, 4 + ti
                        P.op("pe", lambda E, gl=gl, gb=gb, pa=pa, wa_=wa_: E.matmul(
                            ps[pa][:], lhsT=wa_[:, gl, 0, :], rhs=ub[:, gb, :], start=True, stop=True), r=[ka, "ub%d" % gb], w=["ps%d" % pa])
                        P.op("pe", lambda E, gl=gl, gb=gb, pb=pb, wa_=wa_: E.matmul(
                            ps[pb][:], lhsT=wa_[:, gl, 1, :], rhs=ub[:, gb, :], start=True, stop=True), r=[ka, "ub%d" % gb], w=["ps%d" % pb])
                        csb = tb_[:, gl, 0, :].unsqueeze(1).broadcast_to([128, 4, 128])
                        snb = tb_[:, gl, 1, :].unsqueeze(1).broadcast_to([128, 4, 128])
                        P.op("dve", lambda E, ti=ti, pa=pa, csb=csb: E.tensor_tensor(
                            out=ta[ti][:].rearrange("p (k n) -> p k n", n=128), in0=ps[pa][:].rearrange("p (k n) -> p k n", n=128),
                            in1=csb, op=ALU.mult), r=["ps%d" % pa, kt], w=["ta%d" % ti])
                        P.op("dve", lambda E, ti=ti, pb=pb, snb=snb: E.tensor_tensor(
                            out=tb[ti][:].rearrange("p (k n) -> p k n", n=128), in0=ps[pb][:].rearrange("p (k n) -> p k n", n=128),
                            in1=snb, op=ALU.mult), r=["ps%d" % pb, kt], w=["tb%d" % ti])
                        P.op("pool", lambda E, ti=ti, gl=gl: E.tensor_tensor(out=zall[:, gl, :], in0=ta[ti][:], in1=tb[ti][:], op=ALU.add),
                             r=["ta%d" % ti, "tb%d" % ti], w=["z%d" % gl])
                    for k in range(4):
                        sc = slice(k * 128, (k + 1) * 128)
                        for gl in range(8):
                            g_ = gb * 8 + gl
                            P.op("dve", lambda E, gl=gl, g_=g_, sc=sc: E.tensor_tensor_scan(
                                out=vall[:, gl, sc], data0=r2[:, g_:g_ + 1].broadcast_to([128, 128]), data1=zall[:, gl, sc],
                                initial=init[:, g_:g_ + 1], op0=ALU.mult, op1=ALU.add),
                                r=["z%d" % gl, "init", "cc_r"], w=["v%d" % gl])
                        vlast = vall[:, :, k * 128 + 127]
                        gsl = slice(gb * 8, (gb + 1) * 8)
                        P.op("pe", lambda E, vlast=vlast: E.matmul(ps[6][:, 0:8], lhsT=self.pswS[:], rhs=vlast, start=True, stop=True),
                             r=["v%d" % gl for gl in range(8)], w=["ps6"])
                        P.op("dve", lambda E, gsl=gsl: E.tensor_tensor(out=ctmp[:], in0=ps[6][:, 0:8], in1=s128[:, gsl], op=ALU.mult),
                             r=["ps6"], w=["ctmp"])
                        P.op("dve", lambda E, gsl=gsl, vlast=vlast: E.tensor_tensor(out=init[:, gsl], in0=vlast, in1=c128[:, gsl], op=ALU.mult),
                             r=["v%d" % gl for gl in range(8)], w=["init"])
                        P.op("dve", lambda E, gsl=gsl: E.tensor_tensor(out=init[:, gsl], in0=init[:, gsl], in1=ctmp[:], op=ALU.add),
                             r=["ctmp", "init"], w=["init"])
                    for gl in range(8):
                        ei = cnt["e"] % 2
                        cnt["e"] += 1
                        csb = tb_[:, gl, 0, :].unsqueeze(1).broadcast_to([128, 4, 128])
                        snb = tb_[:, gl, 1, :].unsqueeze(1).broadcast_to([128, 4, 128])
                        vv = vall[:, gl, :].rearrange("p (k n) -> p k n", n=128)
                        P.op("pool", lambda E, ei=ei, vv=vv, csb=csb: E.tensor_tensor(
                            out=e1[ei][:].rearrange("p (k n) -> p k n", n=128), in0=vv, in1=csb, op=ALU.mult),
                            r=["v%d" % gl, kt], w=["e1_%d" % ei])
                        P.op("pool", lambda E, ei=ei, vv=vv, snb=snb: E.tensor_tensor(
                            out=e2[ei][:].rearrange("p (k n) -> p k n", n=128), in0=vv, in1=snb, op=ALU.mult),
                            r=["v%d" % gl, kt], w=["e2_%d" % ei])
                        P.op("pe", lambda E, ei=ei, gl=gl, wc_=wc_: E.matmul(ps[7][:], lhsT=wc_[:, gl, 0, :], rhs=e1[ei][:],
                                                                          start=(gl == 0), stop=False), r=[kc_, "e1_%d" % ei], w=["ps7"])
                        P.op("pe", lambda E, ei=ei, gl=gl, wc_=wc_: E.matmul(ps[7][:], lhsT=wc_[:, gl, 1, :], rhs=e2[ei][:],
                                                                          start=False, stop=(gl == 7)), r=[kc_, "e2_%d" % ei], w=["ps7"])
                    P.op("dve", lambda E, gb=gb: E.scalar_tensor_tensor(
                        out=yf[:, gb, :], in0=uf[:, gb, :], scalar=self.ssm_d[:, gb:gb + 1], in1=ps[7][:], op0=ALU.mult, op1=ALU.add),
                        r=["ps7", "uf%d" % gb, "cc_r"], w=["yf%d" % gb])
                for c in range(4):
                    gi = cnt["g"] % 2
                    cnt["g"] += 1
                    x_ = yf[:, c, :]
                    P.op("act", lambda E, gi=gi, x_=x_: E.activation(out=g1[gi][:], in_=x_, func=AF.Square), r=["yf%d" % c], w=["g1_%d" % gi])
                    P.op("pool", lambda E, gi=gi: E.tensor_scalar(out=g1[gi][:], in0=g1[gi][:], scalar1=0.044715, scalar2=1.0,
                                                                  op0=ALU.mult, op1=ALU.add), r=["g1_%d" % gi], w=["g1_%d" % gi])
                    P.op("pool", lambda E, gi=gi, x_=x_: E.tensor_tensor(out=g1[gi][:], in0=g1[gi][:], in1=x_, op=ALU.mult),
                         r=["g1_%d" % gi, "yf%d" % c], w=["g1_%d" % gi])
                    P.op("act", lambda E, gi=gi: E.activation(out=g2[gi][:], in_=g1[gi][:], func=AF.Sigmoid, scale=1.5957691),
                         r=["g1_%d" % gi], w=["g2_%d" % gi])
                    P.op("pool", lambda E, gi=gi, x_=x_, c=c: E.tensor_tensor(out=yf[:, c, :], in0=g2[gi][:], in1=x_, op=ALU.mult),
                         r=["g2_%d" % gi, "yf%d" % c], w=["yg%d" % c])
                    P.op("act", lambda E, c=c: E.activation(out=ygb[:, c, :], in_=yf[:, c, :], func=AF.Copy), r=["yg%d" % c], w=["ygb%d" % c])
                yo_ = yo[t % 2]
                yok = "yo%d" % (t % 2)
                for c in range(4):
                    pb = c % 2
                    for kc in range(4):
                        P.op("pe", lambda E, c=c, kc=kc, pb=pb: E.matmul(
                            ps[pb][:], lhsT=wglu[:, kc, c * 128:(c + 1) * 128], rhs=ygb[:, kc, :], start=(kc == 0), stop=(kc == 3)),
                            r=["wglu", "ygb%d" % kc], w=["ps%d" % pb])
                    si = c % 2
                    P.op("act", lambda E, c=c, pb=pb, si=si: E.activation(out=sg[si][:], in_=ps[pb][:], func=AF.Sigmoid,
                                                                           bias=bglu[:, c:c + 1]), r=["ps%d" % pb, "bglu"], w=["sg%d" % si])
                    P.op("dve", lambda E, c=c, si=si, yo_=yo_: E.tensor_tensor(out=yo_[:, c, :], in0=yf[:, c, :], in1=sg[si][:], op=ALU.mult),
                         r=["sg%d" % si, "yg%d" % c], w=[yok + "_%d" % c])
                P.op("sync", lambda E, yo_=yo_, cols=cols: E.dma_start(out=self.yssmS[:, :, cols].rearrange("c p t -> p c t"), in_=yo_[:]),
                     r=[yok + "_%d" % c for c in range(4)], w=[yok], dma=yok)
            self.end_phase()

    def mix_merge(self, l, bl, src, dst):
        P, nc, ps = self.P, self.nc, self.psum
        tok0 = bl * SEQ
        eps = LN_EPS / (ALPHA * ALPHA)
        srcv = src.rearrange("(kc p) t -> p kc t", p=128)
        dstv = dst.rearrange("(kc p) t -> p kc t", p=128)
        lnS = self.lnS
        with ExitStack() as st:
            def A(name, shape, dt=F32):
                return st.enter_context(self.sbt(name, shape, dt))
            wgs = [A("wg%d" % i, [128, 3, KC, 128], BF16) for i in range(2)]
            wbr = A("wbr", [128, 2, 4, D], BF16)
            wbr0 = A("wbr0", [64, 8, D], BF16)
            wo = A("wo", [128, KC, D], BF16)
            bS = A("b_inM", [128, NCH + 1])
            ht = [A("mht%d" % i, [128, KC, 512]) for i in range(2)]
            hm = A("mhm", [128, KC, 512], BF16)
            ym = [A("ym%d" % i, [64, 8, 512], BF16) for i in range(2)]
            ydl = [A("ydl%d" % i, [128, 4, 512], BF16) for i in range(2)]
            ys = [A("ys%d" % i, [128, 4, 512], BF16) for i in range(2)]
            gt = [A("gt%d" % i, [128, 512]) for i in range(3)]
            acc = [A("acc%d" % i, [128, 512]) for i in range(2)]
            tmpm = [A("tmpm%d" % i, [128, 512]) for i in range(2)]
            mg = A("mg", [128, KC, 512], BF16)
            zb = [A("mzb%d" % i, [128, 512], BF16) for i in range(2)]
            zq = [A("mzq%d" % i, [128, 512], BF16) for i in range(2)]
            lnbufs = (A("mmean", [128, 512]), A("mm2", [128, 512]), A("mrstd", [128, 512]),
                      [A("mlntmp%d" % i, [128, 512]) for i in range(2)])
            for k in range(2):
                P.op("sync", lambda E, k=k: E.dma_start(out=wbr[:, k], in_=self.wbrq[l, k + 1]), w=["wbr"], dma="mw1")
            P.op("sync", lambda E: E.dma_start(out=wbr0[:], in_=self.wbr0q[l]), w=["wbr0"], dma="mw2")
            P.op("sync", lambda E: E.dma_start(out=wo[:], in_=self.woutq[l]), w=["wo"], dma="mw3")
            P.op("sync", lambda E: E.dma_start(out=bS[:], in_=self.b_inT[:, l, :]), w=["bSm"], dma="mw4")
            cnt = dict(b=0, z=0, g=0)
            for t in range(SEQ // 512):
                cols = slice(t * 512, (t + 1) * 512)
                gcols = slice(tok0 + t * 512, tok0 + (t + 1) * 512)
                i2 = t % 2
                h = ht[i2]
                hk = "mht%d" % i2
                P.op("sync", lambda E, h=h, gcols=gcols: E.dma_start(out=h[:], in_=srcv[:, :, gcols]), w=[hk], dma=hk)
                P.op("sync", lambda E, i2=i2, cols=cols: E.dma_start(out=ym[i2][:], in_=self.ymlaS[:, :, cols].rearrange("h p t -> p h t")),
                     w=["ym%d" % i2], dma="ym%d" % i2)
                P.op("sync", lambda E, i2=i2, cols=cols: E.dma_start(out=ydl[i2][:], in_=self.ydilS[:, :, cols].rearrange("c p t -> p c t")),
                     w=["ydl%d" % i2], dma="ydl%d" % i2)
                P.op("sync", lambda E, i2=i2, cols=cols: E.dma_start(out=ys[i2][:], in_=self.yssmS[:, :, cols].rearrange("c p t -> p c t")),
                     w=["ys%d" % i2], dma="ys%d" % i2)
                for kc in range(KC):
                    P.op("act", lambda E, kc=kc, h=h: E.activation(
                        out=hm[:, kc, :], in_=h[:, kc, :], func=AF.Identity,
                        bias=self.ada[:, 24 + kc, bl:bl + 1], scale=self.s1p[:, 1, kc, bl:bl + 1]),
                        r=[hk, "ada", "s1p"], w=["mhm%d" % kc])
                hmk = ["mhm%d" % kc for kc in range(KC)]
                for i in range(KC):
                    wgi = cnt["g"] % 2
                    cnt["g"] += 1
                    wg = wgs[wgi]
                    wgk = "wg%d" % wgi
                    P.op("sync", lambda E, wg=wg, i=i: E.dma_start(
                        out=wg[:], in_=self.winq[l, CH_GATE + i:CH_GATE + 24:8].rearrange("c p k n -> p c k n")),
                        w=[wgk], dma=wgk)
                    for k in range(3):
                        pb = cnt["b"] % 3
                        cnt["b"] += 1
                        gc = k * 8 + i
                        for kc in range(KC):
                            P.op("pe", lambda E, k=k, kc=kc, pb=pb, wg=wg: E.matmul(
                                ps[pb][:], lhsT=wg[:, k, kc, :], rhs=hm[:, kc, :], start=(kc == 0), stop=(kc == KC - 1)),
                                r=[wgk, hmk[kc]], w=["ps%d" % pb])
                        P.op("act", lambda E, k=k, pb=pb, gc=gc: E.activation(
                            out=gt[k][:], in_=ps[pb][:], func=AF.Sigmoid, bias=bS[:, CH_GATE + gc:CH_GATE + gc + 1]),
                            r=["ps%d" % pb, "bSm"], w=["gt%d" % k])
                    for hd in range(8):
                        P.op("pe", lambda E, hd=hd, i=i, i2=i2: E.matmul(
                            ps[3][:], lhsT=wbr0[:, hd, i * 128:(i + 1) * 128], rhs=ym[i2][:, hd, :], start=(hd == 0), stop=(hd == 7)),
                            r=["wbr0", "ym%d" % i2], w=["ps3"])
                    for k, yk_, ysrc in ((0, "ydl%d" % i2, ydl[i2]), (1, "ys%d" % i2, ys[i2])):
                        for kc in range(4):
                            P.op("pe", lambda E, k=k, kc=kc, i=i, ysrc=ysrc: E.matmul(
                                ps[4 + k][:], lhsT=wbr[:, k, kc, i * 128:(i + 1) * 128], rhs=ysrc[:, kc, :], start=(kc == 0), stop=(kc == 3)),
                                r=["wbr", yk_], w=["ps%d" % (4 + k)])
                    ai = i % 2
                    P.op("dve", lambda E, ai=ai: E.tensor_tensor(out=acc[ai][:], in0=ps[3][:], in1=gt[0][:], op=ALU.mult),
                         r=["ps3", "gt0"], w=["acc%d" % ai])
                    P.op("dve", lambda E, ai=ai: E.tensor_tensor(out=tmpm[0][:], in0=ps[4][:], in1=gt[1][:], op=ALU.mult),
                         r=["ps4", "gt1"], w=["tmpm0"])
                    P.op("dve", lambda E, ai=ai: E.tensor_tensor(out=tmpm[1][:], in0=ps[5][:], in1=gt[2][:], op=ALU.mult),
                         r=["ps5", "gt2"], w=["tmpm1"])
                    P.op("pool", lambda E, ai=ai: E.tensor_tensor(out=acc[ai][:], in0=acc[ai][:], in1=tmpm[0][:], op=ALU.add),
                         r=["acc%d" % ai, "tmpm0"], w=["acc%d" % ai])
                    P.op("pool", lambda E, ai=ai, i=i: E.tensor_tensor(out=mg[:, i, :], in0=acc[ai][:], in1=tmpm[1][:], op=ALU.add),
                         r=["acc%d" % ai, "tmpm1"], w=["mg%d" % i])
                zkeys, ykeys = [], []
                for i in range(KC):
                    pf = 6
                    zi = cnt["z"] % 2
                    cnt["z"] += 1
                    for kc in range(KC):
                        P.op("pe", lambda E, i=i, kc=kc: E.matmul(
                            ps[6][:], lhsT=wo[:, kc, i * 128:(i + 1) * 128], rhs=mg[:, kc, :], start=(kc == 0), stop=(kc == KC - 1)),
                            r=["wo", "mg%d" % kc], w=["ps6"])
                    hs = h[:, i, :]
                    zk = hk + "_z%d" % i
                    zkeys.append(zk)
                    ykeys.append(hk + "_y%d" % i)
                    P.op("dve", lambda E, hs=hs, i=i: E.scalar_tensor_tensor(
                        out=hs, in0=ps[6][:], scalar=self.gp[:, 1, i, bl:bl + 1], in1=hs, op0=ALU.mult, op1=ALU.add),
                        r=["ps6", "gp", hk], w=[zk])
                    P.op("act", lambda E, hs=hs, zi=zi: E.activation(out=zb[zi][:], in_=hs, func=AF.Copy), r=[zk], w=["mzb%d" % zi])
                    P.op("act", lambda E, hs=hs, zi=zi: E.activation(out=zq[zi][:], in_=hs, func=AF.Square), r=[zk], w=["mzq%d" % zi])
                    P.op("pe", lambda E, zi=zi, i=i: E.matmul(ps[7][:], lhsT=self.ones_bf[:], rhs=zb[zi][:], start=(i == 0), stop=(i == KC - 1)),
                         r=["ones", "mzb%d" % zi], w=["ps7"])
                    P.op("pe", lambda E, zi=zi, i=i: E.matmul(ps[0][:], lhsT=self.ones_bf[:], rhs=zq[zi][:], start=(i == 0), stop=(i == KC - 1)),
                         r=["ones", "mzq%d" % zi], w=["ps0"])
                self.ln_finish(h, zkeys, ykeys, slice(0, 512), ps[7], ps[0], "ps7", "ps0",
                               lambda i: lnS[:, l, 1, 0, i:i + 1], lambda i: lnS[:, l, 1, 1, i:i + 1], lnbufs, eps)
                P.op("sync", lambda E, h=h, gcols=gcols: E.dma_start(out=dstv[:, :, gcols], in_=h[:]),
                     r=ykeys, w=[hk], dma="mst%d" % i2)
            self.end_phase()


def _consts():
    import ml_dtypes
    cst = np.zeros((128, NCST), np.float32)
    half = 16
    invf = (10000.0 ** (-(np.arange(half, dtype=np.float32)) / half)).astype(np.float32)
    cst[64:96, 0] = np.concatenate([invf, invf])
    cst[64:80, 1] = -1.0
    cst[80:96, 1] = 1.0
    cst[0:64, 2] = -1.0
    cst[64:128, 2] = 1.0
    cst[0:64, 3] = 1.0
    cst[64:128, 3] = -1.0
    cst[:, 5] = -1.0
    cst[:, 6] = np.pi / 2
    for j in range(8):
        cst[16 * j:16 * (j + 1), 8 + j] = 1.0
    ki = np.arange(128)[:, None, None]
    jj = np.arange(4)[None, :, None]
    nn = np.arange(512)[None, None, :]
    maskc = (nn >= jj * 128 + ki).astype(np.float32).astype(ml_dtypes.bfloat16)
    slopes = np.exp2(-8.0 * (np.arange(12, dtype=np.float32) + 1.0) / 12).astype(np.float32)
    kk = np.arange(128)[:, None]
    qq = np.arange(128)[None, :]
    dilmb = np.zeros((128, 12, 256), np.float32)
    for hd in range(12):
        d = DIL_PAIRS[hd // 4][1]
        relL = (qq - kk).astype(np.float32)
        relR = (qq - kk + 128).astype(np.float32)
        dilmb[:, hd, 0:128] = np.where(qq >= kk, -(slopes[hd] * d) * relL, -30000.0)
        dilmb[:, hd, 128:256] = np.where(qq <= kk, -(slopes[hd] * d) * relR, -30000.0)
    ident = np.eye(128, dtype=np.float32)
    pswap = np.zeros((128, 128), np.float32)
    for mcol in range(128):
        pswap[(mcol + 64) % 128, mcol] = 1.0
    return dict(cst=cst, maskc=np.ascontiguousarray(maskc), dilmb=dilmb, ident=ident, pswap=pswap)


def prep_inputs(inp, names, L, cores):
    f32 = np.float32
    g = lambda k: np.asarray(inp[k])[:L]
    sh = {}
    sh.update(_consts())
    sh["w_ada"] = np.ascontiguousarray(g("w_ada"), dtype=f32)
    sh["b_adaT"] = np.ascontiguousarray(g("b_ada").reshape(L, 72, 128).transpose(2, 0, 1))
    ln = np.stack([g("ln_g"), g("ln_b")], axis=2)
    sh["lnT"] = np.ascontiguousarray(ln.reshape(L, 3, 2, KC, 128).transpose(4, 0, 1, 2, 3))
    for k in ("ffn_w1", "ffn_w3", "ffn_w2", "w_in"):
        if k in names:
            sh[k] = np.ascontiguousarray(g(k), dtype=f32)
    if "w_in" in names:
        b_in = g("b_in")
        bT = np.zeros((128, L, NCH + 1), f32)
        for c, (cs, cw) in enumerate(W_CHUNKS):
            bT[:cw, :, c] = b_in[:, cs:cs + cw].T
        bT[64:80, :, NCH] = b_in[:, 656:672].T
        bT[80:96, :, NCH] = b_in[:, 640:656].T
        sh["b_inT"] = bT
        sh["b_vrow"] = np.ascontiguousarray(b_in[:, None, 672 + 3072:672 + 4608])
        sh["qnT"] = np.ascontiguousarray(g("mla_q_norm").reshape(L, 3, 128).transpose(2, 0, 1))
        sh["kvnT"] = np.ascontiguousarray(g("mla_kv_norm").reshape(L, 2, 128).transpose(2, 0, 1))
        sh["w_qb"] = np.ascontiguousarray(g("mla_w_qb"), dtype=f32)
        sh["w_kvb"] = np.ascontiguousarray(g("mla_w_kvb"), dtype=f32)
        sh["w_glu"] = np.ascontiguousarray(g("ssm_w_glu"), dtype=f32)
        sh["b_gluT"] = np.ascontiguousarray(g("ssm_b_glu").reshape(L, 4, 128).transpose(2, 0, 1))
        sh["w_br"] = np.ascontiguousarray(g("w_br"), dtype=f32)
        sh["w_out"] = np.ascontiguousarray(g("w_out"), dtype=f32)
        dup = lambda a: np.ascontiguousarray(np.concatenate([a, a], axis=0))
        sh["lamreT"] = dup(g("ssm_lambda_re").transpose(2, 0, 1))
        sh["lamimT"] = dup(g("ssm_lambda_im").transpose(2, 0, 1))
        sh["logdtB"] = np.ascontiguousarray(np.broadcast_to(g("ssm_log_dt")[None], (128, L, 32)))
        bre = g("ssm_b_re").transpose(2, 0, 1, 3)
        bim = g("ssm_b_im").transpose(2, 0, 1, 3)
        sh["bX"] = np.ascontiguousarray(np.concatenate([bre, bim], axis=0))
        sh["bY"] = np.ascontiguousarray(np.concatenate([bim, bre], axis=0))
        cre = g("ssm_c_re").transpose(3, 0, 1, 2)
        cim = g("ssm_c_im").transpose(3, 0, 1, 2)
        sh["cX"] = np.ascontiguousarray(np.concatenate([cre, cim], axis=0))
        sh["cY"] = np.ascontiguousarray(np.concatenate([cim, cre], axis=0))
        sh["dT"] = np.ascontiguousarray(g("ssm_d").reshape(L, 4, 128).transpose(2, 0, 1))
    x = np.asarray(inp["x"], f32)
    c = np.asarray(inp["c"], f32)
    pos = np.asarray(inp["positions"]).astype(np.int32)
    maps = []
    for i in cores:
        m = {k: v for k, v in sh.items() if k in names}
        xs = x[NB * i:NB * (i + 1)]
        m["xT"] = np.ascontiguousarray(xs.transpose(2, 0, 1).reshape(D, NTOK))
        cs_ = c[NB * i:NB * (i + 1)]
        m["cT"] = np.ascontiguousarray(cs_.reshape(NB, KC, 128).transpose(2, 1, 0))
        if "pos" in names:
            m["pos"] = np.ascontiguousarray(pos[NB * i:NB * (i + 1)])
        maps.append(m)
    return maps


def gather_output(results, cores):
    out = np.zeros((NCORES * NB, SEQ, D), np.float32)
    for i, r in zip(cores, results):
        yT = np.asarray(r["yT"]).reshape(D, NB, SEQ)
        out[NB * i:NB * (i + 1)] = yT.transpose(1, 2, 0)
    return out


_NC_CACHE = {}


def run(inp, cores=None, raw=False, **kw):
    cores = list(range(NCORES)) if cores is None else list(cores)
    key = tuple(sorted((k, str(v)) for k, v in kw.items()))
    if key not in _NC_CACHE:
        b = Builder(**kw)
        b.build()
        _NC_CACHE[key] = b
    b = _NC_CACHE[key]
    maps = prep_inputs(inp, set(b.in_names), b.L, cores)
    res = run_bass_kernel_spmd(b.nc, maps, core_ids=list(range(len(cores))))
    if raw:
        return res.results
    return gather_output(res.results, cores)


def kernel(**inputs):
    return run(inputs)
```
